# Optimizing a Trainium2 kernel written in Bass

```python
import math
import jax, jax.numpy as jnp
from jax import lax
import numpy as np

D_MODEL = 1024
BATCH = 8
SEQ = 2048
DEPTH = 2
DEC_BATCH = 128
DEC_SEQ = 8
PAST_LEN = 16384
PAGE_SIZE = 128

N_MIXERS = 2
N_A = (DEPTH + 1) // 2
N_B = DEPTH // 2
HGRN_HEADS = 8
HGRN_DK = 128
HGRN_DV = D_MODEL // HGRN_HEADS
MLSTM_HEADS = 8
MLSTM_DV = D_MODEL // MLSTM_HEADS
MLSTM_DK = MLSTM_DV // 2
D_FF = 4 * D_MODEL
CHUNK = 64
GATE_SOFTCAP = 15.0
EPS = 1e-6

kernel_name = "hgrn2_mlstm_hybrid_step"

F32 = jnp.float32


def rmsnorm(x, w):
    x32 = x.astype(F32)
    y = x32 * lax.rsqrt(jnp.mean(x32 * x32, axis=-1, keepdims=True) + EPS)
    return (y * w.astype(F32)).astype(x.dtype)


def head_rmsnorm(o, w):
    H, d = o.shape[-2], o.shape[-1]
    return o * lax.rsqrt(jnp.mean(o * o, axis=-1, keepdims=True) + EPS) * w.astype(F32).reshape(H, d)


def softcap(x, cap):
    return cap * jnp.tanh(x / cap)


def _to_chunks(a, L):
    B, T, H, d = a.shape
    return a.reshape(B, T // L, L, H, d).transpose(1, 0, 3, 2, 4)


def _from_chunks(a):
    NC, B, H, L, d = a.shape
    return a.transpose(1, 0, 3, 2, 4).reshape(B, NC * L, H, d)


def hgrn2_recurrence(q, k, v, log_f, S0):
    T = q.shape[1]
    L = math.gcd(T, CHUNK)
    causal = jnp.tril(jnp.ones((L, L), dtype=bool))

    def step(S, inp):
        qc, kc, vc, fc = inp
        b = jnp.cumsum(fc, axis=2)
        diff = b[:, :, :, None, :] - b[:, :, None, :, :]
        decay = jnp.exp(jnp.where(causal[:, :, None], diff, -jnp.inf))
        A = jnp.einsum('bhtd,bhsd,bhtsd->bhts', qc, kc, decay)
        o = jnp.einsum('bhts,bhsv->bhtv', A, vc) + jnp.einsum('bhtd,bhdv->bhtv', qc * jnp.exp(b), S)
        bL = b[:, :, -1:, :]
        S_new = jnp.exp(bL[:, :, 0, :, None]) * S + jnp.einsum('bhsd,bhsv->bhdv', kc * jnp.exp(bL - b), vc)
        return S_new, o

    S_T, o = lax.scan(step, S0, (_to_chunks(q, L), _to_chunks(k, L), _to_chunks(v, L), _to_chunks(log_f, L)))
    return _from_chunks(o), S_T


def hgrn2_mixer(h, S0, w_in, lb, out_norm_w, w_out):
    B, T, _ = h.shape
    hk = HGRN_HEADS * HGRN_DK
    hv = HGRN_HEADS * HGRN_DV
    proj = h @ w_in
    q, f, i, g = jnp.split(proj, [hk, 2 * hk, 2 * hk + hv], axis=-1)
    forget = lb + (1.0 - lb) * jax.nn.sigmoid(f.astype(F32))
    log_f = jnp.log(forget)
    k = 1.0 - forget
    shp = lambda a, d: a.reshape(B, T, HGRN_HEADS, d)
    o, S_T = hgrn2_recurrence(shp(jax.nn.silu(q.astype(F32)), HGRN_DK), shp(k, HGRN_DK),
                              shp(i.astype(F32), HGRN_DV), shp(log_f, HGRN_DK), S0.astype(F32))
    o = head_rmsnorm(o, out_norm_w) * shp(jax.nn.silu(g.astype(F32)), HGRN_DV)
    out = o.reshape(B, T, hv).astype(h.dtype) @ w_out
    return out, S_T.astype(S0.dtype)


def mlstm_recurrence(q, k, v, log_i, log_f, C0, n0, m0):
    T = q.shape[1]
    L = math.gcd(T, CHUNK)
    causal = jnp.tril(jnp.ones((L, L), dtype=bool))

    def step(carry, inp):
        C, n, m = carry
        qc, kc, vc, ic, fc = inp
        ic = ic[..., 0]
        fc = fc[..., 0]
        b = jnp.cumsum(fc, axis=-1)
        logw = jnp.where(causal, b[..., :, None] - b[..., None, :] + ic[..., None, :], -jnp.inf)
        inter = b + m[..., None]
        m_t = jnp.maximum(inter, jnp.max(logw, axis=-1))
        w_intra = jnp.exp(logw - m_t[..., None])
        w_inter = jnp.exp(inter - m_t)
        s = jnp.einsum('bhtd,bhsd->bhts', qc, kc) * w_intra
        num = jnp.einsum('bhts,bhsv->bhtv', s, vc) + w_inter[..., None] * jnp.einsum('bhtd,bhdv->bhtv', qc, C)
        den = jnp.sum(s, axis=-1) + w_inter * jnp.einsum('bhtd,bhd->bht', qc, n)
        h = num / jnp.maximum(jnp.abs(den), jnp.exp(-m_t))[..., None]
        bL = b[..., -1]
        g = bL[..., None] - b + ic
        m_new = jnp.maximum(bL + m, jnp.max(g, axis=-1))
        ws = jnp.exp(g - m_new[..., None])
        wc = jnp.exp(bL + m - m_new)
        C_new = wc[..., None, None] * C + jnp.einsum('bhs,bhsd,bhsv->bhdv', ws, kc, vc)
        n_new = wc[..., None] * n + jnp.einsum('bhs,bhsd->bhd', ws, kc)
        return (C_new, n_new, m_new), h

    (C_T, n_T, m_T), h = lax.scan(
        step, (C0, n0, m0),
        (_to_chunks(q, L), _to_chunks(k, L), _to_chunks(v, L),
         _to_chunks(log_i[..., None], L), _to_chunks(log_f[..., None], L)))
    return _from_chunks(h), C_T, n_T, m_T


def mlstm_mixer(h, C0, n0, m0, w_in, gate_bias, out_norm_w, w_out):
    B, T, _ = h.shape
    H = MLSTM_HEADS
    qk = H * MLSTM_DK
    hv = H * MLSTM_DV
    proj = (h @ w_in).astype(F32)
    q, k, v, o, gi, gf = jnp.split(proj, [qk, 2 * qk, 2 * qk + hv, 2 * qk + 2 * hv, 2 * qk + 2 * hv + H], axis=-1)
    gb = gate_bias.astype(F32)
    log_i = softcap(gi + gb[:H], GATE_SOFTCAP)
    log_f = jax.nn.log_sigmoid(softcap(gf + gb[H:], GATE_SOFTCAP))
    q = q * (MLSTM_DK ** -0.5)
    hh, C_T, n_T, m_T = mlstm_recurrence(
        q.reshape(B, T, H, MLSTM_DK), k.reshape(B, T, H, MLSTM_DK), v.reshape(B, T, H, MLSTM_DV),
        log_i, log_f, C0.astype(F32), n0.astype(F32), m0.astype(F32))
    hh = head_rmsnorm(hh, out_norm_w) * jax.nn.sigmoid(o).reshape(B, T, H, MLSTM_DV)
    out = hh.reshape(B, T, hv).astype(h.dtype) @ w_out
    return out, C_T.astype(C0.dtype), n_T.astype(n0.dtype), m_T.astype(m0.dtype)


def sqrelu_mlp(x, w_up, w_down):
    return jnp.square(jax.nn.relu(x @ w_up)) @ w_down


def trunk(x, S_in, C_in, n_in, m_in, norm_mixer_w, norm_ffn_w, hgrn_w_in, hgrn_lower_bound_logits,
          hgrn_out_norm_w, hgrn_w_out, mlstm_w_in, mlstm_gate_bias, mlstm_out_norm_w, mlstm_w_out,
          ffn_w_up, ffn_w_down, final_norm_w):
    lower_bounds = jnp.cumsum(jax.nn.softmax(hgrn_lower_bound_logits.astype(F32), axis=0), axis=0)
    S_out, C_out, n_out, m_out = [], [], [], []
    for layer in range(DEPTH):
        hn = rmsnorm(x, norm_mixer_w[layer])
        j = layer // N_MIXERS
        if layer % N_MIXERS == 0:
            out, S = hgrn2_mixer(hn, S_in[j], hgrn_w_in[j], lower_bounds[layer], hgrn_out_norm_w[j], hgrn_w_out[j])
            S_out.append(S)
        else:
            out, C, n, m = mlstm_mixer(hn, C_in[j], n_in[j], m_in[j], mlstm_w_in[j], mlstm_gate_bias[j],
                                       mlstm_out_norm_w[j], mlstm_w_out[j])
            C_out.append(C)
            n_out.append(n)
            m_out.append(m)
        x = x + out
        x = x + sqrelu_mlp(rmsnorm(x, norm_ffn_w[layer]), ffn_w_up[layer], ffn_w_down[layer])
    y = rmsnorm(x, final_norm_w)
    return y, jnp.stack(S_out), jnp.stack(C_out), jnp.stack(n_out), jnp.stack(m_out)


def setup_inputs(seed: int = 0) -> dict:
    key = jax.random.key(seed)
    ks = jax.random.split(key, 24)
    nrm = lambda k, shape, s: jax.random.normal(k, shape, F32) * s
    hk = HGRN_HEADS * HGRN_DK
    hv = HGRN_HEADS * HGRN_DV
    qk = MLSTM_HEADS * MLSTM_DK
    mv = MLSTM_HEADS * MLSTM_DV
    H = MLSTM_HEADS
    gate_bias = jnp.concatenate([
        nrm(ks[10], (N_B, H), 0.1),
        3.0 + 3.0 * jax.random.uniform(ks[11], (N_B, H), F32)], axis=-1)
    return {
        "x_prompt": nrm(ks[0], (BATCH, SEQ, D_MODEL), 1.0),
        "x_sample": nrm(ks[1], (DEC_BATCH, DEC_SEQ, D_MODEL), 1.0),
        "state_hgrn_S": nrm(ks[2], (N_A, DEC_BATCH, HGRN_HEADS, HGRN_DK, HGRN_DV), 0.5),
        "state_mlstm_C": nrm(ks[3], (N_B, DEC_BATCH, MLSTM_HEADS, MLSTM_DK, MLSTM_DV), 0.3),
        "state_mlstm_n": nrm(ks[4], (N_B, DEC_BATCH, MLSTM_HEADS, MLSTM_DK), 0.3),
        "state_mlstm_m": jax.random.uniform(ks[5], (N_B, DEC_BATCH, MLSTM_HEADS), F32, -2.0, 2.0),
        "norm_mixer_w": 1.0 + nrm(ks[6], (DEPTH, D_MODEL), 0.02),
        "norm_ffn_w": 1.0 + nrm(ks[7], (DEPTH, D_MODEL), 0.02),
        "hgrn_w_in": nrm(ks[8], (N_A, D_MODEL, 2 * hk + 2 * hv), D_MODEL ** -0.5),
        "hgrn_lower_bound_logits": nrm(ks[9], (DEPTH + 1, hk), 0.1),
        "hgrn_out_norm_w": 1.0 + nrm(ks[12], (N_A, hv), 0.02),
        "hgrn_w_out": nrm(ks[13], (N_A, hv, D_MODEL), hv ** -0.5),
        "mlstm_w_in": nrm(ks[14], (N_B, D_MODEL, 2 * qk + 2 * mv + 2 * H), D_MODEL ** -0.5),
        "mlstm_gate_bias": gate_bias,
        "mlstm_out_norm_w": 1.0 + nrm(ks[15], (N_B, mv), 0.02),
        "mlstm_w_out": nrm(ks[16], (N_B, mv, D_MODEL), mv ** -0.5),
        "ffn_w_up": nrm(ks[17], (DEPTH, D_MODEL, D_FF), D_MODEL ** -0.5),
        "ffn_w_down": nrm(ks[18], (DEPTH, D_FF, D_MODEL), D_FF ** -0.5),
        "final_norm_w": 1.0 + nrm(ks[19], (D_MODEL,), 0.02),
    }


def reference(x_prompt, x_sample, state_hgrn_S, state_mlstm_C, state_mlstm_n, state_mlstm_m,
              norm_mixer_w, norm_ffn_w, hgrn_w_in, hgrn_lower_bound_logits, hgrn_out_norm_w, hgrn_w_out,
              mlstm_w_in, mlstm_gate_bias, mlstm_out_norm_w, mlstm_w_out, ffn_w_up, ffn_w_down, final_norm_w):
    Bp = x_prompt.shape[0]
    S0 = jnp.zeros((N_A, Bp) + state_hgrn_S.shape[2:], state_hgrn_S.dtype)
    C0 = jnp.zeros((N_B, Bp) + state_mlstm_C.shape[2:], state_mlstm_C.dtype)
    n0 = jnp.zeros((N_B, Bp) + state_mlstm_n.shape[2:], state_mlstm_n.dtype)
    m0 = jnp.zeros((N_B, Bp) + state_mlstm_m.shape[2:], state_mlstm_m.dtype)
    y_prompt, S_p, C_p, n_p, m_p = trunk(
        x_prompt, S0, C0, n0, m0, norm_mixer_w, norm_ffn_w, hgrn_w_in, hgrn_lower_bound_logits,
        hgrn_out_norm_w, hgrn_w_out, mlstm_w_in, mlstm_gate_bias, mlstm_out_norm_w, mlstm_w_out,
        ffn_w_up, ffn_w_down, final_norm_w)
    y_sample, S_s, C_s, n_s, m_s = trunk(
        x_sample, state_hgrn_S, state_mlstm_C, state_mlstm_n, state_mlstm_m, norm_mixer_w, norm_ffn_w,
        hgrn_w_in, hgrn_lower_bound_logits, hgrn_out_norm_w, hgrn_w_out, mlstm_w_in, mlstm_gate_bias,
        mlstm_out_norm_w, mlstm_w_out, ffn_w_up, ffn_w_down, final_norm_w)
    return (y_prompt, y_sample, S_p, C_p, n_p, m_p, S_s, C_s, n_s, m_s)
```

```python
import os
from contextlib import ExitStack
import numpy as np
import concourse.bass as bass
import concourse.mybir as mybir
from concourse.bass_utils import run_bass_kernel_spmd

F32 = mybir.dt.float32
BF16 = mybir.dt.bfloat16
AF = mybir.ActivationFunctionType
ALU = mybir.AluOpType
AX = mybir.AxisListType

D = 1024
NP_ = 2048
NS = 128
NT = NP_ + NS
KC = 8
BLOCKS = [(0, 512), (512, 512), (1024, 512), (1536, 512), (2048, 128)]
EPS = 1e-6
WARM_N = int(os.environ.get("MK_WARM_N", "24"))
CAP = 15.0

C_IDENT = 0
C_ONESD = 128
C_ONES128 = 256
C_MASKH = 384
C_BD128 = 512
C_TRI128 = 640
C_SM64 = 768
C_SM8 = 1280
C_IND16 = 1408
C_ONES1 = 1424
C_MAIN = 1552
C_SELP = 1552
C_TOTAL = 1552 + 512


def _make_consts():
    c = np.zeros((128, C_TOTAL), np.float32)
    s = np.arange(128)[:, None]
    t = np.arange(128)[None, :]
    c[:, C_IDENT:C_IDENT + 128] = np.eye(128, dtype=np.float32)
    c[:, C_ONESD:C_ONESD + 128] = 1.0 / 1024.0
    c[:, C_ONES128:C_ONES128 + 128] = 1.0 / 128.0
    c[:, C_MASKH:C_MASKH + 128] = ((s // 64 == t // 64) & (s <= t)).astype(np.float32)
    c[:, C_BD128:C_BD128 + 128] = ((s // 8 == t // 8) & (s <= t)).astype(np.float32)
    c[:, C_TRI128:C_TRI128 + 128] = (s <= t).astype(np.float32)
    col = np.arange(512)[None, :]
    c[:, C_SM64:C_SM64 + 512] = (col % 64 != 0).astype(np.float32)
    c[:, C_SM8:C_SM8 + 128] = (t % 8 != 0).astype(np.float32)
    c[:, C_IND16:C_IND16 + 16] = (s // 8 == np.arange(16)[None, :]).astype(np.float32)
    c[:, C_ONES1:C_ONES1 + 128] = 1.0
    for p in range(4):
        for m in range(128):
            c[2 * p + m // 64, C_SELP + p * 128 + m] = 1.0
    return c


class Buf:
    __slots__ = ("w", "r")

    def __init__(self):
        self.w = None
        self.r = []


class Eng:
    def __init__(self, name, sem):
        self.name = name
        self.sem = sem
        self.count = 0
        self.waited = {}
        self.prog = []


class Prog:
    def __init__(self, nc, stack):
        self.nc = nc
        self.stack = stack
        self.E = {}
        for n in ("tensor", "vector", "scalar", "gpsimd", "sync"):
            self.E[n] = Eng(n, stack.enter_context(nc.semaphore("s_" + n)))
        self.pools = {}
        self.pool_idx = {}
        for ename, k in (("sync", 20), ("gpsimd", 12)):
            self.pools[ename] = [[stack.enter_context(nc.semaphore("dq_%s_%d" % (ename, i))), 0] for i in range(k)]
            self.pool_idx[ename] = 0
        self.n_inst = 0

    def _wait(self, e, tok):
        sem, val = tok
        if sem is e.sem and (e.name == "tensor" or val > e.count):
            return
        key = id(sem)
        if e.waited.get(key, 0) < val:
            e.waited[key] = val
            e.prog.append(lambda eng, sem=sem, val=val: eng.wait_ge(sem, val))

    @staticmethod
    def _flat(bufs):
        out = []
        for b in bufs:
            if isinstance(b, (list, tuple)):
                out.extend(Prog._flat(b))
            else:
                out.append(b)
        return out

    def _deps(self, e, reads, writes):
        for b in reads:
            if b.w is not None:
                self._wait(e, b.w)
        for b in writes:
            if b.w is not None:
                self._wait(e, b.w)
            for t in b.r:
                self._wait(e, t)

    def emit(self, ename, fn, reads=(), writes=(), signal=True):
        e = self.E[ename]
        reads, writes = self._flat(reads), self._flat(writes)
        self._deps(e, reads, writes)
        self.n_inst += 1
        tok = (e.sem, e.count + 1)
        if signal:
            e.count += 1
            e.prog.append(lambda eng, sem=e.sem: fn(eng).then_inc(sem, 1))
        else:
            e.prog.append(lambda eng: fn(eng))
        for b in reads:
            b.r.append(tok)
        for b in writes:
            b.w = tok
            b.r = []
        return tok

    def dma(self, ename, out, in_, reads=(), writes=(), **kw):
        e = self.E[ename]
        pool = self.pools[ename]
        slot = pool[self.pool_idx[ename] % len(pool)]
        self.pool_idx[ename] += 1
        if slot[1] > 0:
            self._wait(e, (slot[0], slot[1]))
        reads, writes = self._flat(reads), self._flat(writes)
        self._deps(e, reads, writes)
        slot[1] += 16
        tok = (slot[0], slot[1])
        self.n_inst += 1
        e.prog.append(lambda eng, sem=slot[0]: eng.dma_start(out=out, in_=in_, **kw).then_inc(sem, 16))
        for b in reads:
            b.r.append(tok)
        for b in writes:
            b.w = tok
            b.r = []
        return tok

    def barrier(self):
        toks = [(e.sem, e.count) for e in self.E.values() if e.count > 0]
        for pool in self.pools.values():
            for s in pool:
                if s[1] > 0:
                    toks.append((s[0], s[1]))
        for e in self.E.values():
            for t in toks:
                self._wait(e, t)

    def finish(self):
        self.barrier()
        nc = self.nc
        with nc.Block() as block:
            for n, deco in (("sync", block.sync), ("gpsimd", block.gpsimd), ("tensor", block.tensor),
                            ("vector", block.vector), ("scalar", block.scalar)):
                prog = self.E[n].prog

                def body(eng, prog=prog):
                    for f in prog:
                        f(eng)
                deco(body)


def build_program(dbg_phase=None):
    nc = bass.Bass("TRN2", target_bir_lowering=False)

    def din(name, shape):
        return nc.dram_tensor(name, list(shape), F32, kind="ExternalInput").ap()

    def dout(name, shape):
        return nc.dram_tensor(name, list(shape), F32, kind="ExternalOutput").ap()

    xin = din("xin", [NT, D])
    S0d = din("S0", [16, 8, 128, 128])
    C0d = din("C0", [16, 8, 64, 128])
    n0d = din("n0", [128, 64])
    m0d = din("m0", [16, 8])
    vecs = din("vecs", [72, 128])
    w_hin = din("w_hin", [D, 4096])
    w_hout = din("w_hout", [D, D])
    w_min = din("w_min", [D, 3088])
    w_mout = din("w_mout", [D, D])
    w_up = din("w_up", [2, D, 4096])
    w_dn = din("w_dn", [2, 4096, D])
    gbias = din("gbias", [16])
    wfin = din("wfin", [D])
    cst = din("cst", [128, C_TOTAL])

    y_o = dout("y", [NT, D])
    Sp_o = dout("S_p", [8, 128, 128])
    Cp_o = dout("C_p", [8, 64, 128])
    np_o = dout("n_p", [8, 64])
    mp_o = dout("m_p", [8, 1])
    Ss_o = dout("S_s", [16, 8, 128, 128])
    Cs_o = dout("C_s", [16, 8, 64, 128])
    ns_o = dout("n_s", [16, 8, 64])
    ms_o = dout("m_s", [16, 8])
    dbg_o = dout("dbg", [128, KC * NT]) if dbg_phase is not None else None

    with ExitStack() as st:
        def sb(name, shape, dtype=F32):
            return st.enter_context(nc.sbuf_tensor(name, list(shape), dtype))

        P = Prog(nc, st)

        xT = sb("xT", [128, KC, NT])
        xnT = sb("xnT", [128, KC, NT], BF16)
        wsl = [sb("wsl%d" % i, [128, KC, 512], BF16) for i in range(4)]
        wsl_b = [Buf() for _ in range(4)]
        cs = sb("cs", [128, C_MAIN])
        cs_b = Buf()
        identb = sb("identb", [128, 128], BF16)
        identb_b = Buf()
        onesDb = sb("onesDb", [128, 128], BF16)
        ones128b = sb("ones128b", [128, 128], BF16)
        onesDb_b = Buf()
        vT = sb("vT", [128, 72])
        vT_b = Buf()
        lbt = sb("lbt", [128, 16])
        lbt_b = Buf()
        epsc = sb("epsc", [128, 1])
        onec = sb("onec", [128, 1])
        misc_b = Buf()
        WORK_F32 = (nc.sbuf_bytes_remaining - 512) // 4 // 8 * 8
        work = sb("work", [128, WORK_F32])
        wptr = [0]

        def walloc(shape, dtype=F32):
            n = 1
            for d_ in shape[1:]:
                n *= d_
            nf32 = (n * (4 if dtype == F32 else 2) + 3) // 4
            nf32 = (nf32 + 7) // 8 * 8
            off = wptr[0]
            wptr[0] += nf32
            assert wptr[0] <= WORK_F32, ("work overflow", wptr[0], WORK_F32)
            ap = work[0:shape[0], off:off + nf32]
            if dtype != F32:
                ap = ap.bitcast(dtype)
            ap = ap[:, 0:n]
            if len(shape) == 3:
                ap = ap.rearrange("p (a b) -> p a b", a=shape[1])
            return ap

        class Ring:
            def __init__(self, shape, dtype, n):
                self.items = [(walloc(shape, dtype), Buf()) for _ in range(n)]
                self.i = 0

            def next(self):
                it = self.items[self.i % len(self.items)]
                self.i += 1
                return it

        def phase_end(mark=0):
            P.barrier()
            wptr[0] = mark

        xT_b = [Buf() for _ in BLOCKS]
        xnT_b = [Buf() for _ in BLOCKS]

        pbank = [nc.alloc_psum_tensor("pb%d" % i, [128, 512], F32) for i in range(8)]
        pb_b = [Buf() for _ in range(8)]

        identf = cs[:, C_IDENT:C_IDENT + 128]
        onesD = cs[:, C_ONESD:C_ONESD + 128]
        ones128 = cs[:, C_ONES128:C_ONES128 + 128]

        def mm(out, lhsT, rhs, start, stop, reads, writes, signal=True, sgc=False):
            return P.emit("tensor", lambda e: e.matmul(out, lhsT=lhsT, rhs=rhs, start=start, stop=stop,
                                                       skip_group_check=sgc), reads, writes, signal)

        def act(out, in_, func, reads, writes, bias=None, scale=1.0, accum_out=None):
            kw = {}
            if bias is not None:
                kw["bias"] = bias
            if accum_out is not None:
                kw["accum_out"] = accum_out
            return P.emit("scalar", lambda e: e.activation(out=out, in_=in_, func=func, scale=scale, **kw),
                          reads, writes)

        def amul(out, in_, mul, reads, writes):
            return P.emit("scalar", lambda e: e.mul(out=out, in_=in_, mul=mul), reads, writes)

        def acopy(out, in_, reads, writes):
            return P.emit("scalar", lambda e: e.copy(out=out, in_=in_), reads, writes)

        def vtt(out, in0, in1, op, reads, writes, eng="vector"):
            return P.emit(eng, lambda e: e.tensor_tensor(out=out, in0=in0, in1=in1, op=op), reads, writes)

        def vts(out, in0, s1, s2, op0, op1, reads, writes, eng="vector"):
            if s2 is None:
                return P.emit(eng, lambda e: e.tensor_scalar(out=out, in0=in0, scalar1=s1, scalar2=None, op0=op0),
                              reads, writes)
            return P.emit(eng, lambda e: e.tensor_scalar(out=out, in0=in0, scalar1=s1, scalar2=s2, op0=op0, op1=op1),
                          reads, writes)

        def vstt(out, in0, scalar, in1, op0, op1, reads, writes, eng="vector"):
            return P.emit(eng, lambda e: e.scalar_tensor_tensor(out=out, in0=in0, scalar=scalar, in1=in1,
                                                               op0=op0, op1=op1), reads, writes)

        def vcopy(out, in_, reads, writes, eng="vector"):
            return P.emit(eng, lambda e: e.tensor_copy(out=out, in_=in_), reads, writes)

        def vrecip(out, in_, reads, writes):
            return P.emit("vector", lambda e: e.reciprocal(out=out, in_=in_), reads, writes)

        def vmemset(ap, val, writes, eng="vector"):
            return P.emit(eng, lambda e: e.memset(ap, val), (), writes)

        def vscan(out, d0, d1, init, op0, op1, reads, writes):
            return P.emit("vector", lambda e: e.tensor_tensor_scan(out=out, data0=d0, data1=d1, initial=init,
                                                                   op0=op0, op1=op1), reads, writes)

        def vreduce(out, in_, op, reads, writes):
            return P.emit("vector", lambda e: e.tensor_reduce(out=out, in_=in_, axis=AX.X, op=op), reads, writes)

        def vabs(out, in_, reads, writes):
            return P.emit("vector", lambda e: e.tensor_single_scalar(out=out, in_=in_, scalar=0.0, op=ALU.abs_max),
                          reads, writes)

        def bc_mid(ap2, n):
            return ap2.unsqueeze(2).to_broadcast([ap2.shape[0], ap2.shape[1], n])

        def bc_first(ap2, n):
            return ap2.unsqueeze(1).to_broadcast([ap2.shape[0], n, ap2.shape[1]])

        def flat(ap3):
            return ap3.rearrange("p a b -> p (a b)")

        def keep_warm(n, bank=7):
            for _ in range(n):
                mm(pbank[bank][:, :], identb[:], xnT[:, 0, 0:512], True, True, [identb_b], [pb_b[bank]], signal=False)

        P.dma("sync", cs[:], cst[:, 0:C_MAIN], writes=[cs_b])
        P.dma("gpsimd", identb[:], cst[:, C_IDENT:C_IDENT + 128], writes=[identb_b])
        P.dma("gpsimd", onesDb[:], cst[:, C_ONESD:C_ONESD + 128], writes=[onesDb_b])
        P.dma("gpsimd", ones128b[:], cst[:, C_ONES128:C_ONES128 + 128], writes=[onesDb_b])
        vmemset(epsc[:], EPS, [misc_b])
        vmemset(onec[:], 1.0, [misc_b])
        vraw = walloc([72, 128])
        vraw_b = Buf()
        lt = walloc([128, 40])
        lt_b = Buf()
        P.dma("sync", vraw[:], vecs[:, :], writes=[vraw_b])
        mm(pbank[0][:, 0:72], vraw[:, :], cs[0:72, C_IDENT:C_IDENT + 72], True, True, [vraw_b, cs_b], [pb_b[0]])
        vcopy(vT[:], pbank[0][:, 0:72], [pb_b[0]], [vT_b])
        l0, l1, l2 = vT[:, 32:40], vT[:, 40:48], vT[:, 48:56]
        vtt(lt[:, 0:8], l0, l1, ALU.max, [vT_b], [lt_b])
        vtt(lt[:, 0:8], lt[:, 0:8], l2, ALU.max, [vT_b, lt_b], [lt_b])
        for i, l in enumerate((l0, l1, l2)):
            vtt(lt[:, 8 + 8 * i:16 + 8 * i], l, lt[:, 0:8], ALU.subtract, [vT_b, lt_b], [lt_b])
        act(lt[:, 8:32], lt[:, 8:32], AF.Exp, [lt_b], [lt_b])
        vtt(lt[:, 32:40], lt[:, 8:16], lt[:, 16:24], ALU.add, [lt_b], [lt_b])
        vtt(lt[:, 32:40], lt[:, 32:40], lt[:, 24:32], ALU.add, [lt_b], [lt_b])
        vrecip(lt[:, 32:40], lt[:, 32:40], [lt_b], [lt_b])
        vtt(lbt[:, 0:8], lt[:, 8:16], lt[:, 32:40], ALU.mult, [lt_b], [lbt_b])
        vts(lbt[:, 8:16], lbt[:, 0:8], -1.0, 1.0, ALU.mult, ALU.add, [lbt_b], [lbt_b])

        xtok = Ring([128, D], F32, 2)
        for tt in range(NT // 128):
            bi = min(tt // 4, 4)
            xt, xt_b = xtok.next()
            P.dma("sync", xt[:], xin[tt * 128:(tt + 1) * 128, :], writes=[xt_b])
            for half in range(2):
                bk = (2 * tt + half) % 8
                for q in range(4):
                    kc = half * 4 + q
                    mm(pbank[bk][:, q * 128:(q + 1) * 128], xt[:, kc * 128:(kc + 1) * 128], identf, True, True,
                       [xt_b, cs_b], [pb_b[bk]], signal=(q == 3))
                dst = xT[:, half * 4:half * 4 + 4, tt * 128:(tt + 1) * 128]
                src = pbank[bk][:, :].rearrange("p (q t) -> p q t", q=4)
                if half == 0:
                    vcopy(dst, src, [pb_b[bk]], [xT_b[bi]])
                else:
                    acopy(dst, src, [pb_b[bk]], [xT_b[bi]])
        phase_end()

        slot_rr = [0]

        def load_rows(dram2d, row0, nrow, col_ranges, i):
            nk = nrow // 128
            src = dram2d[row0:row0 + nrow, :].rearrange("(kc p) n -> p kc n", p=128)
            ncols = sum(c1 - c0 for c0, c1 in col_ranges)
            assert nk * ncols <= KC * 512
            view = flat(wsl[i][:])[:, 0:nk * ncols].rearrange("p (a b) -> p a b", a=nk)
            off = 0
            for c0, c1 in col_ranges:
                P.dma("gpsimd", view[:, :, off:off + (c1 - c0)], src[:, :, c0:c1], writes=[wsl_b[i]])
                off += c1 - c0
            return view, wsl_b[i]

        def rmsnorm_to_xn(wcol0):
            mark = wptr[0]
            sq = Ring([128, 512], BF16, 4)
            rs = Ring([128, 512], F32, 2)
            for bi, (c0, nb) in enumerate(BLOCKS):
                bk = bi % 2
                for kc in range(KC):
                    s_t, s_b = sq.next()
                    act(s_t[:, :nb], xT[:, kc, c0:c0 + nb], AF.Square, [xT_b[bi]], [s_b])
                    mm(pbank[bk][:, :nb], onesDb[:], s_t[:, :nb], kc == 0, kc == KC - 1, [s_b, onesDb_b], [pb_b[bk]],
                       signal=True)
                r_t, r_b = rs.next()
                act(r_t[:, :nb], pbank[bk][:, :nb], AF.Ln, [pb_b[bk], misc_b], [r_b], bias=epsc[:, 0:1])
                act(r_t[:, :nb], r_t[:, :nb], AF.Exp, [r_b], [r_b], scale=-0.5)
                for kc in range(KC):
                    vstt(xnT[:, kc, c0:c0 + nb], xT[:, kc, c0:c0 + nb], vT[:, wcol0 + kc:wcol0 + kc + 1], r_t[:, :nb],
                         ALU.mult, ALU.mult, [xT_b[bi], vT_b, r_b], [xnT_b[bi]])
            phase_end(mark)

        opk = [0]

        def out_proj(wdram, row0, nk, oT, oT_b):
            wv, wb = load_rows(wdram, row0, 128 * nk, [(0, D)], 2)
            for bi, (c0, nb) in enumerate(BLOCKS):
                for m in range(8):
                    bk = 6 + opk[0] % 2
                    opk[0] += 1
                    for kc in range(nk):
                        mm(pbank[bk][:, :nb], wv[:, kc, m * 128:(m + 1) * 128], oT[:, kc, c0:c0 + nb],
                           kc == 0, kc == nk - 1, [wb, oT_b[kc][bi]], [pb_b[bk]], signal=(kc == nk - 1))
                    vtt(xT[:, m, c0:c0 + nb], xT[:, m, c0:c0 + nb], pbank[bk][:, :nb], ALU.add,
                        [pb_b[bk], xT_b[bi]], [xT_b[bi]])

        def run_interleaved(gens):
            alive = [g for g in gens if g is not None]
            while alive:
                for g in list(alive):
                    try:
                        next(g)
                    except StopIteration:
                        alive.remove(g)

        def ffn_phase(layer):
            loads = {}

            def load_slice(s):
                up = load_rows(w_up[layer], 0, D, [(s * 512, (s + 1) * 512)], 2 * (s % 2))
                dn = load_rows(w_dn[layer], s * 512, 512, [(0, D)], 2 * (s % 2) + 1)
                loads[s] = (up, dn)

            load_slice(0)
            rmsnorm_to_xn(16 + 8 * layer)
            relu_r = Ring([128, 512], F32, 3)
            hT = walloc([128, 8, NT], BF16).rearrange("p (s m) t -> p s m t", s=2)
            hT_b = [[Buf() for _ in BLOCKS] for _ in range(2)]
            kup = 0
            kdn = 0
            for s in range(8):
                if s + 1 < 8:
                    load_slice(s + 1)
                (uv, ub), (dv, db) = loads.pop(s)
                par = s % 2
                for bi, (c0, nb) in enumerate(BLOCKS):
                    for m in range(4):
                        bk = kup % 4
                        kup += 1
                        for kc in range(KC):
                            mm(pbank[bk][:, :nb], uv[:, kc, m * 128:(m + 1) * 128], xnT[:, kc, c0:c0 + nb],
                               kc == 0, kc == KC - 1, [ub, xnT_b[bi]], [pb_b[bk]], signal=(kc == KC - 1))
                        r_t, r_b = relu_r.next()
                        act(r_t[:, :nb], pbank[bk][:, :nb], AF.Relu, [pb_b[bk]], [r_b])
                        act(hT[:, par, m, c0:c0 + nb], r_t[:, :nb], AF.Square, [r_b], [hT_b[par][bi]])
                for bi, (c0, nb) in enumerate(BLOCKS):
                    for m in range(8):
                        bk = 4 + kdn % 4
                        kdn += 1
                        for kc in range(4):
                            mm(pbank[bk][:, :nb], dv[:, kc, m * 128:(m + 1) * 128], hT[:, par, kc, c0:c0 + nb],
                               kc == 0, kc == 3, [db, hT_b[par][bi]], [pb_b[bk]], signal=(kc == 3))
                        vtt(xT[:, m, c0:c0 + nb], xT[:, m, c0:c0 + nb], pbank[bk][:, :nb], ALU.add,
                            [pb_b[bk], xT_b[bi]], [xT_b[bi]])
            phase_end()

        def hgrn_phase():
            w_head0 = load_rows(w_hin, 0, D, [(0, 512)], 0)
            rmsnorm_to_xn(0)
            oT = walloc([128, 4, NT], BF16)
            oT_b = [[Buf() for _ in BLOCKS] for _ in range(4)]
            R = Ring
            t_f = R([128, 512], F32, 1)
            t_kk = R([128, 512], F32, 1)
            t_b = R([128, 512], F32, 1)
            t_e1 = R([128, 512], F32, 1)
            t_sq = R([128, 512], F32, 1)
            t_sg = R([128, 512], F32, 2)
            t_qt = R([128, 512], BF16, 2)
            t_kt = R([128, 512], BF16, 1)
            t_v = R([128, 4, 128], BF16, 1)
            t_ktk = R([128, 4, 128], BF16, 1)
            t_at = R([128, 4, 128], BF16, 1)
            t_sc = R([128, 64], F32, 2)
            t_S = R([128, 128], F32, 5)
            t_Sp = R([128, 128], BF16, 2)
            t_T = R([128, 8, 128], F32, 1)
            S0r = R([128, 16, 128], F32, 1)
            Sp16 = R([128, 16, 128], BF16, 1)
            Vblk = R([128, 16, 128], BF16, 1)
            Tq = R([128, 4, 128], F32, 1)
            p_osq = R([128, 256], BF16, 1)
            p_rs = R([128, 256], F32, 1)
            p_on = R([128, 256], F32, 1)
            hs = {}
            bq, bf_, bg, bv = 0, 1, 2, 3
            half_bufs = {}

            def halves(bf):
                if id(bf) not in half_bufs:
                    half_bufs[id(bf)] = (bf, Buf())
                return half_bufs[id(bf)]

            class HB(list):
                pass

            sce_bufs = {}

            def sce_halves(bf):
                if id(bf) not in sce_bufs:
                    sce_bufs[id(bf)] = (Buf(), Buf())
                return sce_bufs[id(bf)]

            def chain_post(h, bi, d):
                yield from st_chain(h, bi, d)
                yield from st_post(h, bi, d)

            def head_cols(h):
                return [(h * 512, (h + 1) * 512)]

            wslots = {0: w_head0}

            def geom(bi):
                c0, nb = BLOCKS[bi]
                is_s = bi == 4
                L = 8 if is_s else 64
                return c0, nb, is_s, nb // 128, L, nb // L

            def st_proj(h, bi):
                c0, nb, is_s, ntile, L, nch = geom(bi)
                if bi == 0 and h + 1 < 8:
                    wslots[h + 1] = load_rows(w_hin, 0, D, head_cols(h + 1), (h + 1) % 2)
                wv, wb = wslots[h]
                for comp, bk in ((1, bf_), (0, bq), (3, bg)):
                    for kc in range(KC):
                        mm(pbank[bk][:, :nb], wv[:, kc, comp * 128:(comp + 1) * 128], xnT[:, kc, c0:c0 + nb],
                           kc == 0, kc == KC - 1, [wb, xnT_b[bi]], [pb_b[bk]], signal=(kc == KC - 1))
                for tt in range(ntile):
                    for kc in range(KC):
                        mm(pbank[bv][:, tt * 128:(tt + 1) * 128], xnT[:, kc, c0 + tt * 128:c0 + (tt + 1) * 128],
                           wv[:, kc, 256:384], kc == 0, kc == KC - 1, [wb, xnT_b[bi]], [pb_b[bv]],
                           signal=(kc == KC - 1 and tt == ntile - 1))

            def st_ew(h, bi, out):
                c0, nb, is_s, ntile, L, nch = geom(bi)
                f_t, f_b = t_f.next()
                kk_t, kk_b = t_kk.next()
                b_t, b_b = t_b.next()
                e1_t, e1_b = t_e1.next()
                sq_t, sq_b = t_sq.next()
                sg_t, sg_b = t_sg.next()
                qt_t, qt_b = t_qt.next()
                kt_t, kt_b = t_kt.next()
                v_t, v_b = t_v.next()
                sc_t, sc_b = t_sc.next()
                f_b, kk_b, b_b, e1_b, sq_b, sg_b, qt_b, kt_b, sc_b = (halves(x) for x in (f_b, kk_b, b_b, e1_b, sq_b, sg_b, qt_b, kt_b, sc_b))
                sce_b = sce_halves(sc_b[0])

                def ew_half(hf, lo, hi, ch0, nchh):
                    cl = slice(lo, hi)
                    fb, kkb, bb, e1b, sqb, sgb, qtb, ktb, scb = (x[hf] for x in (f_b, kk_b, b_b, e1_b, sq_b, sg_b, qt_b, kt_b, sc_b))
                    sceb = sce_b[hf]

                    def sig3(dst, dst_b, src_ps, src_b):
                        act(dst[:, cl], src_ps[:, cl], AF.Exp, [src_b], [dst_b], scale=-1.0)
                        yield
                        act(dst[:, cl], dst[:, cl], AF.Ln, [dst_b, misc_b], [dst_b], bias=onec[:, 0:1])
                        yield
                        act(dst[:, cl], dst[:, cl], AF.Exp, [dst_b], [dst_b], scale=-1.0)
                        yield

                    yield from sig3(f_t, fb, pbank[bf_], pb_b[bf_])
                    vts(f_t[:, cl], f_t[:, cl], lbt[:, 8 + h:9 + h], lbt[:, h:h + 1], ALU.mult, ALU.add, [fb, lbt_b], [fb])
                    yield
                    vts(kk_t[:, cl], f_t[:, cl], -1.0, 1.0, ALU.mult, ALU.add, [fb], [kkb])
                    yield
                    act(f_t[:, cl], f_t[:, cl], AF.Ln, [fb], [fb])
                    yield
                    yield from sig3(sq_t, sqb, pbank[bq], pb_b[bq])
                    smc = cs[:, C_SM8:C_SM8 + (hi - lo)] if is_s else cs[:, C_SM64:C_SM64 + (hi - lo)]
                    vscan(b_t[:, cl], smc, f_t[:, cl], 0.0, ALU.mult, ALU.add, [fb, cs_b], [bb])
                    yield
                    bview = b_t[:, cl].rearrange("p (c l) -> p c l", l=L)
                    bL = bview[:, :, L - 1]
                    vcopy(sc_t[:, ch0:ch0 + nchh], bL, [bb], [scb])
                    yield
                    vtt(sq_t[:, cl], sq_t[:, cl], pbank[bq][:, cl], ALU.mult, [sqb, pb_b[bq]], [sqb])
                    yield
                    act(sc_t[:, 16 + ch0:16 + ch0 + nchh], bL, AF.Exp, [bb], [sceb])
                    yield
                    vtt(bview, bview, bc_mid(sc_t[:, ch0:ch0 + nchh], L), ALU.subtract, [bb, scb], [bb])
                    yield
                    act(e1_t[:, cl], b_t[:, cl], AF.Exp, [bb], [e1b])
                    yield
                    act(b_t[:, cl], b_t[:, cl], AF.Exp, [bb], [bb], scale=-1.0)
                    yield
                    vtt(qt_t[:, cl], sq_t[:, cl], e1_t[:, cl], ALU.mult, [sqb, e1b], [qtb])
                    yield
                    vtt(kt_t[:, cl], kk_t[:, cl], b_t[:, cl], ALU.mult, [kkb, bb], [ktb])
                    yield
                    yield from sig3(sg_t, sgb, pbank[bg], pb_b[bg])
                    vtt(sg_t[:, cl], sg_t[:, cl], pbank[bg][:, cl], ALU.mult, [sgb, pb_b[bg]], [sgb])
                    yield

                if is_s:
                    ga, gb_ = ew_half(0, 0, 64, 0, 8), ew_half(1, 64, 128, 8, 8)
                else:
                    ga, gb_ = ew_half(0, 0, 256, 0, 4), ew_half(1, 256, 512, 4, 4)
                alive = [ga, gb_]
                while alive:
                    for g in list(alive):
                        try:
                            next(g)
                            yield
                        except StopIteration:
                            alive.remove(g)
                acopy(v_t[:, 0:ntile, :], pbank[bv][:, :nb].rearrange("p (a b) -> p a b", b=128), [pb_b[bv]], [v_b])
                yield
                out["d"] = dict(qt=(qt_t, HB(qt_b)), kt=(kt_t, HB(kt_b)), v=(v_t, v_b), sc=(sc_t, HB(list(sc_b) + list(sce_b))), sg=(sg_t, HB(sg_b)))

            def st_mid(h, bi, d):
                c0, nb, is_s, ntile, L, nch = geom(bi)
                qt_t, qt_b = d["qt"]
                kt_t, kt_b = d["kt"]
                v_t, v_b = d["v"]
                sc_t, sc_b = d["sc"]
                ktk_t, ktk_b = t_ktk.next()
                at_t, at_b = t_at.next()
                for tt in range(ntile):
                    mm(pbank[0][:, tt * 128:(tt + 1) * 128], kt_t[:, tt * 128:(tt + 1) * 128],
                       qt_t[:, tt * 128:(tt + 1) * 128], True, True, [kt_b, qt_b], [pb_b[0]],
                       signal=(tt == ntile - 1))
                for tt in range(ntile):
                    mm(pbank[1][:, tt * 128:(tt + 1) * 128], kt_t[:, tt * 128:(tt + 1) * 128], identb[:],
                       True, True, [kt_b, identb_b], [pb_b[1]], signal=(tt == ntile - 1))
                mask = cs[:, C_BD128:C_BD128 + 128] if is_s else cs[:, C_MASKH:C_MASKH + 128]
                vtt(at_t[:, 0:ntile, :], pbank[0][:, :nb].rearrange("p (a b) -> p a b", b=128),
                    bc_first(mask, ntile), ALU.mult, [pb_b[0], cs_b], [at_b])
                acopy(ktk_t[:, 0:ntile, :], pbank[1][:, :nb].rearrange("p (a b) -> p a b", b=128), [pb_b[1]], [ktk_b])
                for tt in range(ntile):
                    mm(pbank[4][:, tt * 128:(tt + 1) * 128], v_t[:, tt, :], at_t[:, tt, :], tt == 0, False,
                       [v_b, at_b], [pb_b[4]], signal=False, sgc=True)
                d["ktk"] = (ktk_t, ktk_b)
                if not is_s:
                    for ci in range(nch):
                        tt, hf = ci // 2, ci % 2
                        r0 = hf * 64
                        bk = 5 + hf
                        mm(pbank[bk][:, tt * 128:(tt + 1) * 128], ktk_t[r0:r0 + 64, tt, :], v_t[r0:r0 + 64, tt, :],
                           True, True, [ktk_b, v_b], [pb_b[bk]], signal=(ci >= nch - 2))

            def st_chain(h, bi, d):
                c0, nb, is_s, ntile, L, nch = geom(bi)
                qt_t, qt_b = d["qt"]
                v_t, v_b = d["v"]
                sc_t, sc_b = d["sc"]
                ktk_t, ktk_b = d["ktk"]
                if bi == 0:
                    S0t, S0b = S0r.next()
                    P.dma("sync", S0t[:], S0d[:, h, :, :].rearrange("j d v -> d j v"), writes=[S0b])
                    S_t, S_b = t_S.next()
                    vmemset(S_t[:], 0.0, [S_b])
                    hs["S"] = (S_t, S_b)
                    hs["S0"] = (S0t, S0b)
                S_t, S_b = hs["S"]
                yield
                if not is_s:
                    for ci in range(nch):
                        Sn_t, Sn_b = t_S.next()
                        vstt(Sn_t[:], S_t[:], sc_t[:, 16 + ci:17 + ci],
                             pbank[5 + ci % 2][:, (ci // 2) * 128:(ci // 2 + 1) * 128], ALU.mult, ALU.add,
                             [S_b, sc_b, pb_b[5 + ci % 2]], [Sn_b])
                        Sp_t, Sp_b = t_Sp.next()
                        amul(Sp_t[:], S_t[:], sc_t[:, 16 + ci:17 + ci], [S_b, sc_b], [Sp_b])
                        mm(pbank[4][:, ci * 64:(ci + 1) * 64], Sp_t[:], qt_t[:, ci * 64:(ci + 1) * 64], False, True,
                           [Sp_b, qt_b], [pb_b[4]], signal=True, sgc=True)
                        S_t, S_b = Sn_t, Sn_b
                        yield
                    hs["S"] = (S_t, S_b)
                else:
                    S0t, S0b = hs["S0"]
                    P.dma("sync", Sp_o[h, :, :], S_t[:], reads=[S_b])
                    sp16, sp16_b = Sp16.next()
                    vtt(sp16[:], S0t[:], bc_mid(sc_t[:, 16:32], 128), ALU.mult, [S0b, sc_b], [sp16_b])
                    for j in range(16):
                        mm(pbank[4][:, j * 8:(j + 1) * 8], sp16[:, j, :], qt_t[:, j * 8:(j + 1) * 8], False, True,
                           [sp16_b, qt_b], [pb_b[4]], signal=(j == 15), sgc=True)
                    vb_t, vb_b = Vblk.next()
                    vtt(vb_t[:], bc_first(v_t[:, 0, :], 16), bc_mid(cs[:, C_IND16:C_IND16 + 16], 128), ALU.mult,
                        [v_b, cs_b], [vb_b])
                    vtt(S0t[:], S0t[:], bc_mid(sc_t[:, 16:32], 128), ALU.mult, [S0b, sc_b], [S0b])
                    for q in range(4):
                        bk = 6 + q % 2
                        mm(pbank[bk][:, :], ktk_t[:, 0, :], flat(vb_t[:, 4 * q:4 * q + 4, :]),
                           True, True, [ktk_b, vb_b], [pb_b[bk]])
                        vtt(S0t[:, 4 * q:4 * q + 4, :], S0t[:, 4 * q:4 * q + 4, :],
                            pbank[bk][:, :].rearrange("p (a b) -> p a b", b=128), ALU.add, [S0b, pb_b[bk]], [S0b])
                        yield
                    P.dma("sync", Ss_o[:, h, :, :].rearrange("j d v -> d j v"), S0t[:], reads=[S0b])

            def st_post(h, bi, d):
                c0, nb, is_s, ntile, L, nch = geom(bi)
                hl = h % 4
                sg_t, sg_b = d["sg"]
                step = 256 if nb == 512 else nb
                for lo in range(0, nb, step):
                    cl = slice(lo, lo + step)
                    osq_t, osq_b = p_osq.next()
                    rs_t, rs_b = p_rs.next()
                    on_t, on_b = p_on.next()
                    act(osq_t[:, 0:step], pbank[4][:, cl], AF.Square, [pb_b[4]], [osq_b])
                    yield
                    mm(pbank[7][:, cl], ones128b[:], osq_t[:, 0:step], True, True, [osq_b, onesDb_b], [pb_b[7]])
                    yield
                    act(rs_t[:, 0:step], pbank[7][:, cl], AF.Ln, [pb_b[7], misc_b], [rs_b], bias=epsc[:, 0:1])
                    yield
                    act(rs_t[:, 0:step], rs_t[:, 0:step], AF.Exp, [rs_b], [rs_b], scale=-0.5)
                    yield
                    vtt(on_t[:, 0:step], pbank[4][:, cl], rs_t[:, 0:step], ALU.mult, [pb_b[4], rs_b], [on_b])
                    yield
                    vstt(oT[:, hl, c0 + lo:c0 + lo + step], on_t[:, 0:step], vT[:, 56 + h:57 + h], sg_t[:, cl], ALU.mult, ALU.mult,
                         [on_b, vT_b, sg_b], [oT_b[hl][bi]])
                    yield

            items = [(h, bi) for h in range(8) for bi in range(len(BLOCKS))]
            hold = {}
            st_proj(*items[0])
            run_interleaved([st_ew(items[0][0], items[0][1], hold)])
            cur = hold["d"]
            st_mid(items[0][0], items[0][1], cur)
            for i, (h, bi) in enumerate(items):
                nx = items[i + 1] if i + 1 < len(items) else None
                if nx:
                    st_proj(*nx)
                run_interleaved([st_ew(nx[0], nx[1], hold) if nx else None, chain_post(h, bi, cur)])
                if bi == len(BLOCKS) - 1 and h % 4 == 3:
                    out_proj(w_hout, (h // 4) * 512, 4, oT, oT_b)
                if nx:
                    cur = hold["d"]
                    st_mid(nx[0], nx[1], cur)
            phase_end()

        def mlstm_phase():
            pair0 = (load_rows(w_min, 0, D, [(0, 512)], 0), load_rows(w_min, 0, D, [(512, 768)], 1))
            rmsnorm_to_xn(8)
            R = Ring
            oT = walloc([128, 4, NT], BF16)
            oT_b = [[Buf() for _ in BLOCKS] for _ in range(4)]
            rsel = walloc([8, 65])
            rsel_b = Buf()
            tokq = walloc([128, 17, 24])
            tokq_b = Buf()
            nT = walloc([128, 16, 8])
            nT_b = Buf()
            selp = walloc([8, 512])
            selp_b = Buf()
            P.dma("sync", selp[:], cst[0:8, C_SELP:C_SELP + 512], writes=[selp_b])
            mark = wptr[0]
            wg = walloc([128, KC, 16], BF16)
            wg_b = Buf()
            P.dma("gpsimd", wg[:], w_min[:, 3072:3088].rearrange("(kc p) n -> p kc n", p=128), writes=[wg_b])
            gb = walloc([8, 4])
            gb_b = Buf()
            P.dma("sync", gb[:, 0:1], gbias[0:8].rearrange("(h o) -> h o", o=1), writes=[gb_b])
            P.dma("sync", gb[:, 1:2], gbias[8:16].rearrange("(h o) -> h o", o=1), writes=[gb_b])
            vts(gb[:, 2:4], gb[:, 0:2], 1.0 / CAP, None, ALU.mult, None, [gb_b], [gb_b])
            m0T = walloc([8, 16])
            m0T_b = Buf()
            P.dma("sync", m0T[:], m0d.rearrange("j h -> h j"), writes=[m0T_b], allow_slow_non_contiguous=True)
            stat = walloc([8, 96])
            stat_b = Buf()
            g1 = R([8, 512], F32, 2)
            g2 = R([8, 512], F32, 2)
            g3 = R([8, 512], F32, 2)
            g4 = R([8, 512], F32, 2)
            g5 = R([8, 512], F32, 2)
            g6 = R([8, 512], F32, 2)
            stat_bs = [Buf() for _ in BLOCKS]

            def gate_block(bi, bki, bkf):
                c0, nb = BLOCKS[bi]
                is_s = bi == 4
                L = 8 if is_s else 128
                nch = nb // L
                ntile = nb // 128
                sb_ = stat_bs[bi]
                for gsel, bk in ((0, bki), (1, bkf)):
                    for kc in range(KC):
                        mm(pbank[bk][0:8, :nb], wg[:, kc, gsel * 8:(gsel + 1) * 8], xnT[:, kc, c0:c0 + nb],
                           kc == 0, kc == KC - 1, [wg_b, xnT_b[bi]], [pb_b[bk]], signal=(kc == KC - 1))
                    yield
                li_t, li_b = g1.next()
                lf_t, lf_b = g2.next()
                b_t, b_b = g3.next()
                a_t, a_b = g4.next()
                e_t, e_b = g5.next()
                w_t, w_b = g6.next()
                act(li_t[:, :nb], pbank[bki][0:8, :nb], AF.Tanh, [pb_b[bki], gb_b], [li_b], bias=gb[:, 2:3], scale=1.0 / CAP)
                yield
                vts(li_t[:, :nb], li_t[:, :nb], CAP, None, ALU.mult, None, [li_b], [li_b])
                yield
                act(lf_t[:, :nb], pbank[bkf][0:8, :nb], AF.Tanh, [pb_b[bkf], gb_b], [lf_b], bias=gb[:, 3:4], scale=1.0 / CAP)
                yield
                act(lf_t[:, :nb], lf_t[:, :nb], AF.Exp, [lf_b], [lf_b], scale=-CAP)
                yield
                act(lf_t[:, :nb], lf_t[:, :nb], AF.Ln, [lf_b, misc_b], [lf_b], bias=onec[0:8, 0:1])
                yield
                vts(lf_t[:, :nb], lf_t[:, :nb], -1.0, None, ALU.mult, None, [lf_b], [lf_b])
                yield
                if is_s:
                    vscan(b_t[:, :nb], cs[0:8, C_SM8:C_SM8 + 128], lf_t[:, :nb], 0.0, ALU.mult, ALU.add, [lf_b, cs_b], [b_b])
                    yield
                else:
                    for tt in range(ntile):
                        vscan(b_t[:, tt * 128:(tt + 1) * 128], cs[0:8, C_ONES1:C_ONES1 + 128], lf_t[:, tt * 128:(tt + 1) * 128],
                              0.0, ALU.mult, ALU.add, [lf_b, cs_b], [b_b])
                        yield
                vtt(a_t[:, :nb], li_t[:, :nb], b_t[:, :nb], ALU.subtract, [li_b, b_b], [a_b])
                yield
                bview = b_t[:, :nb].rearrange("p (c l) -> p c l", l=L)
                aview = a_t[:, :nb].rearrange("p (c l) -> p c l", l=L)
                so = 16 if is_s else 4 * bi
                vcopy(stat[:, so:so + nch], bview[:, :, L - 1], [b_b], [sb_])
                yield
                vreduce(stat[:, 32 + so:32 + so + nch], aview, ALU.max, [a_b], [sb_])
                yield
                act(e_t[:, :nb], a_t[:, :nb], AF.Exp, [a_b], [e_b])
                yield
                vtt(w_t[:, :nb].rearrange("p (c l) -> p c l", l=L), aview, bc_mid(stat[:, so:so + nch], L), ALU.add,
                    [a_b, sb_], [w_b])
                yield
                act(w_t[:, :nb], w_t[:, :nb], AF.Exp, [w_b], [w_b])
                yield
                act(b_t[:, :nb], b_t[:, :nb], AF.Exp, [b_b], [b_b], scale=-1.0)
                yield
                for tt in range(ntile):
                    gt = c0 // 128 + tt
                    for qi, (src, srcb) in enumerate(((e_t, e_b), (w_t, w_b), (b_t, b_b))):
                        mm(pbank[2][:, gt * 24 + qi * 8:gt * 24 + qi * 8 + 8], src[0:8, tt * 128:(tt + 1) * 128],
                           cs[0:8, C_IDENT:C_IDENT + 8], True, True, [srcb, cs_b], [pb_b[2]])
                    yield

            run_interleaved([gate_block(0, 0, 1), gate_block(1, 4, 5)])
            run_interleaved([gate_block(2, 0, 1), gate_block(3, 4, 5)])
            run_interleaved([gate_block(4, 0, 1)])
            stat_b = stat_bs
            vcopy(flat(tokq[:]), pbank[2][:, 0:17 * 24], [pb_b[2]], [tokq_b])
            vscan(stat[:, 64:80], stat[:, 32:48], stat[:, 0:16], 0.0, ALU.max, ALU.add, [stat_b], [stat_b])
            vtt(stat[:, 80:96], stat[:, 48:64], m0T[:], ALU.max, [stat_b, m0T_b], [stat_b])
            vtt(stat[:, 80:96], stat[:, 80:96], stat[:, 16:32], ALU.add, [stat_b], [stat_b])
            act(rsel[:, 0:32], stat[:, 0:32], AF.Exp, [stat_b], [rsel_b])
            act(rsel[:, 32:48], m0T[:], AF.Exp, [m0T_b], [rsel_b])
            act(rsel[:, 48:64], stat[:, 80:96], AF.Exp, [stat_b], [rsel_b], scale=-1.0)
            act(rsel[:, 64:65], stat[:, 79:80], AF.Exp, [stat_b], [rsel_b], scale=-1.0)
            P.dma("sync", mp_o[:, :], stat[:, 79:80], reads=[stat_b])
            P.dma("sync", ms_o.rearrange("j h -> h j"), stat[:, 80:96], reads=[stat_b], allow_slow_non_contiguous=True)
            n0t = walloc([128, 128])
            n0t_b = Buf()
            P.dma("sync", n0t[:, 0:64], n0d[:, :], writes=[n0t_b])
            P.dma("sync", n0t[:, 64:128], n0d[:, :], writes=[n0t_b])
            mm(pbank[3][:, 0:128], n0t[:], identf, True, True, [n0t_b, cs_b], [pb_b[3]])
            vcopy(flat(nT[:]), pbank[3][:, 0:128], [pb_b[3]], [nT_b])
            phase_end(mark)

            qT_r = R([128, 512], BF16, 2)
            kT_r = R([128, 512], BF16, 2)
            vaug_r = R([128, 2, 129], BF16, 3)
            for it, itb in vaug_r.items:
                vmemset(it[:, :, 128:129], 1.0, [itb])
            k2_r = R([128, 2, 64], BF16, 3)
            sigo_r = R([128, 256], F32, 2)
            at_r = R([128, 128], BF16, 4)
            sm_r = R([128, 16], F32, 3)
            ssq_r = R([128, 4], F32, 3)
            t4_r = R([128, 256], F32, 2)
            hht_r = R([128, 256], BF16, 2)
            junk_r = R([128, 128], F32, 2)
            Cst = walloc([128, 129])
            Cst_b = Buf()
            Cbf_r = R([128, 129], BF16, 2)
            selsb_r = R([128, 65], F32, 2)
            C0a_r = R([128, 16, 129], F32, 1)
            C0bf_r = R([128, 16, 129], BF16, 1)
            qm = walloc([128, NT], BF16)
            qm_b = Buf()
            vmemset(qm[:], 0.0, [qm_b])
            vblk_r = R([128, 16, 129], BF16, 1)
            nout_r = R([128, 16], F32, 2)
            nout2_r = R([16, 128], F32, 2)

            def load_pair(p):
                a = load_rows(w_min, 0, D, [(p * 768, p * 768 + 512)], 2 * (p % 2))
                b = load_rows(w_min, 0, D, [(p * 768 + 512, (p + 1) * 768)], 2 * (p % 2) + 1)
                return a, b

            b7_st = [pb_b[0], pb_b[0]]
            b7_ht = pb_b[1]
            st_ps = [pbank[0][:, 0:128], pbank[0][:, 128:256]]
            ht_ps = pbank[1][:, 0:256]

            def front_a(p, t, pc, wa, wa_b, wbv, wb_b, sel_t, sel_b):
                bi = min(t // 4, 4)
                c0, nb = BLOCKS[bi]
                is_s = bi == 4
                tt = t - 4 * bi
                if tt == 0:
                    qT_t, qT_b = qT_r.next()
                    kT_t, kT_b = kT_r.next()
                    pc["qk"] = (qT_t, qT_b, kT_t, kT_b)
                    for which, bk in ((0, 2), (1, 3)):
                        for kc in range(KC):
                            mm(pbank[bk][:, :nb], wa[:, kc, which * 128:(which + 1) * 128], xnT[:, kc, c0:c0 + nb],
                               kc == 0, kc == KC - 1, [wa_b, xnT_b[bi]], [pb_b[bk]], signal=(kc == KC - 1))
                            yield
                    vts(qT_t[:, :nb], pbank[2][:, :nb], 0.125, None, ALU.mult, None, [pb_b[2]], [qT_b])
                    acopy(kT_t[:, :nb], pbank[3][:, :nb], [pb_b[3]], [kT_b])
                    yield
                    if is_s:
                        C0a, C0a_b = C0a_r.next()
                        for hh in range(2):
                            P.dma("sync", C0a[hh * 64:(hh + 1) * 64, :, 0:128],
                                  C0d[:, 2 * p + hh, :, :].rearrange("j k v -> k j v"), writes=[C0a_b])
                            vcopy(C0a[hh * 64:(hh + 1) * 64, :, 128], nT[hh * 64:(hh + 1) * 64, :, 2 * p + hh],
                                  [nT_b], [C0a_b])
                        vtt(C0a[:], C0a[:], bc_mid(sel_t[:, 32:48], 129), ALU.mult, [C0a_b, sel_b], [C0a_b])
                        yield
                        C0bf, C0bf_b = C0bf_r.next()
                        acopy(C0bf[:], C0a[:], [C0a_b], [C0bf_b])
                        qmv = qm[:].rearrange("p (j x) -> p j x", x=136)[:, :, 0:8]
                        vcopy(qmv, qT_t[:, 0:128].rearrange("p (j i) -> p j i", i=8), [qT_b], [qm_b])
                        pc["c0"] = (C0a, C0a_b, C0bf, C0bf_b)
                        yield
                tc0 = t * 128
                for kc in range(KC):
                    mm(pbank[2][:, 0:384], xnT[:, kc, tc0:tc0 + 128], wa[:, kc, 128:512], kc == 0, kc == KC - 1,
                       [wa_b, xnT_b[bi]], [pb_b[2]], signal=(kc == KC - 1))
                    yield
                for kc in range(KC):
                    mm(pbank[3][:, 0:256], xnT[:, kc, tc0:tc0 + 128], wbv[:, kc, 0:256], kc == 0, kc == KC - 1,
                       [wb_b, xnT_b[bi]], [pb_b[3]], signal=(kc == KC - 1))
                    yield
                pc["fa"][t] = (pc["qk"], pc.get("c0"))

            def front_b(p, t, pc, wa, wa_b, wbv, wb_b, sel_t, sel_b):
                bi = min(t // 4, 4)
                c0, nb = BLOCKS[bi]
                is_s = bi == 4
                tt = t - 4 * bi
                (qT_t, qT_b, kT_t, kT_b), c0pack = pc["fa"].pop(t)
                gt = t
                tc0 = t * 128
                tl = tt * 128
                va_t, va_b = vaug_r.next()
                k2_t, k2_b = k2_r.next()
                so_t, so_b = sigo_r.next()
                vcopy(va_t[:, :, 0:128], pbank[2][:, 128:384].rearrange("p (a b) -> p a b", b=128), [pb_b[2]], [va_b])
                yield
                vtt(k2_t[:], pbank[2][:, 0:128].rearrange("p (a b) -> p a b", b=64),
                    bc_mid(tokq[:, gt, 8 + 2 * p:10 + 2 * p], 64), ALU.mult, [pb_b[2], tokq_b], [k2_b])
                yield
                act(so_t[:], pbank[3][:, 0:256], AF.Exp, [pb_b[3]], [so_b], scale=-1.0)
                yield
                act(so_t[:], so_t[:], AF.Ln, [so_b, misc_b], [so_b], bias=onec[:, 0:1])
                yield
                act(so_t[:], so_t[:], AF.Exp, [so_b], [so_b], scale=-1.0)
                yield
                mask = cs[:, C_BD128:C_BD128 + 128] if is_s else cs[:, C_TRI128:C_TRI128 + 128]
                ob = 4 + (gt % 2)
                ov = pbank[ob][:, :].rearrange("p (a b) -> p a b", a=2)
                dv_ = pbank[6][:, :].rearrange("p (a b) -> p a b", a=2)
                Cbf_t, Cbf_b = pc["cbf"]
                for hh in range(2):
                    r0 = hh * 64
                    hg = 2 * p + hh
                    mm(st_ps[hh], kT_t[r0:r0 + 64, tl:tl + 128], qT_t[r0:r0 + 64, tl:tl + 128], True, True,
                       [kT_b, qT_b], [b7_st[hh]])
                    yield
                    at_t, at_b = at_r.next()
                    vstt(at_t[:], st_ps[hh], tokq[:, gt, hg:hg + 1], mask, ALU.mult, ALU.mult,
                         [b7_st[hh], tokq_b, cs_b], [at_b])
                    yield
                    mm(ov[:, hh, 0:129], at_t[:], va_t[:, hh, :], True, False, [at_b, va_b], [pb_b[ob]], signal=True, sgc=True)
                    if not is_s:
                        mm(ov[:, hh, 0:129], qT_t[r0:r0 + 64, tl:tl + 128], Cbf_t[r0:r0 + 64, :], False, True,
                           [qT_b, Cbf_b], [pb_b[ob]], sgc=True)
                        yield
                    else:
                        C0a, C0a_b, C0bf, C0bf_b = c0pack
                        for j in range(16):
                            mm(ov[:, hh, 0:129], qm[r0:r0 + 64, j * 128:(j + 1) * 128], C0bf[r0:r0 + 64, j, :],
                               False, j == 15, [qm_b, C0bf_b], [pb_b[ob]], signal=True, sgc=True)
                        yield
                pc["post"] = (ob, ov, so_t, so_b, bi, tt, tc0)
                if not is_s:
                    for hh in range(2):
                        mm(dv_[:, hh, 0:129], flat(k2_t[:]), va_t[:, hh, :], True, True,
                           [k2_b, va_b], [pb_b[6]], signal=True)
                    yield
                    for hh in range(2):
                        r0 = hh * 64
                        vstt(Cst[r0:r0 + 64, :], Cst[r0:r0 + 64, :], sel_t[r0:r0 + 64, gt:gt + 1], dv_[r0:r0 + 64, hh, 0:129],
                             ALU.mult, ALU.add, [Cst_b, sel_b, pb_b[6]], [Cst_b])
                        yield
                    Cbf_t, Cbf_b = Cbf_r.next()
                    acopy(Cbf_t[:], Cst[:], [Cst_b], [Cbf_b])
                    pc["cbf"] = (Cbf_t, Cbf_b)
                    yield
                    if gt == 15:
                        vts(Cst[:], Cst[:], sel_t[:, 64:65], None, ALU.mult, None, [Cst_b, sel_b], [Cst_b])
                        P.dma("sync", Cp_o[2 * p:2 * p + 2, :, :].rearrange("h k v -> (h k) v"), Cst[:, 0:128], reads=[Cst_b])
                        P.dma("sync", np_o[2 * p:2 * p + 2, :].rearrange("h (k o) -> (h k) o", o=1), Cst[:, 128:129],
                              reads=[Cst_b])
                        yield
                else:
                    C0a, C0a_b, C0bf, C0bf_b = c0pack
                    for hh in range(2):
                        r0 = hh * 64
                        vb_t, vb_b = vblk_r.next()
                        vtt(vb_t[:], bc_first(va_t[:, hh, :], 16), bc_mid(cs[:, C_IND16:C_IND16 + 16], 129), ALU.mult,
                            [va_b, cs_b], [vb_b])
                        yield
                        vtt(C0a[r0:r0 + 64, :, :], C0a[r0:r0 + 64, :, :], bc_mid(sel_t[r0:r0 + 64, 16:32], 129), ALU.mult,
                            [C0a_b, sel_b], [C0a_b])
                        yield
                        for g in range(6):
                            j0 = 3 * g
                            nj = min(3, 16 - j0)
                            mm(pbank[6][:, 0:nj * 129], flat(k2_t[:]), flat(vb_t[:, j0:j0 + nj, :]), True, True,
                               [k2_b, vb_b], [pb_b[6]])
                            vtt(C0a[r0:r0 + 64, j0:j0 + nj, :], C0a[r0:r0 + 64, j0:j0 + nj, :],
                                pbank[6][r0:r0 + 64, 0:nj * 129].rearrange("p (a b) -> p a b", b=129), ALU.add,
                                [C0a_b, pb_b[6]], [C0a_b])
                            yield
                    vtt(C0a[:], C0a[:], bc_mid(sel_t[:, 48:64], 129), ALU.mult, [C0a_b, sel_b], [C0a_b])
                    for hh in range(2):
                        P.dma("sync", Cs_o[:, 2 * p + hh, :, :].rearrange("j k v -> k j v"),
                              C0a[hh * 64:(hh + 1) * 64, :, 0:128], reads=[C0a_b])
                    yield
                    no_t, no_b = nout_r.next()
                    vcopy(no_t[:], C0a[:, :, 128], [C0a_b], [no_b])
                    mm(pbank[6][0:16, 0:128], no_t[:], identf, True, True, [no_b, cs_b], [pb_b[6]])
                    no2_t, no2_b = nout2_r.next()
                    vcopy(no2_t[:], pbank[6][0:16, 0:128], [pb_b[6]], [no2_b])
                    P.dma("sync", ns_o[:, 2 * p:2 * p + 2, :].rearrange("j h k -> j (h k)"), no2_t[:], reads=[no2_b])
                    yield

            def back(p, t, post):
                ob, ov, so_t, so_b, bi, tt, tc0 = post
                gt = t
                sm_t, sm_b = sm_r.next()
                den = ov[:, :, 128]
                einv2 = tokq[:, gt, 16 + 2 * p:18 + 2 * p]
                act(sm_t[:, 0:2], den, AF.Abs, [pb_b[ob]], [sm_b])
                yield
                ssq_t, ssq_b = ssq_r.next()
                for hh in range(2):
                    jk_t, jk_b = junk_r.next()
                    act(jk_t[:], ov[:, hh, 0:128], AF.Square, [pb_b[ob]], [jk_b, ssq_b], accum_out=ssq_t[:, hh:hh + 1])
                    yield
                vtt(sm_t[:, 0:2], sm_t[:, 0:2], einv2, ALU.max, [sm_b, tokq_b], [sm_b])
                yield
                vtt(sm_t[:, 2:4], sm_t[:, 0:2], sm_t[:, 0:2], ALU.mult, [sm_b], [sm_b])
                yield
                vstt(sm_t[:, 6:8], sm_t[:, 2:4], EPS * 128.0, ssq_t[:, 0:2], ALU.mult, ALU.add, [sm_b, ssq_b], [sm_b])
                yield
                act(sm_t[:, 8:10], sm_t[:, 6:8], AF.Ln, [sm_b], [sm_b], scale=1.0 / 128.0)
                yield
                act(sm_t[:, 10:12], sm_t[:, 8:10], AF.Exp, [sm_b], [sm_b], scale=-0.5)
                yield
                t4_t, t4_b = t4_r.next()
                hht_t, hht_b = hht_r.next()
                vtt(t4_t[:].rearrange("p (a b) -> p a b", a=2), ov[:, :, 0:128], bc_mid(sm_t[:, 10:12], 128), ALU.mult,
                    [pb_b[ob], sm_b], [t4_b])
                yield
                vtt(hht_t[:], t4_t[:], so_t[:], ALU.mult, [t4_b, so_b], [hht_b])
                yield
                for hh in range(2):
                    mm(ht_ps[:, hh * 128:(hh + 1) * 128], hht_t[:, hh * 128:(hh + 1) * 128], identb[:], True, True,
                       [hht_b, identb_b], [b7_ht], signal=True)
                yield
                for hh in range(2):
                    hl = (2 * p + hh) % 4
                    vts(oT[:, hl, tc0:tc0 + 128], ht_ps[:, hh * 128:(hh + 1) * 128], vT[:, 64 + 2 * p + hh:65 + 2 * p + hh], None,
                        ALU.mult, None, [b7_ht, vT_b], [oT_b[hl][bi]])
                    yield

            nxt = pair0
            for p in range(4):
                (wa, wa_b), (wbv, wb_b) = nxt
                if p + 1 < 4:
                    nxt = load_pair(p + 1)
                sel_t, sel_b = selsb_r.next()
                mm(pbank[6][:, 0:65], selp[0:8, p * 128:(p + 1) * 128], rsel[:, :], True, True,
                   [selp_b, rsel_b], [pb_b[6]])
                vcopy(sel_t[:], pbank[6][:, 0:65], [pb_b[6]], [sel_b])
                vmemset(Cst[:], 0.0, [Cst_b])
                Cbf_t, Cbf_b = Cbf_r.next()
                vmemset(Cbf_t[:], 0.0, [Cbf_b])
                pc = {"cbf": (Cbf_t, Cbf_b), "fa": {}}
                args = (pc, wa, wa_b, wbv, wb_b, sel_t, sel_b)
                run_interleaved([front_a(p, 0, *args)])
                prev_post = None
                for t in range(17):
                    g_a = front_a(p, t + 1, *args) if t + 1 < 17 else None
                    g_b = front_b(p, t, *args)
                    g_back = back(p, t - 1, prev_post) if prev_post is not None else None
                    for _ in range(5):
                        next(g_b)
                    run_interleaved([g_b, g_a, g_back])
                    prev_post = pc["post"]
                run_interleaved([back(p, 16, prev_post)])
                if p % 2 == 1:
                    out_proj(w_mout, (p // 2) * 512, 4, oT, oT_b)
            phase_end()

        def final_phase():
            wfin_bc = walloc([128, D])
            bc_b = Buf()
            P.dma("sync", wfin_bc[:], wfin.partition_broadcast(128), writes=[bc_b])
            yt_r = Ring([128, D], F32, 2)
            jk_r = Ring([128, 512], F32, 2)
            sm_r = Ring([128, 4], F32, 2)
            for tt in range(NT // 128):
                bi = min(tt // 4, 4)
                bks = [(2 * tt) % 8, (2 * tt + 1) % 8]
                for half in range(2):
                    bk = bks[half]
                    for q in range(4):
                        kc = half * 4 + q
                        mm(pbank[bk][:, q * 128:(q + 1) * 128], xT[:, kc, tt * 128:(tt + 1) * 128], identf, True, True,
                           [xT_b[bi], cs_b], [pb_b[bk]], signal=(q == 3))
                sm_t, sm_b = sm_r.next()
                for half in range(2):
                    jk_t, jk_b = jk_r.next()
                    act(jk_t[:], pbank[bks[half]][:, :], AF.Square, [pb_b[bks[half]]], [jk_b, sm_b], accum_out=sm_t[:, half:half + 1])
                vtt(sm_t[:, 2:3], sm_t[:, 0:1], sm_t[:, 1:2], ALU.add, [sm_b], [sm_b])
                act(sm_t[:, 3:4], sm_t[:, 2:3], AF.Ln, [sm_b, misc_b], [sm_b], bias=epsc[:, 0:1], scale=1.0 / D)
                act(sm_t[:, 3:4], sm_t[:, 3:4], AF.Exp, [sm_b], [sm_b], scale=-0.5)
                y_t, y_b = yt_r.next()
                for half in range(2):
                    vstt(y_t[:, half * 512:(half + 1) * 512], pbank[bks[half]][:, :], sm_t[:, 3:4],
                         wfin_bc[:, half * 512:(half + 1) * 512], ALU.mult, ALU.mult, [pb_b[bks[half]], sm_b, bc_b], [y_b])
                P.dma("sync", y_o[tt * 128:(tt + 1) * 128, :], y_t[:], reads=[y_b])
            phase_end()

        def dump_xT():
            P.dma("sync", dbg_o[:, :], flat(xT[:]), reads=xT_b)

        phases = [("hgrn", hgrn_phase), ("ffn0", lambda: ffn_phase(0)), ("mlstm", mlstm_phase), ("ffn1", lambda: ffn_phase(1))]
        if dbg_phase == "x0":
            dump_xT()
        for name, fn in phases:
            if os.environ.get("MK_SKIP_" + name.upper()):
                continue
            fn()
            if dbg_phase == name:
                dump_xT()
        final_phase()
        P.finish()
    print("n_inst", P.n_inst, {n: len(e.prog) for n, e in P.E.items()})
    return nc


def _relayout_mlstm(w):
    parts = []
    for p in range(4):
        parts += [w[:, p * 128:(p + 1) * 128], w[:, 512 + p * 128:512 + (p + 1) * 128],
                  w[:, 1024 + p * 256:1024 + (p + 1) * 256], w[:, 2048 + p * 256:2048 + (p + 1) * 256]]
    parts.append(w[:, 3072:3088])
    return np.ascontiguousarray(np.concatenate(parts, axis=1))


_NC_CACHE = {}


def kernel(x_prompt, x_sample, state_hgrn_S, state_mlstm_C, state_mlstm_n, state_mlstm_m,
           norm_mixer_w, norm_ffn_w, hgrn_w_in, hgrn_lower_bound_logits, hgrn_out_norm_w, hgrn_w_out,
           mlstm_w_in, mlstm_gate_bias, mlstm_out_norm_w, mlstm_w_out, ffn_w_up, ffn_w_down, final_norm_w,
           _dbg_phase=None, _cores=8):
    f = lambda a: np.ascontiguousarray(np.asarray(a, dtype=np.float32))
    x_prompt, x_sample = f(x_prompt), f(x_sample)
    S, C, n, m = f(state_hgrn_S), f(state_mlstm_C), f(state_mlstm_n), f(state_mlstm_m)
    vecs = np.concatenate([f(norm_mixer_w).reshape(16, 128), f(norm_ffn_w).reshape(16, 128),
                           f(hgrn_lower_bound_logits).reshape(24, 128), f(hgrn_out_norm_w).reshape(8, 128),
                           f(mlstm_out_norm_w).reshape(8, 128)], axis=0)
    shared = {
        "vecs": np.ascontiguousarray(vecs),
        "w_hin": np.ascontiguousarray(f(hgrn_w_in)[0].reshape(D, 4, 8, 128).transpose(0, 2, 1, 3).reshape(D, 4096)),
        "w_hout": f(hgrn_w_out)[0], "w_min": _relayout_mlstm(f(mlstm_w_in)[0]), "w_mout": f(mlstm_w_out)[0],
        "w_up": f(ffn_w_up), "w_dn": f(ffn_w_down),
        "gbias": f(mlstm_gate_bias).reshape(16), "wfin": f(final_norm_w),
        "cst": _make_consts(),
    }
    in_maps = []
    for c in range(_cores):
        d = dict(shared)
        d["xin"] = np.ascontiguousarray(np.concatenate([x_prompt[c], x_sample[16 * c:16 * (c + 1)].reshape(NS, D)], axis=0))
        d["S0"] = np.ascontiguousarray(S[0, 16 * c:16 * (c + 1)])
        d["C0"] = np.ascontiguousarray(C[0, 16 * c:16 * (c + 1)])
        d["n0"] = np.ascontiguousarray(n[0, 16 * c:16 * (c + 1)].reshape(128, 64))
        d["m0"] = np.ascontiguousarray(m[0, 16 * c:16 * (c + 1)])
        in_maps.append(d)
    key = _dbg_phase
    if key not in _NC_CACHE:
        _NC_CACHE[key] = build_program(_dbg_phase)
    nc = _NC_CACHE[key]
    runner = globals().get("_RUNNER") or (lambda nc_, im: run_bass_kernel_spmd(nc_, im, core_ids=list(range(len(im)))))
    res = runner(nc, in_maps).results
    B = _cores
    y_prompt = np.stack([res[c]["y"][:NP_] for c in range(B)], axis=0)
    y_sample = np.concatenate([res[c]["y"][NP_:].reshape(16, 8, D) for c in range(B)], axis=0)
    S_p = np.stack([res[c]["S_p"] for c in range(B)], axis=0)[None]
    C_p = np.stack([res[c]["C_p"] for c in range(B)], axis=0)[None]
    n_p = np.stack([res[c]["n_p"] for c in range(B)], axis=0)[None]
    m_p = np.stack([res[c]["m_p"].reshape(8) for c in range(B)], axis=0)[None]
    S_s = np.concatenate([res[c]["S_s"] for c in range(B)], axis=0)[None]
    C_s = np.concatenate([res[c]["C_s"] for c in range(B)], axis=0)[None]
    n_s = np.concatenate([res[c]["n_s"] for c in range(B)], axis=0)[None]
    m_s = np.concatenate([res[c]["m_s"] for c in range(B)], axis=0)[None]
    outs = (y_prompt, y_sample, S_p, C_p, n_p, m_p, S_s, C_s, n_s, m_s)
    if _dbg_phase is not None:
        return outs, [res[c]["dbg"] for c in range(B)]
    return tuple(np.ascontiguousarray(o, dtype=np.float32) for o in outs)
```

```python
import os
from contextlib import ExitStack
import numpy as np
import concourse.bass as bass
import concourse.mybir as mybir
from concourse.bass_utils import run_bass_kernel_spmd

F32 = mybir.dt.float32
BF16 = mybir.dt.bfloat16
AF = mybir.ActivationFunctionType
ALU = mybir.AluOpType
AX = mybir.AxisListType

D = 1024
NP_ = 2048
NS = 128
NT = NP_ + NS
KC = 8
BLOCKS = [(0, 512), (512, 512), (1024, 512), (1536, 512), (2048, 128)]
EPS = 1e-6
WARM_N = int(os.environ.get("MK_WARM_N", "24"))
CAP = 15.0

C_IDENT = 0
C_ONESD = 128
C_ONES128 = 256
C_MASKH = 384
C_BD128 = 512
C_TRI128 = 640
C_SM64 = 768
C_SM8 = 1280
C_IND16 = 1408
C_ONES1 = 1424
C_MAIN = 1552
C_SELP = 1552
C_TOTAL = 1552 + 512


def _make_consts():
    c = np.zeros((128, C_TOTAL), np.float32)
    s = np.arange(128)[:, None]
    t = np.arange(128)[None, :]
    c[:, C_IDENT:C_IDENT + 128] = np.eye(128, dtype=np.float32)
    c[:, C_ONESD:C_ONESD + 128] = 1.0 / 1024.0
    c[:, C_ONES128:C_ONES128 + 128] = 1.0 / 128.0
    c[:, C_MASKH:C_MASKH + 128] = ((s // 64 == t // 64) & (s <= t)).astype(np.float32)
    c[:, C_BD128:C_BD128 + 128] = ((s // 8 == t // 8) & (s <= t)).astype(np.float32)
    c[:, C_TRI128:C_TRI128 + 128] = (s <= t).astype(np.float32)
    col = np.arange(512)[None, :]
    c[:, C_SM64:C_SM64 + 512] = (col % 64 != 0).astype(np.float32)
    c[:, C_SM8:C_SM8 + 128] = (t % 8 != 0).astype(np.float32)
    c[:, C_IND16:C_IND16 + 16] = (s // 8 == np.arange(16)[None, :]).astype(np.float32)
    c[:, C_ONES1:C_ONES1 + 128] = 1.0
    for p in range(4):
        for m in range(128):
            c[2 * p + m // 64, C_SELP + p * 128 + m] = 1.0
    return c


class Buf:
    __slots__ = ("w", "r")

    def __init__(self):
        self.w = None
        self.r = []


class Eng:
    def __init__(self, name, sem):
        self.name = name
        self.sem = sem
        self.count = 0
        self.waited = {}
        self.prog = []


class Prog:
    def __init__(self, nc, stack):
        self.nc = nc
        self.stack = stack
        self.E = {}
        for n in ("tensor", "vector", "scalar", "gpsimd", "sync"):
            self.E[n] = Eng(n, stack.enter_context(nc.semaphore("s_" + n)))
        self.pools = {}
        self.pool_idx = {}
        for ename, k in (("sync", 20), ("gpsimd", 12)):
            self.pools[ename] = [[stack.enter_context(nc.semaphore("dq_%s_%d" % (ename, i))), 0] for i in range(k)]
            self.pool_idx[ename] = 0
        self.n_inst = 0

    def _wait(self, e, tok):
        sem, val = tok
        if sem is e.sem and (e.name == "tensor" or val > e.count):
            return
        key = id(sem)
        if e.waited.get(key, 0) < val:
            e.waited[key] = val
            e.prog.append(lambda eng, sem=sem, val=val: eng.wait_ge(sem, val))

    @staticmethod
    def _flat(bufs):
        out = []
        for b in bufs:
            if isinstance(b, (list, tuple)):
                out.extend(Prog._flat(b))
            else:
                out.append(b)
        return out

    def _deps(self, e, reads, writes):
        for b in reads:
            if b.w is not None:
                self._wait(e, b.w)
        for b in writes:
            if b.w is not None:
                self._wait(e, b.w)
            for t in b.r:
                self._wait(e, t)

    def emit(self, ename, fn, reads=(), writes=(), signal=True):
        e = self.E[ename]
        reads, writes = self._flat(reads), self._flat(writes)
        self._deps(e, reads, writes)
        self.n_inst += 1
        tok = (e.sem, e.count + 1)
        if signal:
            e.count += 1
            e.prog.append(lambda eng, sem=e.sem: fn(eng).then_inc(sem, 1))
        else:
            e.prog.append(lambda eng: fn(eng))
        for b in reads:
            b.r.append(tok)
        for b in writes:
            b.w = tok
            b.r = []
        return tok

    def dma(self, ename, out, in_, reads=(), writes=(), **kw):
        e = self.E[ename]
        pool = self.pools[ename]
        slot = pool[self.pool_idx[ename] % len(pool)]
        self.pool_idx[ename] += 1
        if slot[1] > 0:
            self._wait(e, (slot[0], slot[1]))
        reads, writes = self._flat(reads), self._flat(writes)
        self._deps(e, reads, writes)
        slot[1] += 16
        tok = (slot[0], slot[1])
        self.n_inst += 1
        e.prog.append(lambda eng, sem=slot[0]: eng.dma_start(out=out, in_=in_, **kw).then_inc(sem, 16))
        for b in reads:
            b.r.append(tok)
        for b in writes:
            b.w = tok
            b.r = []
        return tok

    def barrier(self):
        toks = [(e.sem, e.count) for e in self.E.values() if e.count > 0]
        for pool in self.pools.values():
            for s in pool:
                if s[1] > 0:
                    toks.append((s[0], s[1]))
        for e in self.E.values():
            for t in toks:
                self._wait(e, t)

    def finish(self):
        self.barrier()
        nc = self.nc
        with nc.Block() as block:
            for n, deco in (("sync", block.sync), ("gpsimd", block.gpsimd), ("tensor", block.tensor),
                            ("vector", block.vector), ("scalar", block.scalar)):
                prog = self.E[n].prog

                def body(eng, prog=prog):
                    for f in prog:
                        f(eng)
                deco(body)


def build_program(dbg_phase=None):
    nc = bass.Bass("TRN2", target_bir_lowering=False)

    def din(name, shape):
        return nc.dram_tensor(name, list(shape), F32, kind="ExternalInput").ap()

    def dout(name, shape):
        return nc.dram_tensor(name, list(shape), F32, kind="ExternalOutput").ap()

    xin = din("xin", [NT, D])
    S0d = din("S0", [16, 8, 128, 128])
    C0d = din("C0", [16, 8, 64, 128])
    n0d = din("n0", [128, 64])
    m0d = din("m0", [16, 8])
    vecs = din("vecs", [72, 128])
    w_hin = din("w_hin", [D, 4096])
    w_hout = din("w_hout", [D, D])
    w_min = din("w_min", [D, 3088])
    w_mout = din("w_mout", [D, D])
    w_up = din("w_up", [2, D, 4096])
    w_dn = din("w_dn", [2, 4096, D])
    gbias = din("gbias", [16])
    wfin = din("wfin", [D])
    cst = din("cst", [128, C_TOTAL])

    y_o = dout("y", [NT, D])
    Sp_o = dout("S_p", [8, 128, 128])
    Cp_o = dout("C_p", [8, 64, 128])
    np_o = dout("n_p", [8, 64])
    mp_o = dout("m_p", [8, 1])
    Ss_o = dout("S_s", [16, 8, 128, 128])
    Cs_o = dout("C_s", [16, 8, 64, 128])
    ns_o = dout("n_s", [16, 8, 64])
    ms_o = dout("m_s", [16, 8])
    dbg_o = dout("dbg", [128, KC * NT]) if dbg_phase is not None else None

    with ExitStack() as st:
        def sb(name, shape, dtype=F32):
            return st.enter_context(nc.sbuf_tensor(name, list(shape), dtype))

        P = Prog(nc, st)

        xT = sb("xT", [128, KC, NT])
        xnT = sb("xnT", [128, KC, NT], BF16)
        wsl = [sb("wsl%d" % i, [128, KC, 512], BF16) for i in range(4)]
        wsl_b = [Buf() for _ in range(4)]
        cs = sb("cs", [128, C_MAIN])
        cs_b = Buf()
        identb = sb("identb", [128, 128], BF16)
        identb_b = Buf()
        onesDb = sb("onesDb", [128, 128], BF16)
        ones128b = sb("ones128b", [128, 128], BF16)
        onesDb_b = Buf()
        vT = sb("vT", [128, 72])
        vT_b = Buf()
        lbt = sb("lbt", [128, 16])
        lbt_b = Buf()
        epsc = sb("epsc", [128, 1])
        onec = sb("onec", [128, 1])
        misc_b = Buf()
        WORK_F32 = (nc.sbuf_bytes_remaining - 512) // 4 // 8 * 8
        work = sb("work", [128, WORK_F32])
        wptr = [0]

        def walloc(shape, dtype=F32):
            n = 1
            for d_ in shape[1:]:
                n *= d_
            nf32 = (n * (4 if dtype == F32 else 2) + 3) // 4
            nf32 = (nf32 + 7) // 8 * 8
            off = wptr[0]
            wptr[0] += nf32
            assert wptr[0] <= WORK_F32, ("work overflow", wptr[0], WORK_F32)
            ap = work[0:shape[0], off:off + nf32]
            if dtype != F32:
                ap = ap.bitcast(dtype)
            ap = ap[:, 0:n]
            if len(shape) == 3:
                ap = ap.rearrange("p (a b) -> p a b", a=shape[1])
            return ap

        class Ring:
            def __init__(self, shape, dtype, n):
                self.items = [(walloc(shape, dtype), Buf()) for _ in range(n)]
                self.i = 0

            def next(self):
                it = self.items[self.i % len(self.items)]
                self.i += 1
                return it

        def phase_end(mark=0):
            P.barrier()
            wptr[0] = mark

        xT_b = [Buf() for _ in BLOCKS]
        xnT_b = [Buf() for _ in BLOCKS]

        pbank = [nc.alloc_psum_tensor("pb%d" % i, [128, 512], F32) for i in range(8)]
        pb_b = [Buf() for _ in range(8)]

        identf = cs[:, C_IDENT:C_IDENT + 128]
        onesD = cs[:, C_ONESD:C_ONESD + 128]
        ones128 = cs[:, C_ONES128:C_ONES128 + 128]

        def mm(out, lhsT, rhs, start, stop, reads, writes, signal=True, sgc=False):
            return P.emit("tensor", lambda e: e.matmul(out, lhsT=lhsT, rhs=rhs, start=start, stop=stop,
                                                       skip_group_check=sgc), reads, writes, signal)

        def act(out, in_, func, reads, writes, bias=None, scale=1.0, accum_out=None):
            kw = {}
            if bias is not None:
                kw["bias"] = bias
            if accum_out is not None:
                kw["accum_out"] = accum_out
            return P.emit("scalar", lambda e: e.activation(out=out, in_=in_, func=func, scale=scale, **kw),
                          reads, writes)

        def amul(out, in_, mul, reads, writes):
            return P.emit("scalar", lambda e: e.mul(out=out, in_=in_, mul=mul), reads, writes)

        def acopy(out, in_, reads, writes):
            return P.emit("scalar", lambda e: e.copy(out=out, in_=in_), reads, writes)

        def vtt(out, in0, in1, op, reads, writes, eng="vector"):
            return P.emit(eng, lambda e: e.tensor_tensor(out=out, in0=in0, in1=in1, op=op), reads, writes)

        def vts(out, in0, s1, s2, op0, op1, reads, writes, eng="vector"):
            if s2 is None:
                return P.emit(eng, lambda e: e.tensor_scalar(out=out, in0=in0, scalar1=s1, scalar2=None, op0=op0),
                              reads, writes)
            return P.emit(eng, lambda e: e.tensor_scalar(out=out, in0=in0, scalar1=s1, scalar2=s2, op0=op0, op1=op1),
                          reads, writes)

        def vstt(out, in0, scalar, in1, op0, op1, reads, writes, eng="vector"):
            return P.emit(eng, lambda e: e.scalar_tensor_tensor(out=out, in0=in0, scalar=scalar, in1=in1,
                                                               op0=op0, op1=op1), reads, writes)

        def vcopy(out, in_, reads, writes, eng="vector"):
            return P.emit(eng, lambda e: e.tensor_copy(out=out, in_=in_), reads, writes)

        def vrecip(out, in_, reads, writes):
            return P.emit("vector", lambda e: e.reciprocal(out=out, in_=in_), reads, writes)

        def vmemset(ap, val, writes, eng="vector"):
            return P.emit(eng, lambda e: e.memset(ap, val), (), writes)

        def vscan(out, d0, d1, init, op0, op1, reads, writes):
            return P.emit("vector", lambda e: e.tensor_tensor_scan(out=out, data0=d0, data1=d1, initial=init,
                                                                   op0=op0, op1=op1), reads, writes)

        def vreduce(out, in_, op, reads, writes):
            return P.emit("vector", lambda e: e.tensor_reduce(out=out, in_=in_, axis=AX.X, op=op), reads, writes)

        def vabs(out, in_, reads, writes):
            return P.emit("vector", lambda e: e.tensor_single_scalar(out=out, in_=in_, scalar=0.0, op=ALU.abs_max),
                          reads, writes)

        def bc_mid(ap2, n):
            return ap2.unsqueeze(2).to_broadcast([ap2.shape[0], ap2.shape[1], n])

        def bc_first(ap2, n):
            return ap2.unsqueeze(1).to_broadcast([ap2.shape[0], n, ap2.shape[1]])

        def flat(ap3):
            return ap3.rearrange("p a b -> p (a b)")

        def keep_warm(n, bank=7):
            for _ in range(n):
                mm(pbank[bank][:, :], identb[:], xnT[:, 0, 0:512], True, True, [identb_b], [pb_b[bank]], signal=False)

        P.dma("sync", cs[:], cst[:, 0:C_MAIN], writes=[cs_b])
        P.dma("gpsimd", identb[:], cst[:, C_IDENT:C_IDENT + 128], writes=[identb_b])
        P.dma("gpsimd", onesDb[:], cst[:, C_ONESD:C_ONESD + 128], writes=[onesDb_b])
        P.dma("gpsimd", ones128b[:], cst[:, C_ONES128:C_ONES128 + 128], writes=[onesDb_b])
        vmemset(epsc[:], EPS, [misc_b])
        vmemset(onec[:], 1.0, [misc_b])
        vraw = walloc([72, 128])
        vraw_b = Buf()
        lt = walloc([128, 40])
        lt_b = Buf()
        P.dma("sync", vraw[:], vecs[:, :], writes=[vraw_b])
        mm(pbank[0][:, 0:72], vraw[:, :], cs[0:72, C_IDENT:C_IDENT + 72], True, True, [vraw_b, cs_b], [pb_b[0]])
        vcopy(vT[:], pbank[0][:, 0:72], [pb_b[0]], [vT_b])
        l0, l1, l2 = vT[:, 32:40], vT[:, 40:48], vT[:, 48:56]
        vtt(lt[:, 0:8], l0, l1, ALU.max, [vT_b], [lt_b])
        vtt(lt[:, 0:8], lt[:, 0:8], l2, ALU.max, [vT_b, lt_b], [lt_b])
        for i, l in enumerate((l0, l1, l2)):
            vtt(lt[:, 8 + 8 * i:16 + 8 * i], l, lt[:, 0:8], ALU.subtract, [vT_b, lt_b], [lt_b])
        act(lt[:, 8:32], lt[:, 8:32], AF.Exp, [lt_b], [lt_b])
        vtt(lt[:, 32:40], lt[:, 8:16], lt[:, 16:24], ALU.add, [lt_b], [lt_b])
        vtt(lt[:, 32:40], lt[:, 32:40], lt[:, 24:32], ALU.add, [lt_b], [lt_b])
        vrecip(lt[:, 32:40], lt[:, 32:40], [lt_b], [lt_b])
        vtt(lbt[:, 0:8], lt[:, 8:16], lt[:, 32:40], ALU.mult, [lt_b], [lbt_b])
        vts(lbt[:, 8:16], lbt[:, 0:8], -1.0, 1.0, ALU.mult, ALU.add, [lbt_b], [lbt_b])

        xtok = Ring([128, D], F32, 2)
        for tt in range(NT // 128):
            bi = min(tt // 4, 4)
            xt, xt_b = xtok.next()
            P.dma("sync", xt[:], xin[tt * 128:(tt + 1) * 128, :], writes=[xt_b])
            for half in range(2):
                bk = (2 * tt + half) % 8
                for q in range(4):
                    kc = half * 4 + q
                    mm(pbank[bk][:, q * 128:(q + 1) * 128], xt[:, kc * 128:(kc + 1) * 128], identf, True, True,
                       [xt_b, cs_b], [pb_b[bk]], signal=(q == 3))
                dst = xT[:, half * 4:half * 4 + 4, tt * 128:(tt + 1) * 128]
                src = pbank[bk][:, :].rearrange("p (q t) -> p q t", q=4)
                if half == 0:
                    vcopy(dst, src, [pb_b[bk]], [xT_b[bi]])
                else:
                    acopy(dst, src, [pb_b[bk]], [xT_b[bi]])
        phase_end()

        slot_rr = [0]

        def load_rows(dram2d, row0, nrow, col_ranges, i):
            nk = nrow // 128
            src = dram2d[row0:row0 + nrow, :].rearrange("(kc p) n -> p kc n", p=128)
            ncols = sum(c1 - c0 for c0, c1 in col_ranges)
            assert nk * ncols <= KC * 512
            view = flat(wsl[i][:])[:, 0:nk * ncols].rearrange("p (a b) -> p a b", a=nk)
            off = 0
            for c0, c1 in col_ranges:
                P.dma("gpsimd", view[:, :, off:off + (c1 - c0)], src[:, :, c0:c1], writes=[wsl_b[i]])
                off += c1 - c0
            return view, wsl_b[i]

        def rmsnorm_to_xn(wcol0):
            mark = wptr[0]
            sq = Ring([128, 512], BF16, 4)
            rs = Ring([128, 512], F32, 2)
            for bi, (c0, nb) in enumerate(BLOCKS):
                bk = bi % 2
                for kc in range(KC):
                    s_t, s_b = sq.next()
                    act(s_t[:, :nb], xT[:, kc, c0:c0 + nb], AF.Square, [xT_b[bi]], [s_b])
                    mm(pbank[bk][:, :nb], onesDb[:], s_t[:, :nb], kc == 0, kc == KC - 1, [s_b, onesDb_b], [pb_b[bk]],
                       signal=True)
                r_t, r_b = rs.next()
                act(r_t[:, :nb], pbank[bk][:, :nb], AF.Ln, [pb_b[bk], misc_b], [r_b], bias=epsc[:, 0:1])
                act(r_t[:, :nb], r_t[:, :nb], AF.Exp, [r_b], [r_b], scale=-0.5)
                for kc in range(KC):
                    vstt(xnT[:, kc, c0:c0 + nb], xT[:, kc, c0:c0 + nb], vT[:, wcol0 + kc:wcol0 + kc + 1], r_t[:, :nb],
                         ALU.mult, ALU.mult, [xT_b[bi], vT_b, r_b], [xnT_b[bi]])
            phase_end(mark)

        opk = [0]

        def out_proj(wdram, row0, nk, oT, oT_b):
            wv, wb = load_rows(wdram, row0, 128 * nk, [(0, D)], 2)
            for bi, (c0, nb) in enumerate(BLOCKS):
                for m in range(8):
                    bk = 6 + opk[0] % 2
                    opk[0] += 1
                    for kc in range(nk):
                        mm(pbank[bk][:, :nb], wv[:, kc, m * 128:(m + 1) * 128], oT[:, kc, c0:c0 + nb],
                           kc == 0, kc == nk - 1, [wb, oT_b[kc][bi]], [pb_b[bk]], signal=(kc == nk - 1))
                    vtt(xT[:, m, c0:c0 + nb], xT[:, m, c0:c0 + nb], pbank[bk][:, :nb], ALU.add,
                        [pb_b[bk], xT_b[bi]], [xT_b[bi]])

        def run_interleaved(gens):
            alive = [g for g in gens if g is not None]
            while alive:
                for g in list(alive):
                    try:
                        next(g)
                    except StopIteration:
                        alive.remove(g)

        def ffn_phase(layer):
            loads = {}

            def load_slice(s):
                up = load_rows(w_up[layer], 0, D, [(s * 512, (s + 1) * 512)], 2 * (s % 2))
                dn = load_rows(w_dn[layer], s * 512, 512, [(0, D)], 2 * (s % 2) + 1)
                loads[s] = (up, dn)

            load_slice(0)
            rmsnorm_to_xn(16 + 8 * layer)
            relu_r = Ring([128, 512], F32, 3)
            hT = walloc([128, 8, NT], BF16).rearrange("p (s m) t -> p s m t", s=2)
            hT_b = [[Buf() for _ in BLOCKS] for _ in range(2)]
            kup = 0
            kdn = 0
            for s in range(8):
                if s + 1 < 8:
                    load_slice(s + 1)
                (uv, ub), (dv, db) = loads.pop(s)
                par = s % 2
                for bi, (c0, nb) in enumerate(BLOCKS):
                    for m in range(4):
                        bk = kup % 4
                        kup += 1
                        for kc in range(KC):
                            mm(pbank[bk][:, :nb], uv[:, kc, m * 128:(m + 1) * 128], xnT[:, kc, c0:c0 + nb],
                               kc == 0, kc == KC - 1, [ub, xnT_b[bi]], [pb_b[bk]], signal=(kc == KC - 1))
                        r_t, r_b = relu_r.next()
                        act(r_t[:, :nb], pbank[bk][:, :nb], AF.Relu, [pb_b[bk]], [r_b])
                        act(hT[:, par, m, c0:c0 + nb], r_t[:, :nb], AF.Square, [r_b], [hT_b[par][bi]])
                for bi, (c0, nb) in enumerate(BLOCKS):
                    for m in range(8):
                        bk = 4 + kdn % 4
                        kdn += 1
                        for kc in range(4):
                            mm(pbank[bk][:, :nb], dv[:, kc, m * 128:(m + 1) * 128], hT[:, par, kc, c0:c0 + nb],
                               kc == 0, kc == 3, [db, hT_b[par][bi]], [pb_b[bk]], signal=(kc == 3))
                        vtt(xT[:, m, c0:c0 + nb], xT[:, m, c0:c0 + nb], pbank[bk][:, :nb], ALU.add,
                            [pb_b[bk], xT_b[bi]], [xT_b[bi]])
            phase_end()

        def hgrn_phase():
            w_head0 = load_rows(w_hin, 0, D, [(0, 512)], 0)
            rmsnorm_to_xn(0)
            oT = walloc([128, 4, NT], BF16)
            oT_b = [[Buf() for _ in BLOCKS] for _ in range(4)]
            R = Ring
            t_f = R([128, 512], F32, 1)
            t_kk = R([128, 512], F32, 1)
            t_b = R([128, 512], F32, 1)
            t_e1 = R([128, 512], F32, 1)
            t_sq = R([128, 512], F32, 1)
            t_sg = R([128, 512], F32, 2)
            t_qt = R([128, 512], BF16, 2)
            t_kt = R([128, 512], BF16, 1)
            t_v = R([128, 4, 128], BF16, 1)
            t_ktk = R([128, 4, 128], BF16, 1)
            t_at = R([128, 4, 128], BF16, 1)
            t_sc = R([128, 64], F32, 2)
            t_S = R([128, 128], F32, 5)
            t_Sp = R([128, 128], BF16, 2)
            t_T = R([128, 8, 128], F32, 1)
            S0r = R([128, 16, 128], F32, 1)
            Sp16 = R([128, 16, 128], BF16, 1)
            Vblk = R([128, 16, 128], BF16, 1)
            Tq = R([128, 4, 128], F32, 1)
            p_osq = R([128, 256], BF16, 1)
            p_rs = R([128, 256], F32, 1)
            p_on = R([128, 256], F32, 1)
            hs = {}
            bq, bf_, bg, bv = 0, 1, 2, 3
            half_bufs = {}

            def halves(bf):
                if id(bf) not in half_bufs:
                    half_bufs[id(bf)] = (bf, Buf())
                return half_bufs[id(bf)]

            class HB(list):
                pass

            sce_bufs = {}

            def sce_halves(bf):
                if id(bf) not in sce_bufs:
                    sce_bufs[id(bf)] = (Buf(), Buf())
                return sce_bufs[id(bf)]

            def chain_post(h, bi, d):
                yield from st_chain(h, bi, d)
                yield from st_post(h, bi, d)

            def head_cols(h):
                return [(h * 512, (h + 1) * 512)]

            wslots = {0: w_head0}

            def geom(bi):
                c0, nb = BLOCKS[bi]
                is_s = bi == 4
                L = 8 if is_s else 64
                return c0, nb, is_s, nb // 128, L, nb // L

            def st_proj(h, bi):
                c0, nb, is_s, ntile, L, nch = geom(bi)
                if bi == 0 and h + 1 < 8:
                    wslots[h + 1] = load_rows(w_hin, 0, D, head_cols(h + 1), (h + 1) % 2)
                wv, wb = wslots[h]
                for comp, bk in ((1, bf_), (0, bq), (3, bg)):
                    for kc in range(KC):
                        mm(pbank[bk][:, :nb], wv[:, kc, comp * 128:(comp + 1) * 128], xnT[:, kc, c0:c0 + nb],
                           kc == 0, kc == KC - 1, [wb, xnT_b[bi]], [pb_b[bk]], signal=(kc == KC - 1))
                for tt in range(ntile):
                    for kc in range(KC):
                        mm(pbank[bv][:, tt * 128:(tt + 1) * 128], xnT[:, kc, c0 + tt * 128:c0 + (tt + 1) * 128],
                           wv[:, kc, 256:384], kc == 0, kc == KC - 1, [wb, xnT_b[bi]], [pb_b[bv]],
                           signal=(kc == KC - 1 and tt == ntile - 1))

            def st_ew(h, bi, out):
                c0, nb, is_s, ntile, L, nch = geom(bi)
                f_t, f_b = t_f.next()
                kk_t, kk_b = t_kk.next()
                b_t, b_b = t_b.next()
                e1_t, e1_b = t_e1.next()
                sq_t, sq_b = t_sq.next()
                sg_t, sg_b = t_sg.next()
                qt_t, qt_b = t_qt.next()
                kt_t, kt_b = t_kt.next()
                v_t, v_b = t_v.next()
                sc_t, sc_b = t_sc.next()
                f_b, kk_b, b_b, e1_b, sq_b, sg_b, qt_b, kt_b, sc_b = (halves(x) for x in (f_b, kk_b, b_b, e1_b, sq_b, sg_b, qt_b, kt_b, sc_b))
                sce_b = sce_halves(sc_b[0])

                def ew_half(hf, lo, hi, ch0, nchh):
                    cl = slice(lo, hi)
                    fb, kkb, bb, e1b, sqb, sgb, qtb, ktb, scb = (x[hf] for x in (f_b, kk_b, b_b, e1_b, sq_b, sg_b, qt_b, kt_b, sc_b))
                    sceb = sce_b[hf]

                    def sig3(dst, dst_b, src_ps, src_b):
                        act(dst[:, cl], src_ps[:, cl], AF.Exp, [src_b], [dst_b], scale=-1.0)
                        yield
                        act(dst[:, cl], dst[:, cl], AF.Ln, [dst_b, misc_b], [dst_b], bias=onec[:, 0:1])
                        yield
                        act(dst[:, cl], dst[:, cl], AF.Exp, [dst_b], [dst_b], scale=-1.0)
                        yield

                    yield from sig3(f_t, fb, pbank[bf_], pb_b[bf_])
                    vts(f_t[:, cl], f_t[:, cl], lbt[:, 8 + h:9 + h], lbt[:, h:h + 1], ALU.mult, ALU.add, [fb, lbt_b], [fb])
                    yield
                    vts(kk_t[:, cl], f_t[:, cl], -1.0, 1.0, ALU.mult, ALU.add, [fb], [kkb])
                    yield
                    act(f_t[:, cl], f_t[:, cl], AF.Ln, [fb], [fb])
                    yield
                    yield from sig3(sq_t, sqb, pbank[bq], pb_b[bq])
                    smc = cs[:, C_SM8:C_SM8 + (hi - lo)] if is_s else cs[:, C_SM64:C_SM64 + (hi - lo)]
                    vscan(b_t[:, cl], smc, f_t[:, cl], 0.0, ALU.mult, ALU.add, [fb, cs_b], [bb])
                    yield
                    bview = b_t[:, cl].rearrange("p (c l) -> p c l", l=L)
                    bL = bview[:, :, L - 1]
                    vcopy(sc_t[:, ch0:ch0 + nchh], bL, [bb], [scb])
                    yield
                    vtt(sq_t[:, cl], sq_t[:, cl], pbank[bq][:, cl], ALU.mult, [sqb, pb_b[bq]], [sqb])
                    yield
                    act(sc_t[:, 16 + ch0:16 + ch0 + nchh], bL, AF.Exp, [bb], [sceb])
                    yield
                    vtt(bview, bview, bc_mid(sc_t[:, ch0:ch0 + nchh], L), ALU.subtract, [bb, scb], [bb])
                    yield
                    act(e1_t[:, cl], b_t[:, cl], AF.Exp, [bb], [e1b])
                    yield
                    act(b_t[:, cl], b_t[:, cl], AF.Exp, [bb], [bb], scale=-1.0)
                    yield
                    vtt(qt_t[:, cl], sq_t[:, cl], e1_t[:, cl], ALU.mult, [sqb, e1b], [qtb])
                    yield
                    vtt(kt_t[:, cl], kk_t[:, cl], b_t[:, cl], ALU.mult, [kkb, bb], [ktb])
                    yield
                    yield from sig3(sg_t, sgb, pbank[bg], pb_b[bg])
                    vtt(sg_t[:, cl], sg_t[:, cl], pbank[bg][:, cl], ALU.mult, [sgb, pb_b[bg]], [sgb])
                    yield

                if is_s:
                    ga, gb_ = ew_half(0, 0, 64, 0, 8), ew_half(1, 64, 128, 8, 8)
                else:
                    ga, gb_ = ew_half(0, 0, 256, 0, 4), ew_half(1, 256, 512, 4, 4)
                alive = [ga, gb_]
                while alive:
                    for g in list(alive):
                        try:
                            next(g)
                            yield
                        except StopIteration:
                            alive.remove(g)
                acopy(v_t[:, 0:ntile, :], pbank[bv][:, :nb].rearrange("p (a b) -> p a b", b=128), [pb_b[bv]], [v_b])
                yield
                out["d"] = dict(qt=(qt_t, HB(qt_b)), kt=(kt_t, HB(kt_b)), v=(v_t, v_b), sc=(sc_t, HB(list(sc_b) + list(sce_b))), sg=(sg_t, HB(sg_b)))

            def st_mid(h, bi, d):
                c0, nb, is_s, ntile, L, nch = geom(bi)
                qt_t, qt_b = d["qt"]
                kt_t, kt_b = d["kt"]
                v_t, v_b = d["v"]
                sc_t, sc_b = d["sc"]
                ktk_t, ktk_b = t_ktk.next()
                at_t, at_b = t_at.next()
                for tt in range(ntile):
                    mm(pbank[0][:, tt * 128:(tt + 1) * 128], kt_t[:, tt * 128:(tt + 1) * 128],
                       qt_t[:, tt * 128:(tt + 1) * 128], True, True, [kt_b, qt_b], [pb_b[0]],
                       signal=(tt == ntile - 1))
                for tt in range(ntile):
                    mm(pbank[1][:, tt * 128:(tt + 1) * 128], kt_t[:, tt * 128:(tt + 1) * 128], identb[:],
                       True, True, [kt_b, identb_b], [pb_b[1]], signal=(tt == ntile - 1))
                mask = cs[:, C_BD128:C_BD128 + 128] if is_s else cs[:, C_MASKH:C_MASKH + 128]
                vtt(at_t[:, 0:ntile, :], pbank[0][:, :nb].rearrange("p (a b) -> p a b", b=128),
                    bc_first(mask, ntile), ALU.mult, [pb_b[0], cs_b], [at_b])
                acopy(ktk_t[:, 0:ntile, :], pbank[1][:, :nb].rearrange("p (a b) -> p a b", b=128), [pb_b[1]], [ktk_b])
                for tt in range(ntile):
                    mm(pbank[4][:, tt * 128:(tt + 1) * 128], v_t[:, tt, :], at_t[:, tt, :], tt == 0, False,
                       [v_b, at_b], [pb_b[4]], signal=False, sgc=True)
                d["ktk"] = (ktk_t, ktk_b)
                if not is_s:
                    for ci in range(nch):
                        tt, hf = ci // 2, ci % 2
                        r0 = hf * 64
                        bk = 5 + hf
                        mm(pbank[bk][:, tt * 128:(tt + 1) * 128], ktk_t[r0:r0 + 64, tt, :], v_t[r0:r0 + 64, tt, :],
                           True, True, [ktk_b, v_b], [pb_b[bk]], signal=(ci >= nch - 2))

            def st_chain(h, bi, d):
                c0, nb, is_s, ntile, L, nch = geom(bi)
                qt_t, qt_b = d["qt"]
                v_t, v_b = d["v"]
                sc_t, sc_b = d["sc"]
                ktk_t, ktk_b = d["ktk"]
                if bi == 0:
                    S0t, S0b = S0r.next()
                    P.dma("sync", S0t[:], S0d[:, h, :, :].rearrange("j d v -> d j v"), writes=[S0b])
                    S_t, S_b = t_S.next()
                    vmemset(S_t[:], 0.0, [S_b])
                    hs["S"] = (S_t, S_b)
                    hs["S0"] = (S0t, S0b)
                S_t, S_b = hs["S"]
                yield
                if not is_s:
                    for ci in range(nch):
                        Sn_t, Sn_b = t_S.next()
                        vstt(Sn_t[:], S_t[:], sc_t[:, 16 + ci:17 + ci],
                             pbank[5 + ci % 2][:, (ci // 2) * 128:(ci // 2 + 1) * 128], ALU.mult, ALU.add,
                             [S_b, sc_b, pb_b[5 + ci % 2]], [Sn_b])
                        Sp_t, Sp_b = t_Sp.next()
                        amul(Sp_t[:], S_t[:], sc_t[:, 16 + ci:17 + ci], [S_b, sc_b], [Sp_b])
                        mm(pbank[4][:, ci * 64:(ci + 1) * 64], Sp_t[:], qt_t[:, ci * 64:(ci + 1) * 64], False, True,
                           [Sp_b, qt_b], [pb_b[4]], signal=True, sgc=True)
                        S_t, S_b = Sn_t, Sn_b
                        yield
                    hs["S"] = (S_t, S_b)
                else:
                    S0t, S0b = hs["S0"]
                    P.dma("sync", Sp_o[h, :, :], S_t[:], reads=[S_b])
                    sp16, sp16_b = Sp16.next()
                    vtt(sp16[:], S0t[:], bc_mid(sc_t[:, 16:32], 128), ALU.mult, [S0b, sc_b], [sp16_b])
                    for j in range(16):
                        mm(pbank[4][:, j * 8:(j + 1) * 8], sp16[:, j, :], qt_t[:, j * 8:(j + 1) * 8], False, True,
                           [sp16_b, qt_b], [pb_b[4]], signal=(j == 15), sgc=True)
                    vb_t, vb_b = Vblk.next()
                    vtt(vb_t[:], bc_first(v_t[:, 0, :], 16), bc_mid(cs[:, C_IND16:C_IND16 + 16], 128), ALU.mult,
                        [v_b, cs_b], [vb_b])
                    vtt(S0t[:], S0t[:], bc_mid(sc_t[:, 16:32], 128), ALU.mult, [S0b, sc_b], [S0b])
                    for q in range(4):
                        bk = 6 + q % 2
                        mm(pbank[bk][:, :], ktk_t[:, 0, :], flat(vb_t[:, 4 * q:4 * q + 4, :]),
                           True, True, [ktk_b, vb_b], [pb_b[bk]])
                        vtt(S0t[:, 4 * q:4 * q + 4, :], S0t[:, 4 * q:4 * q + 4, :],
                            pbank[bk][:, :].rearrange("p (a b) -> p a b", b=128), ALU.add, [S0b, pb_b[bk]], [S0b])
                        yield
                    P.dma("sync", Ss_o[:, h, :, :].rearrange("j d v -> d j v"), S0t[:], reads=[S0b])

            def st_post(h, bi, d):
                c0, nb, is_s, ntile, L, nch = geom(bi)
                hl = h % 4
                sg_t, sg_b = d["sg"]
                step = 256 if nb == 512 else nb
                for lo in range(0, nb, step):
                    cl = slice(lo, lo + step)
                    osq_t, osq_b = p_osq.next()
                    rs_t, rs_b = p_rs.next()
                    on_t, on_b = p_on.next()
                    act(osq_t[:, 0:step], pbank[4][:, cl], AF.Square, [pb_b[4]], [osq_b])
                    yield
                    mm(pbank[7][:, cl], ones128b[:], osq_t[:, 0:step], True, True, [osq_b, onesDb_b], [pb_b[7]])
                    yield
                    act(rs_t[:, 0:step], pbank[7][:, cl], AF.Ln, [pb_b[7], misc_b], [rs_b], bias=epsc[:, 0:1])
                    yield
                    act(rs_t[:, 0:step], rs_t[:, 0:step], AF.Exp, [rs_b], [rs_b], scale=-0.5)
                    yield
                    vtt(on_t[:, 0:step], pbank[4][:, cl], rs_t[:, 0:step], ALU.mult, [pb_b[4], rs_b], [on_b])
                    yield
                    vstt(oT[:, hl, c0 + lo:c0 + lo + step], on_t[:, 0:step], vT[:, 56 + h:57 + h], sg_t[:, cl], ALU.mult, ALU.mult,
                         [on_b, vT_b, sg_b], [oT_b[hl][bi]])
                    yield

            items = [(h, bi) for h in range(8) for bi in range(len(BLOCKS))]
            hold = {}
            st_proj(*items[0])
            run_interleaved([st_ew(items[0][0], items[0][1], hold)])
            cur = hold["d"]
            st_mid(items[0][0], items[0][1], cur)
            for i, (h, bi) in enumerate(items):
                nx = items[i + 1] if i + 1 < len(items) else None
                if nx:
                    st_proj(*nx)
                run_interleaved([st_ew(nx[0], nx[1], hold) if nx else None, chain_post(h, bi, cur)])
                if bi == len(BLOCKS) - 1 and h % 4 == 3:
                    out_proj(w_hout, (h // 4) * 512, 4, oT, oT_b)
                if nx:
                    cur = hold["d"]
                    st_mid(nx[0], nx[1], cur)
            phase_end()

        def mlstm_phase():
            pair0 = (load_rows(w_min, 0, D, [(0, 512)], 0), load_rows(w_min, 0, D, [(512, 768)], 1))
            rmsnorm_to_xn(8)
            R = Ring
            oT = walloc([128, 4, NT], BF16)
            oT_b = [[Buf() for _ in BLOCKS] for _ in range(4)]
            rsel = walloc([8, 65])
            rsel_b = Buf()
            tokq = walloc([128, 17, 24])
            tokq_b = Buf()
            nT = walloc([128, 16, 8])
            nT_b = Buf()
            selp = walloc([8, 512])
            selp_b = Buf()
            P.dma("sync", selp[:], cst[0:8, C_SELP:C_SELP + 512], writes=[selp_b])
            mark = wptr[0]
            wg = walloc([128, KC, 16], BF16)
            wg_b = Buf()
            P.dma("gpsimd", wg[:], w_min[:, 3072:3088].rearrange("(kc p) n -> p kc n", p=128), writes=[wg_b])
            gb = walloc([8, 4])
            gb_b = Buf()
            P.dma("sync", gb[:, 0:1], gbias[0:8].rearrange("(h o) -> h o", o=1), writes=[gb_b])
            P.dma("sync", gb[:, 1:2], gbias[8:16].rearrange("(h o) -> h o", o=1), writes=[gb_b])
            vts(gb[:, 2:4], gb[:, 0:2], 1.0 / CAP, None, ALU.mult, None, [gb_b], [gb_b])
            m0T = walloc([8, 16])
            m0T_b = Buf()
            P.dma("sync", m0T[:], m0d.rearrange("j h -> h j"), writes=[m0T_b], allow_slow_non_contiguous=True)
            stat = walloc([8, 96])
            stat_b = Buf()
            g1 = R([8, 512], F32, 2)
            g2 = R([8, 512], F32, 2)
            g3 = R([8, 512], F32, 2)
            g4 = R([8, 512], F32, 2)
            g5 = R([8, 512], F32, 2)
            g6 = R([8, 512], F32, 2)
            stat_bs = [Buf() for _ in BLOCKS]

            def gate_block(bi, bki, bkf):
                c0, nb = BLOCKS[bi]
                is_s = bi == 4
                L = 8 if is_s else 128
                nch = nb // L
                ntile = nb // 128
                sb_ = stat_bs[bi]
                for gsel, bk in ((0, bki), (1, bkf)):
                    for kc in range(KC):
                        mm(pbank[bk][0:8, :nb], wg[:, kc, gsel * 8:(gsel + 1) * 8], xnT[:, kc, c0:c0 + nb],
                           kc == 0, kc == KC - 1, [wg_b, xnT_b[bi]], [pb_b[bk]], signal=(kc == KC - 1))
                    yield
                li_t, li_b = g1.next()
                lf_t, lf_b = g2.next()
                b_t, b_b = g3.next()
                a_t, a_b = g4.next()
                e_t, e_b = g5.next()
                w_t, w_b = g6.next()
                act(li_t[:, :nb], pbank[bki][0:8, :nb], AF.Tanh, [pb_b[bki], gb_b], [li_b], bias=gb[:, 2:3], scale=1.0 / CAP)
                yield
                vts(li_t[:, :nb], li_t[:, :nb], CAP, None, ALU.mult, None, [li_b], [li_b])
                yield
                act(lf_t[:, :nb], pbank[bkf][0:8, :nb], AF.Tanh, [pb_b[bkf], gb_b], [lf_b], bias=gb[:, 3:4], scale=1.0 / CAP)
                yield
                act(lf_t[:, :nb], lf_t[:, :nb], AF.Exp, [lf_b], [lf_b], scale=-CAP)
                yield
                act(lf_t[:, :nb], lf_t[:, :nb], AF.Ln, [lf_b, misc_b], [lf_b], bias=onec[0:8, 0:1])
                yield
                vts(lf_t[:, :nb], lf_t[:, :nb], -1.0, None, ALU.mult, None, [lf_b], [lf_b])
                yield
                if is_s:
                    vscan(b_t[:, :nb], cs[0:8, C_SM8:C_SM8 + 128], lf_t[:, :nb], 0.0, ALU.mult, ALU.add, [lf_b, cs_b], [b_b])
                    yield
                else:
                    for tt in range(ntile):
                        vscan(b_t[:, tt * 128:(tt + 1) * 128], cs[0:8, C_ONES1:C_ONES1 + 128], lf_t[:, tt * 128:(tt + 1) * 128],
                              0.0, ALU.mult, ALU.add, [lf_b, cs_b], [b_b])
                        yield
                vtt(a_t[:, :nb], li_t[:, :nb], b_t[:, :nb], ALU.subtract, [li_b, b_b], [a_b])
                yield
                bview = b_t[:, :nb].rearrange("p (c l) -> p c l", l=L)
                aview = a_t[:, :nb].rearrange("p (c l) -> p c l", l=L)
                so = 16 if is_s else 4 * bi
                vcopy(stat[:, so:so + nch], bview[:, :, L - 1], [b_b], [sb_])
                yield
                vreduce(stat[:, 32 + so:32 + so + nch], aview, ALU.max, [a_b], [sb_])
                yield
                act(e_t[:, :nb], a_t[:, :nb], AF.Exp, [a_b], [e_b])
                yield
                vtt(w_t[:, :nb].rearrange("p (c l) -> p c l", l=L), aview, bc_mid(stat[:, so:so + nch], L), ALU.add,
                    [a_b, sb_], [w_b])
                yield
                act(w_t[:, :nb], w_t[:, :nb], AF.Exp, [w_b], [w_b])
                yield
                act(b_t[:, :nb], b_t[:, :nb], AF.Exp, [b_b], [b_b], scale=-1.0)
                yield
                for tt in range(ntile):
                    gt = c0 // 128 + tt
                    for qi, (src, srcb) in enumerate(((e_t, e_b), (w_t, w_b), (b_t, b_b))):
                        mm(pbank[2][:, gt * 24 + qi * 8:gt * 24 + qi * 8 + 8], src[0:8, tt * 128:(tt + 1) * 128],
                           cs[0:8, C_IDENT:C_IDENT + 8], True, True, [srcb, cs_b], [pb_b[2]])
                    yield

            run_interleaved([gate_block(0, 0, 1), gate_block(1, 4, 5)])
            run_interleaved([gate_block(2, 0, 1), gate_block(3, 4, 5)])
            run_interleaved([gate_block(4, 0, 1)])
            stat_b = stat_bs
            vcopy(flat(tokq[:]), pbank[2][:, 0:17 * 24], [pb_b[2]], [tokq_b])
            vscan(stat[:, 64:80], stat[:, 32:48], stat[:, 0:16], 0.0, ALU.max, ALU.add, [stat_b], [stat_b])
            vtt(stat[:, 80:96], stat[:, 48:64], m0T[:], ALU.max, [stat_b, m0T_b], [stat_b])
            vtt(stat[:, 80:96], stat[:, 80:96], stat[:, 16:32], ALU.add, [stat_b], [stat_b])
            act(rsel[:, 0:32], stat[:, 0:32], AF.Exp, [stat_b], [rsel_b])
            act(rsel[:, 32:48], m0T[:], AF.Exp, [m0T_b], [rsel_b])
            act(rsel[:, 48:64], stat[:, 80:96], AF.Exp, [stat_b], [rsel_b], scale=-1.0)
            act(rsel[:, 64:65], stat[:, 79:80], AF.Exp, [stat_b], [rsel_b], scale=-1.0)
            P.dma("sync", mp_o[:, :], stat[:, 79:80], reads=[stat_b])
            P.dma("sync", ms_o.rearrange("j h -> h j"), stat[:, 80:96], reads=[stat_b], allow_slow_non_contiguous=True)
            n0t = walloc([128, 128])
            n0t_b = Buf()
            P.dma("sync", n0t[:, 0:64], n0d[:, :], writes=[n0t_b])
            P.dma("sync", n0t[:, 64:128], n0d[:, :], writes=[n0t_b])
            mm(pbank[3][:, 0:128], n0t[:], identf, True, True, [n0t_b, cs_b], [pb_b[3]])
            vcopy(flat(nT[:]), pbank[3][:, 0:128], [pb_b[3]], [nT_b])
            phase_end(mark)

            qT_r = R([128, 512], BF16, 2)
            kT_r = R([128, 512], BF16, 2)
            vaug_r = R([128, 2, 129], BF16, 3)
            for it, itb in vaug_r.items:
                vmemset(it[:, :, 128:129], 1.0, [itb])
            k2_r = R([128, 2, 64], BF16, 3)
            sigo_r = R([128, 256], F32, 2)
            at_r = R([128, 128], BF16, 4)
            sm_r = R([128, 16], F32, 3)
            ssq_r = R([128, 4], F32, 3)
            t4_r = R([128, 256], F32, 2)
            hht_r = R([128, 256], BF16, 2)
            junk_r = R([128, 128], F32, 2)
            Cst = walloc([128, 129])
            Cst_b = Buf()
            Cbf_r = R([128, 129], BF16, 2)
            selsb_r = R([128, 65], F32, 2)
            C0a_r = R([128, 16, 129], F32, 1)
            C0bf_r = R([128, 16, 129], BF16, 1)
            qm = walloc([128, NT], BF16)
            qm_b = Buf()
            vmemset(qm[:], 0.0, [qm_b])
            vblk_r = R([128, 16, 129], BF16, 1)
            nout_r = R([128, 16], F32, 2)
            nout2_r = R([16, 128], F32, 2)

            def load_pair(p):
                a = load_rows(w_min, 0, D, [(p * 768, p * 768 + 512)], 2 * (p % 2))
                b = load_rows(w_min, 0, D, [(p * 768 + 512, (p + 1) * 768)], 2 * (p % 2) + 1)
                return a, b

            b7_st = [pb_b[0], pb_b[7]]
            b7_ht = pb_b[1]
            st_ps = [pbank[0][:, 0:128], pbank[7][:, 0:128]]
            ht_ps = pbank[1][:, 0:256]

            def front_a(p, t, pc, wa, wa_b, wbv, wb_b, sel_t, sel_b):
                bi = min(t // 4, 4)
                c0, nb = BLOCKS[bi]
                is_s = bi == 4
                tt = t - 4 * bi
                if tt == 0:
                    qT_t, qT_b = qT_r.next()
                    kT_t, kT_b = kT_r.next()
                    pc["qk"] = (qT_t, qT_b, kT_t, kT_b)
                    for which, bk in ((0, 2), (1, 3)):
                        for kc in range(KC):
                            mm(pbank[bk][:, :nb], wa[:, kc, which * 128:(which + 1) * 128], xnT[:, kc, c0:c0 + nb],
                               kc == 0, kc == KC - 1, [wa_b, xnT_b[bi]], [pb_b[bk]], signal=(kc == KC - 1))
                            yield
                    vts(qT_t[:, :nb], pbank[2][:, :nb], 0.125, None, ALU.mult, None, [pb_b[2]], [qT_b])
                    acopy(kT_t[:, :nb], pbank[3][:, :nb], [pb_b[3]], [kT_b])
                    yield
                    if is_s:
                        C0a, C0a_b = C0a_r.next()
                        for hh in range(2):
                            P.dma("sync", C0a[hh * 64:(hh + 1) * 64, :, 0:128],
                                  C0d[:, 2 * p + hh, :, :].rearrange("j k v -> k j v"), writes=[C0a_b])
                            vcopy(C0a[hh * 64:(hh + 1) * 64, :, 128], nT[hh * 64:(hh + 1) * 64, :, 2 * p + hh],
                                  [nT_b], [C0a_b])
                        vtt(C0a[:], C0a[:], bc_mid(sel_t[:, 32:48], 129), ALU.mult, [C0a_b, sel_b], [C0a_b])
                        yield
                        C0bf, C0bf_b = C0bf_r.next()
                        acopy(C0bf[:], C0a[:], [C0a_b], [C0bf_b])
                        qmv = qm[:].rearrange("p (j x) -> p j x", x=136)[:, :, 0:8]
                        vcopy(qmv, qT_t[:, 0:128].rearrange("p (j i) -> p j i", i=8), [qT_b], [qm_b])
                        pc["c0"] = (C0a, C0a_b, C0bf, C0bf_b)
                        yield
                tc0 = t * 128
                for kc in range(KC):
                    mm(pbank[2][:, 0:384], xnT[:, kc, tc0:tc0 + 128], wa[:, kc, 128:512], kc == 0, kc == KC - 1,
                       [wa_b, xnT_b[bi]], [pb_b[2]], signal=(kc == KC - 1))
                    yield
                for kc in range(KC):
                    mm(pbank[3][:, 0:256], xnT[:, kc, tc0:tc0 + 128], wbv[:, kc, 0:256], kc == 0, kc == KC - 1,
                       [wb_b, xnT_b[bi]], [pb_b[3]], signal=(kc == KC - 1))
                    yield
                pc["fa"][t] = (pc["qk"], pc.get("c0"))

            def front_b(p, t, pc, wa, wa_b, wbv, wb_b, sel_t, sel_b):
                bi = min(t // 4, 4)
                c0, nb = BLOCKS[bi]
                is_s = bi == 4
                tt = t - 4 * bi
                (qT_t, qT_b, kT_t, kT_b), c0pack = pc["fa"].pop(t)
                gt = t
                tc0 = t * 128
                tl = tt * 128
                va_t, va_b = vaug_r.next()
                k2_t, k2_b = k2_r.next()
                so_t, so_b = sigo_r.next()
                vcopy(va_t[:, :, 0:128], pbank[2][:, 128:384].rearrange("p (a b) -> p a b", b=128), [pb_b[2]], [va_b])
                yield
                vtt(k2_t[:], pbank[2][:, 0:128].rearrange("p (a b) -> p a b", b=64),
                    bc_mid(tokq[:, gt, 8 + 2 * p:10 + 2 * p], 64), ALU.mult, [pb_b[2], tokq_b], [k2_b])
                yield
                act(so_t[:], pbank[3][:, 0:256], AF.Exp, [pb_b[3]], [so_b], scale=-1.0)
                yield
                act(so_t[:], so_t[:], AF.Ln, [so_b, misc_b], [so_b], bias=onec[:, 0:1])
                yield
                act(so_t[:], so_t[:], AF.Exp, [so_b], [so_b], scale=-1.0)
                yield
                mask = cs[:, C_BD128:C_BD128 + 128] if is_s else cs[:, C_TRI128:C_TRI128 + 128]
                ob = 4 + (gt % 2)
                ov = pbank[ob][:, :].rearrange("p (a b) -> p a b", a=2)
                dv_ = pbank[6][:, :].rearrange("p (a b) -> p a b", a=2)
                Cbf_t, Cbf_b = pc["cbf"]
                for hh in range(2):
                    r0 = hh * 64
                    hg = 2 * p + hh
                    mm(st_ps[hh], kT_t[r0:r0 + 64, tl:tl + 128], qT_t[r0:r0 + 64, tl:tl + 128], True, True,
                       [kT_b, qT_b], [b7_st[hh]])
                    yield
                    at_t, at_b = at_r.next()
                    vstt(at_t[:], st_ps[hh], tokq[:, gt, hg:hg + 1], mask, ALU.mult, ALU.mult,
                         [b7_st[hh], tokq_b, cs_b], [at_b])
                    yield
                    mm(ov[:, hh, 0:129], at_t[:], va_t[:, hh, :], True, False, [at_b, va_b], [pb_b[ob]], signal=True, sgc=True)
                    if not is_s:
                        mm(ov[:, hh, 0:129], qT_t[r0:r0 + 64, tl:tl + 128], Cbf_t[r0:r0 + 64, :], False, True,
                           [qT_b, Cbf_b], [pb_b[ob]], sgc=True)
                        yield
                    else:
                        C0a, C0a_b, C0bf, C0bf_b = c0pack
                        for j in range(16):
                            mm(ov[:, hh, 0:129], qm[r0:r0 + 64, j * 128:(j + 1) * 128], C0bf[r0:r0 + 64, j, :],
                               False, j == 15, [qm_b, C0bf_b], [pb_b[ob]], signal=True, sgc=True)
                        yield
                pc["post"] = (ob, ov, so_t, so_b, bi, tt, tc0)
                if not is_s:
                    for hh in range(2):
                        mm(dv_[:, hh, 0:129], flat(k2_t[:]), va_t[:, hh, :], True, True,
                           [k2_b, va_b], [pb_b[6]], signal=True)
                    yield
                    for hh in range(2):
                        r0 = hh * 64
                        vstt(Cst[r0:r0 + 64, :], Cst[r0:r0 + 64, :], sel_t[r0:r0 + 64, gt:gt + 1], dv_[r0:r0 + 64, hh, 0:129],
                             ALU.mult, ALU.add, [Cst_b, sel_b, pb_b[6]], [Cst_b])
                        yield
                    Cbf_t, Cbf_b = Cbf_r.next()
                    acopy(Cbf_t[:], Cst[:], [Cst_b], [Cbf_b])
                    pc["cbf"] = (Cbf_t, Cbf_b)
                    yield
                    if gt == 15:
                        vts(Cst[:], Cst[:], sel_t[:, 64:65], None, ALU.mult, None, [Cst_b, sel_b], [Cst_b])
                        P.dma("sync", Cp_o[2 * p:2 * p + 2, :, :].rearrange("h k v -> (h k) v"), Cst[:, 0:128], reads=[Cst_b])
                        P.dma("sync", np_o[2 * p:2 * p + 2, :].rearrange("h (k o) -> (h k) o", o=1), Cst[:, 128:129],
                              reads=[Cst_b])
                        yield
                else:
                    C0a, C0a_b, C0bf, C0bf_b = c0pack
                    for hh in range(2):
                        r0 = hh * 64
                        vb_t, vb_b = vblk_r.next()
                        vtt(vb_t[:], bc_first(va_t[:, hh, :], 16), bc_mid(cs[:, C_IND16:C_IND16 + 16], 129), ALU.mult,
                            [va_b, cs_b], [vb_b])
                        yield
                        vtt(C0a[r0:r0 + 64, :, :], C0a[r0:r0 + 64, :, :], bc_mid(sel_t[r0:r0 + 64, 16:32], 129), ALU.mult,
                            [C0a_b, sel_b], [C0a_b])
                        yield
                        for g in range(6):
                            j0 = 3 * g
                            nj = min(3, 16 - j0)
                            mm(pbank[6][:, 0:nj * 129], flat(k2_t[:]), flat(vb_t[:, j0:j0 + nj, :]), True, True,
                               [k2_b, vb_b], [pb_b[6]])
                            vtt(C0a[r0:r0 + 64, j0:j0 + nj, :], C0a[r0:r0 + 64, j0:j0 + nj, :],
                                pbank[6][r0:r0 + 64, 0:nj * 129].rearrange("p (a b) -> p a b", b=129), ALU.add,
                                [C0a_b, pb_b[6]], [C0a_b])
                            yield
                    vtt(C0a[:], C0a[:], bc_mid(sel_t[:, 48:64], 129), ALU.mult, [C0a_b, sel_b], [C0a_b])
                    for hh in range(2):
                        P.dma("sync", Cs_o[:, 2 * p + hh, :, :].rearrange("j k v -> k j v"),
                              C0a[hh * 64:(hh + 1) * 64, :, 0:128], reads=[C0a_b])
                    yield
                    no_t, no_b = nout_r.next()
                    vcopy(no_t[:], C0a[:, :, 128], [C0a_b], [no_b])
                    mm(pbank[6][0:16, 0:128], no_t[:], identf, True, True, [no_b, cs_b], [pb_b[6]])
                    no2_t, no2_b = nout2_r.next()
                    vcopy(no2_t[:], pbank[6][0:16, 0:128], [pb_b[6]], [no2_b])
                    P.dma("sync", ns_o[:, 2 * p:2 * p + 2, :].rearrange("j h k -> j (h k)"), no2_t[:], reads=[no2_b])
                    yield

            def back(p, t, post):
                ob, ov, so_t, so_b, bi, tt, tc0 = post
                gt = t
                sm_t, sm_b = sm_r.next()
                den = ov[:, :, 128]
                einv2 = tokq[:, gt, 16 + 2 * p:18 + 2 * p]
                act(sm_t[:, 0:2], den, AF.Abs, [pb_b[ob]], [sm_b])
                yield
                ssq_t, ssq_b = ssq_r.next()
                for hh in range(2):
                    jk_t, jk_b = junk_r.next()
                    act(jk_t[:], ov[:, hh, 0:128], AF.Square, [pb_b[ob]], [jk_b, ssq_b], accum_out=ssq_t[:, hh:hh + 1])
                    yield
                vtt(sm_t[:, 0:2], sm_t[:, 0:2], einv2, ALU.max, [sm_b, tokq_b], [sm_b])
                yield
                vtt(sm_t[:, 2:4], sm_t[:, 0:2], sm_t[:, 0:2], ALU.mult, [sm_b], [sm_b])
                yield
                vstt(sm_t[:, 6:8], sm_t[:, 2:4], EPS * 128.0, ssq_t[:, 0:2], ALU.mult, ALU.add, [sm_b, ssq_b], [sm_b])
                yield
                act(sm_t[:, 8:10], sm_t[:, 6:8], AF.Ln, [sm_b], [sm_b], scale=1.0 / 128.0)
                yield
                act(sm_t[:, 10:12], sm_t[:, 8:10], AF.Exp, [sm_b], [sm_b], scale=-0.5)
                yield
                t4_t, t4_b = t4_r.next()
                hht_t, hht_b = hht_r.next()
                vtt(t4_t[:].rearrange("p (a b) -> p a b", a=2), ov[:, :, 0:128], bc_mid(sm_t[:, 10:12], 128), ALU.mult,
                    [pb_b[ob], sm_b], [t4_b])
                yield
                vtt(hht_t[:], t4_t[:], so_t[:], ALU.mult, [t4_b, so_b], [hht_b])
                yield
                for hh in range(2):
                    mm(ht_ps[:, hh * 128:(hh + 1) * 128], hht_t[:, hh * 128:(hh + 1) * 128], identb[:], True, True,
                       [hht_b, identb_b], [b7_ht], signal=True)
                yield
                for hh in range(2):
                    hl = (2 * p + hh) % 4
                    vts(oT[:, hl, tc0:tc0 + 128], ht_ps[:, hh * 128:(hh + 1) * 128], vT[:, 64 + 2 * p + hh:65 + 2 * p + hh], None,
                        ALU.mult, None, [b7_ht, vT_b], [oT_b[hl][bi]])
                    yield

            nxt = pair0
            for p in range(4):
                (wa, wa_b), (wbv, wb_b) = nxt
                if p + 1 < 4:
                    nxt = load_pair(p + 1)
                sel_t, sel_b = selsb_r.next()
                mm(pbank[6][:, 0:65], selp[0:8, p * 128:(p + 1) * 128], rsel[:, :], True, True,
                   [selp_b, rsel_b], [pb_b[6]])
                vcopy(sel_t[:], pbank[6][:, 0:65], [pb_b[6]], [sel_b])
                vmemset(Cst[:], 0.0, [Cst_b])
                Cbf_t, Cbf_b = Cbf_r.next()
                vmemset(Cbf_t[:], 0.0, [Cbf_b])
                pc = {"cbf": (Cbf_t, Cbf_b), "fa": {}}
                args = (pc, wa, wa_b, wbv, wb_b, sel_t, sel_b)
                run_interleaved([front_a(p, 0, *args)])
                prev_post = None
                for t in range(17):
                    g_a = front_a(p, t + 1, *args) if t + 1 < 17 else None
                    g_b = front_b(p, t, *args)
                    g_back = back(p, t - 1, prev_post) if prev_post is not None else None
                    for _ in range(5):
                        next(g_b)
                    run_interleaved([g_b, g_a, g_back])
                    prev_post = pc["post"]
                run_interleaved([back(p, 16, prev_post)])
                if p % 2 == 1:
                    out_proj(w_mout, (p // 2) * 512, 4, oT, oT_b)
            phase_end()

        def final_phase():
            wfin_bc = walloc([128, D])
            bc_b = Buf()
            P.dma("sync", wfin_bc[:], wfin.partition_broadcast(128), writes=[bc_b])
            yt_r = Ring([128, D], F32, 2)
            jk_r = Ring([128, 512], F32, 2)
            sm_r = Ring([128, 4], F32, 2)
            for tt in range(NT // 128):
                bi = min(tt // 4, 4)
                bks = [(2 * tt) % 8, (2 * tt + 1) % 8]
                for half in range(2):
                    bk = bks[half]
                    for q in range(4):
                        kc = half * 4 + q
                        mm(pbank[bk][:, q * 128:(q + 1) * 128], xT[:, kc, tt * 128:(tt + 1) * 128], identf, True, True,
                           [xT_b[bi], cs_b], [pb_b[bk]], signal=(q == 3))
                sm_t, sm_b = sm_r.next()
                for half in range(2):
                    jk_t, jk_b = jk_r.next()
                    act(jk_t[:], pbank[bks[half]][:, :], AF.Square, [pb_b[bks[half]]], [jk_b, sm_b], accum_out=sm_t[:, half:half + 1])
                vtt(sm_t[:, 2:3], sm_t[:, 0:1], sm_t[:, 1:2], ALU.add, [sm_b], [sm_b])
                act(sm_t[:, 3:4], sm_t[:, 2:3], AF.Ln, [sm_b, misc_b], [sm_b], bias=epsc[:, 0:1], scale=1.0 / D)
                act(sm_t[:, 3:4], sm_t[:, 3:4], AF.Exp, [sm_b], [sm_b], scale=-0.5)
                y_t, y_b = yt_r.next()
                for half in range(2):
                    vstt(y_t[:, half * 512:(half + 1) * 512], pbank[bks[half]][:, :], sm_t[:, 3:4],
                         wfin_bc[:, half * 512:(half + 1) * 512], ALU.mult, ALU.mult, [pb_b[bks[half]], sm_b, bc_b], [y_b])
                P.dma("sync", y_o[tt * 128:(tt + 1) * 128, :], y_t[:], reads=[y_b])
            phase_end()

        def dump_xT():
            P.dma("sync", dbg_o[:, :], flat(xT[:]), reads=xT_b)

        phases = [("hgrn", hgrn_phase), ("ffn0", lambda: ffn_phase(0)), ("mlstm", mlstm_phase), ("ffn1", lambda: ffn_phase(1))]
        if dbg_phase == "x0":
            dump_xT()
        for name, fn in phases:
            if os.environ.get("MK_SKIP_" + name.upper()):
                continue
            fn()
            if dbg_phase == name:
                dump_xT()
        final_phase()
        P.finish()
    print("n_inst", P.n_inst, {n: len(e.prog) for n, e in P.E.items()})
    return nc


def _relayout_mlstm(w):
    parts = []
    for p in range(4):
        parts += [w[:, p * 128:(p + 1) * 128], w[:, 512 + p * 128:512 + (p + 1) * 128],
                  w[:, 1024 + p * 256:1024 + (p + 1) * 256], w[:, 2048 + p * 256:2048 + (p + 1) * 256]]
    parts.append(w[:, 3072:3088])
    return np.ascontiguousarray(np.concatenate(parts, axis=1))


_NC_CACHE = {}


def kernel(x_prompt, x_sample, state_hgrn_S, state_mlstm_C, state_mlstm_n, state_mlstm_m,
           norm_mixer_w, norm_ffn_w, hgrn_w_in, hgrn_lower_bound_logits, hgrn_out_norm_w, hgrn_w_out,
           mlstm_w_in, mlstm_gate_bias, mlstm_out_norm_w, mlstm_w_out, ffn_w_up, ffn_w_down, final_norm_w,
           _dbg_phase=None, _cores=8):
    f = lambda a: np.ascontiguousarray(np.asarray(a, dtype=np.float32))
    x_prompt, x_sample = f(x_prompt), f(x_sample)
    S, C, n, m = f(state_hgrn_S), f(state_mlstm_C), f(state_mlstm_n), f(state_mlstm_m)
    vecs = np.concatenate([f(norm_mixer_w).reshape(16, 128), f(norm_ffn_w).reshape(16, 128),
                           f(hgrn_lower_bound_logits).reshape(24, 128), f(hgrn_out_norm_w).reshape(8, 128),
                           f(mlstm_out_norm_w).reshape(8, 128)], axis=0)
    shared = {
        "vecs": np.ascontiguousarray(vecs),
        "w_hin": np.ascontiguousarray(f(hgrn_w_in)[0].reshape(D, 4, 8, 128).transpose(0, 2, 1, 3).reshape(D, 4096)),
        "w_hout": f(hgrn_w_out)[0], "w_min": _relayout_mlstm(f(mlstm_w_in)[0]), "w_mout": f(mlstm_w_out)[0],
        "w_up": f(ffn_w_up), "w_dn": f(ffn_w_down),
        "gbias": f(mlstm_gate_bias).reshape(16), "wfin": f(final_norm_w),
        "cst": _make_consts(),
    }
    in_maps = []
    for c in range(_cores):
        d = dict(shared)
        d["xin"] = np.ascontiguousarray(np.concatenate([x_prompt[c], x_sample[16 * c:16 * (c + 1)].reshape(NS, D)], axis=0))
        d["S0"] = np.ascontiguousarray(S[0, 16 * c:16 * (c + 1)])
        d["C0"] = np.ascontiguousarray(C[0, 16 * c:16 * (c + 1)])
        d["n0"] = np.ascontiguousarray(n[0, 16 * c:16 * (c + 1)].reshape(128, 64))
        d["m0"] = np.ascontiguousarray(m[0, 16 * c:16 * (c + 1)])
        in_maps.append(d)
    key = _dbg_phase
    if key not in _NC_CACHE:
        _NC_CACHE[key] = build_program(_dbg_phase)
    nc = _NC_CACHE[key]
    runner = globals().get("_RUNNER") or (lambda nc_, im: run_bass_kernel_spmd(nc_, im, core_ids=list(range(len(im)))))
    res = runner(nc, in_maps).results
    B = _cores
    y_prompt = np.stack([res[c]["y"][:NP_] for c in range(B)], axis=0)
    y_sample = np.concatenate([res[c]["y"][NP_:].reshape(16, 8, D) for c in range(B)], axis=0)
    S_p = np.stack([res[c]["S_p"] for c in range(B)], axis=0)[None]
    C_p = np.stack([res[c]["C_p"] for c in range(B)], axis=0)[None]
    n_p = np.stack([res[c]["n_p"] for c in range(B)], axis=0)[None]
    m_p = np.stack([res[c]["m_p"].reshape(8) for c in range(B)], axis=0)[None]
    S_s = np.concatenate([res[c]["S_s"] for c in range(B)], axis=0)[None]
    C_s = np.concatenate([res[c]["C_s"] for c in range(B)], axis=0)[None]
    n_s = np.concatenate([res[c]["n_s"] for c in range(B)], axis=0)[None]
    m_s = np.concatenate([res[c]["m_s"] for c in range(B)], axis=0)[None]
    outs = (y_prompt, y_sample, S_p, C_p, n_p, m_p, S_s, C_s, n_s, m_s)
    if _dbg_phase is not None:
        return outs, [res[c]["dbg"] for c in range(B)]
    return tuple(np.ascontiguousarray(o, dtype=np.float32) for o in outs)
```

```python
import os
from contextlib import ExitStack
import numpy as np
import concourse.bass as bass
import concourse.mybir as mybir
from concourse.bass_utils import run_bass_kernel_spmd

F32 = mybir.dt.float32
BF16 = mybir.dt.bfloat16
AF = mybir.ActivationFunctionType
ALU = mybir.AluOpType
AX = mybir.AxisListType

D = 1024
NP_ = 2048
NS = 128
NT = NP_ + NS
KC = 8
BLOCKS = [(0, 512), (512, 512), (1024, 512), (1536, 512), (2048, 128)]
EPS = 1e-6
WARM_N = int(os.environ.get("MK_WARM_N", "24"))
CAP = 15.0

C_IDENT = 0
C_ONESD = 128
C_ONES128 = 256
C_MASKH = 384
C_BD128 = 512
C_TRI128 = 640
C_SM64 = 768
C_SM8 = 1280
C_IND16 = 1408
C_ONES1 = 1424
C_MAIN = 1552
C_SELP = 1552
C_TOTAL = 1552 + 512


def _make_consts():
    c = np.zeros((128, C_TOTAL), np.float32)
    s = np.arange(128)[:, None]
    t = np.arange(128)[None, :]
    c[:, C_IDENT:C_IDENT + 128] = np.eye(128, dtype=np.float32)
    c[:, C_ONESD:C_ONESD + 128] = 1.0 / 1024.0
    c[:, C_ONES128:C_ONES128 + 128] = 1.0 / 128.0
    c[:, C_MASKH:C_MASKH + 128] = ((s // 64 == t // 64) & (s <= t)).astype(np.float32)
    c[:, C_BD128:C_BD128 + 128] = ((s // 8 == t // 8) & (s <= t)).astype(np.float32)
    c[:, C_TRI128:C_TRI128 + 128] = (s <= t).astype(np.float32)
    col = np.arange(512)[None, :]
    c[:, C_SM64:C_SM64 + 512] = (col % 64 != 0).astype(np.float32)
    c[:, C_SM8:C_SM8 + 128] = (t % 8 != 0).astype(np.float32)
    c[:, C_IND16:C_IND16 + 16] = (s // 8 == np.arange(16)[None, :]).astype(np.float32)
    c[:, C_ONES1:C_ONES1 + 128] = 1.0
    for p in range(4):
        for m in range(128):
            c[2 * p + m // 64, C_SELP + p * 128 + m] = 1.0
    return c


class Buf:
    __slots__ = ("w", "r")

    def __init__(self):
        self.w = None
        self.r = []


class Eng:
    def __init__(self, name, sem):
        self.name = name
        self.sem = sem
        self.count = 0
        self.waited = {}
        self.prog = []


class Prog:
    def __init__(self, nc, stack):
        self.nc = nc
        self.stack = stack
        self.E = {}
        for n in ("tensor", "vector", "scalar", "gpsimd", "sync"):
            self.E[n] = Eng(n, stack.enter_context(nc.semaphore("s_" + n)))
        self.pools = {}
        self.pool_idx = {}
        for ename, k in (("sync", 20), ("gpsimd", 12)):
            self.pools[ename] = [[stack.enter_context(nc.semaphore("dq_%s_%d" % (ename, i))), 0] for i in range(k)]
            self.pool_idx[ename] = 0
        self.n_inst = 0

    def _wait(self, e, tok):
        sem, val = tok
        if sem is e.sem and (e.name == "tensor" or val > e.count):
            return
        key = id(sem)
        if e.waited.get(key, 0) < val:
            e.waited[key] = val
            e.prog.append(lambda eng, sem=sem, val=val: eng.wait_ge(sem, val))

    @staticmethod
    def _flat(bufs):
        out = []
        for b in bufs:
            if isinstance(b, (list, tuple)):
                out.extend(Prog._flat(b))
            else:
                out.append(b)
        return out

    def _deps(self, e, reads, writes):
        for b in reads:
            if b.w is not None:
                self._wait(e, b.w)
        for b in writes:
            if b.w is not None:
                self._wait(e, b.w)
            for t in b.r:
                self._wait(e, t)

    def emit(self, ename, fn, reads=(), writes=(), signal=True):
        e = self.E[ename]
        reads, writes = self._flat(reads), self._flat(writes)
        self._deps(e, reads, writes)
        self.n_inst += 1
        tok = (e.sem, e.count + 1)
        if signal:
            e.count += 1
            e.prog.append(lambda eng, sem=e.sem: fn(eng).then_inc(sem, 1))
        else:
            e.prog.append(lambda eng: fn(eng))
        for b in reads:
            b.r.append(tok)
        for b in writes:
            b.w = tok
            b.r = []
        return tok

    def dma(self, ename, out, in_, reads=(), writes=(), **kw):
        e = self.E[ename]
        pool = self.pools[ename]
        slot = pool[self.pool_idx[ename] % len(pool)]
        self.pool_idx[ename] += 1
        if slot[1] > 0:
            self._wait(e, (slot[0], slot[1]))
        reads, writes = self._flat(reads), self._flat(writes)
        self._deps(e, reads, writes)
        slot[1] += 16
        tok = (slot[0], slot[1])
        self.n_inst += 1
        e.prog.append(lambda eng, sem=slot[0]: eng.dma_start(out=out, in_=in_, **kw).then_inc(sem, 16))
        for b in reads:
            b.r.append(tok)
        for b in writes:
            b.w = tok
            b.r = []
        return tok

    def barrier(self):
        toks = [(e.sem, e.count) for e in self.E.values() if e.count > 0]
        for pool in self.pools.values():
            for s in pool:
                if s[1] > 0:
                    toks.append((s[0], s[1]))
        for e in self.E.values():
            for t in toks:
                self._wait(e, t)

    def finish(self):
        self.barrier()
        nc = self.nc
        with nc.Block() as block:
            for n, deco in (("sync", block.sync), ("gpsimd", block.gpsimd), ("tensor", block.tensor),
                            ("vector", block.vector), ("scalar", block.scalar)):
                prog = self.E[n].prog

                def body(eng, prog=prog):
                    for f in prog:
                        f(eng)
                deco(body)


def build_program(dbg_phase=None):
    nc = bass.Bass("TRN2", target_bir_lowering=False)

    def din(name, shape):
        return nc.dram_tensor(name, list(shape), F32, kind="ExternalInput").ap()

    def dout(name, shape):
        return nc.dram_tensor(name, list(shape), F32, kind="ExternalOutput").ap()

    xin = din("xin", [NT, D])
    S0d = din("S0", [16, 8, 128, 128])
    C0d = din("C0", [16, 8, 64, 128])
    n0d = din("n0", [128, 64])
    m0d = din("m0", [16, 8])
    vecs = din("vecs", [72, 128])
    w_hin = din("w_hin", [D, 4096])
    w_hout = din("w_hout", [D, D])
    w_min = din("w_min", [D, 3088])
    w_mout = din("w_mout", [D, D])
    w_up = din("w_up", [2, D, 4096])
    w_dn = din("w_dn", [2, 4096, D])
    gbias = din("gbias", [16])
    wfin = din("wfin", [D])
    cst = din("cst", [128, C_TOTAL])

    y_o = dout("y", [NT, D])
    Sp_o = dout("S_p", [8, 128, 128])
    Cp_o = dout("C_p", [8, 64, 128])
    np_o = dout("n_p", [8, 64])
    mp_o = dout("m_p", [8, 1])
    Ss_o = dout("S_s", [16, 8, 128, 128])
    Cs_o = dout("C_s", [16, 8, 64, 128])
    ns_o = dout("n_s", [16, 8, 64])
    ms_o = dout("m_s", [16, 8])
    dbg_o = dout("dbg", [128, KC * NT]) if dbg_phase is not None else None

    with ExitStack() as st:
        def sb(name, shape, dtype=F32):
            return st.enter_context(nc.sbuf_tensor(name, list(shape), dtype))

        P = Prog(nc, st)

        xT = sb("xT", [128, KC, NT])
        xnT = sb("xnT", [128, KC, NT], BF16)
        wsl = [sb("wsl%d" % i, [128, KC, 512], BF16) for i in range(4)]
        wsl_b = [Buf() for _ in range(4)]
        cs = sb("cs", [128, C_MAIN])
        cs_b = Buf()
        identb = sb("identb", [128, 128], BF16)
        identb_b = Buf()
        onesDb = sb("onesDb", [128, 128], BF16)
        ones128b = sb("ones128b", [128, 128], BF16)
        onesDb_b = Buf()
        vT = sb("vT", [128, 72])
        vT_b = Buf()
        lbt = sb("lbt", [128, 16])
        lbt_b = Buf()
        epsc = sb("epsc", [128, 1])
        onec = sb("onec", [128, 1])
        misc_b = Buf()
        WORK_F32 = (nc.sbuf_bytes_remaining - 512) // 4 // 8 * 8
        work = sb("work", [128, WORK_F32])
        wptr = [0]

        def walloc(shape, dtype=F32):
            n = 1
            for d_ in shape[1:]:
                n *= d_
            nf32 = (n * (4 if dtype == F32 else 2) + 3) // 4
            nf32 = (nf32 + 7) // 8 * 8
            off = wptr[0]
            wptr[0] += nf32
            assert wptr[0] <= WORK_F32, ("work overflow", wptr[0], WORK_F32)
            ap = work[0:shape[0], off:off + nf32]
            if dtype != F32:
                ap = ap.bitcast(dtype)
            ap = ap[:, 0:n]
            if len(shape) == 3:
                ap = ap.rearrange("p (a b) -> p a b", a=shape[1])
            return ap

        class Ring:
            def __init__(self, shape, dtype, n):
                self.items = [(walloc(shape, dtype), Buf()) for _ in range(n)]
                self.i = 0

            def next(self):
                it = self.items[self.i % len(self.items)]
                self.i += 1
                return it

        def phase_end(mark=0):
            P.barrier()
            wptr[0] = mark

        xT_b = [Buf() for _ in BLOCKS]
        xnT_b = [Buf() for _ in BLOCKS]

        pbank = [nc.alloc_psum_tensor("pb%d" % i, [128, 512], F32) for i in range(8)]
        pb_b = [Buf() for _ in range(8)]

        identf = cs[:, C_IDENT:C_IDENT + 128]
        onesD = cs[:, C_ONESD:C_ONESD + 128]
        ones128 = cs[:, C_ONES128:C_ONES128 + 128]

        def mm(out, lhsT, rhs, start, stop, reads, writes, signal=True, sgc=False):
            return P.emit("tensor", lambda e: e.matmul(out, lhsT=lhsT, rhs=rhs, start=start, stop=stop,
                                                       skip_group_check=sgc), reads, writes, signal)

        def act(out, in_, func, reads, writes, bias=None, scale=1.0, accum_out=None):
            kw = {}
            if bias is not None:
                kw["bias"] = bias
            if accum_out is not None:
                kw["accum_out"] = accum_out
            return P.emit("scalar", lambda e: e.activation(out=out, in_=in_, func=func, scale=scale, **kw),
                          reads, writes)

        def amul(out, in_, mul, reads, writes):
            return P.emit("scalar", lambda e: e.mul(out=out, in_=in_, mul=mul), reads, writes)

        def acopy(out, in_, reads, writes):
            return P.emit("scalar", lambda e: e.copy(out=out, in_=in_), reads, writes)

        def vtt(out, in0, in1, op, reads, writes, eng="vector"):
            return P.emit(eng, lambda e: e.tensor_tensor(out=out, in0=in0, in1=in1, op=op), reads, writes)

        def vts(out, in0, s1, s2, op0, op1, reads, writes, eng="vector"):
            if s2 is None:
                return P.emit(eng, lambda e: e.tensor_scalar(out=out, in0=in0, scalar1=s1, scalar2=None, op0=op0),
                              reads, writes)
            return P.emit(eng, lambda e: e.tensor_scalar(out=out, in0=in0, scalar1=s1, scalar2=s2, op0=op0, op1=op1),
                          reads, writes)

        def vstt(out, in0, scalar, in1, op0, op1, reads, writes, eng="vector"):
            return P.emit(eng, lambda e: e.scalar_tensor_tensor(out=out, in0=in0, scalar=scalar, in1=in1,
                                                               op0=op0, op1=op1), reads, writes)

        def vcopy(out, in_, reads, writes, eng="vector"):
            return P.emit(eng, lambda e: e.tensor_copy(out=out, in_=in_), reads, writes)

        def vrecip(out, in_, reads, writes):
            return P.emit("vector", lambda e: e.reciprocal(out=out, in_=in_), reads, writes)

        def vmemset(ap, val, writes, eng="vector"):
            return P.emit(eng, lambda e: e.memset(ap, val), (), writes)

        def vscan(out, d0, d1, init, op0, op1, reads, writes):
            return P.emit("vector", lambda e: e.tensor_tensor_scan(out=out, data0=d0, data1=d1, initial=init,
                                                                   op0=op0, op1=op1), reads, writes)

        def vreduce(out, in_, op, reads, writes):
            return P.emit("vector", lambda e: e.tensor_reduce(out=out, in_=in_, axis=AX.X, op=op), reads, writes)

        def vabs(out, in_, reads, writes):
            return P.emit("vector", lambda e: e.tensor_single_scalar(out=out, in_=in_, scalar=0.0, op=ALU.abs_max),
                          reads, writes)

        def bc_mid(ap2, n):
            return ap2.unsqueeze(2).to_broadcast([ap2.shape[0], ap2.shape[1], n])

        def bc_first(ap2, n):
            return ap2.unsqueeze(1).to_broadcast([ap2.shape[0], n, ap2.shape[1]])

        def flat(ap3):
            return ap3.rearrange("p a b -> p (a b)")

        def keep_warm(n, bank=7):
            for _ in range(n):
                mm(pbank[bank][:, :], identb[:], xnT[:, 0, 0:512], True, True, [identb_b], [pb_b[bank]], signal=False)

        P.dma("sync", cs[:], cst[:, 0:C_MAIN], writes=[cs_b])
        P.dma("gpsimd", identb[:], cst[:, C_IDENT:C_IDENT + 128], writes=[identb_b])
        P.dma("gpsimd", onesDb[:], cst[:, C_ONESD:C_ONESD + 128], writes=[onesDb_b])
        P.dma("gpsimd", ones128b[:], cst[:, C_ONES128:C_ONES128 + 128], writes=[onesDb_b])
        vmemset(epsc[:], EPS, [misc_b])
        vmemset(onec[:], 1.0, [misc_b])
        vraw = walloc([72, 128])
        vraw_b = Buf()
        lt = walloc([128, 40])
        lt_b = Buf()
        P.dma("sync", vraw[:], vecs[:, :], writes=[vraw_b])
        mm(pbank[0][:, 0:72], vraw[:, :], cs[0:72, C_IDENT:C_IDENT + 72], True, True, [vraw_b, cs_b], [pb_b[0]])
        vcopy(vT[:], pbank[0][:, 0:72], [pb_b[0]], [vT_b])
        l0, l1, l2 = vT[:, 32:40], vT[:, 40:48], vT[:, 48:56]
        vtt(lt[:, 0:8], l0, l1, ALU.max, [vT_b], [lt_b])
        vtt(lt[:, 0:8], lt[:, 0:8], l2, ALU.max, [vT_b, lt_b], [lt_b])
        for i, l in enumerate((l0, l1, l2)):
            vtt(lt[:, 8 + 8 * i:16 + 8 * i], l, lt[:, 0:8], ALU.subtract, [vT_b, lt_b], [lt_b])
        act(lt[:, 8:32], lt[:, 8:32], AF.Exp, [lt_b], [lt_b])
        vtt(lt[:, 32:40], lt[:, 8:16], lt[:, 16:24], ALU.add, [lt_b], [lt_b])
        vtt(lt[:, 32:40], lt[:, 32:40], lt[:, 24:32], ALU.add, [lt_b], [lt_b])
        vrecip(lt[:, 32:40], lt[:, 32:40], [lt_b], [lt_b])
        vtt(lbt[:, 0:8], lt[:, 8:16], lt[:, 32:40], ALU.mult, [lt_b], [lbt_b])
        vts(lbt[:, 8:16], lbt[:, 0:8], -1.0, 1.0, ALU.mult, ALU.add, [lbt_b], [lbt_b])

        xtok = Ring([128, D], F32, 2)
        for tt in range(NT // 128):
            bi = min(tt // 4, 4)
            xt, xt_b = xtok.next()
            P.dma("sync", xt[:], xin[tt * 128:(tt + 1) * 128, :], writes=[xt_b])
            for half in range(2):
                bk = (2 * tt + half) % 8
                for q in range(4):
                    kc = half * 4 + q
                    mm(pbank[bk][:, q * 128:(q + 1) * 128], xt[:, kc * 128:(kc + 1) * 128], identf, True, True,
                       [xt_b, cs_b], [pb_b[bk]], signal=(q == 3))
                dst = xT[:, half * 4:half * 4 + 4, tt * 128:(tt + 1) * 128]
                src = pbank[bk][:, :].rearrange("p (q t) -> p q t", q=4)
                if half == 0:
                    vcopy(dst, src, [pb_b[bk]], [xT_b[bi]])
                else:
                    acopy(dst, src, [pb_b[bk]], [xT_b[bi]])
        phase_end()

        slot_rr = [0]

        def load_rows(dram2d, row0, nrow, col_ranges, i):
            nk = nrow // 128
            src = dram2d[row0:row0 + nrow, :].rearrange("(kc p) n -> p kc n", p=128)
            ncols = sum(c1 - c0 for c0, c1 in col_ranges)
            assert nk * ncols <= KC * 512
            view = flat(wsl[i][:])[:, 0:nk * ncols].rearrange("p (a b) -> p a b", a=nk)
            off = 0
            for c0, c1 in col_ranges:
                P.dma("gpsimd", view[:, :, off:off + (c1 - c0)], src[:, :, c0:c1], writes=[wsl_b[i]])
                off += c1 - c0
            return view, wsl_b[i]

        def rmsnorm_to_xn(wcol0):
            mark = wptr[0]
            sq = Ring([128, 512], BF16, 4)
            rs = Ring([128, 512], F32, 2)
            for bi, (c0, nb) in enumerate(BLOCKS):
                bk = bi % 2
                for kc in range(KC):
                    s_t, s_b = sq.next()
                    act(s_t[:, :nb], xT[:, kc, c0:c0 + nb], AF.Square, [xT_b[bi]], [s_b])
                    mm(pbank[bk][:, :nb], onesDb[:], s_t[:, :nb], kc == 0, kc == KC - 1, [s_b, onesDb_b], [pb_b[bk]],
                       signal=True)
                r_t, r_b = rs.next()
                act(r_t[:, :nb], pbank[bk][:, :nb], AF.Ln, [pb_b[bk], misc_b], [r_b], bias=epsc[:, 0:1])
                act(r_t[:, :nb], r_t[:, :nb], AF.Exp, [r_b], [r_b], scale=-0.5)
                for kc in range(KC):
                    vstt(xnT[:, kc, c0:c0 + nb], xT[:, kc, c0:c0 + nb], vT[:, wcol0 + kc:wcol0 + kc + 1], r_t[:, :nb],
                         ALU.mult, ALU.mult, [xT_b[bi], vT_b, r_b], [xnT_b[bi]])
            phase_end(mark)

        opk = [0]

        def out_proj(wdram, row0, nk, oT, oT_b):
            wv, wb = load_rows(wdram, row0, 128 * nk, [(0, D)], 2)
            for bi, (c0, nb) in enumerate(BLOCKS):
                for m in range(8):
                    bk = 6 + opk[0] % 2
                    opk[0] += 1
                    for kc in range(nk):
                        mm(pbank[bk][:, :nb], wv[:, kc, m * 128:(m + 1) * 128], oT[:, kc, c0:c0 + nb],
                           kc == 0, kc == nk - 1, [wb, oT_b[kc][bi]], [pb_b[bk]], signal=(kc == nk - 1))
                    vtt(xT[:, m, c0:c0 + nb], xT[:, m, c0:c0 + nb], pbank[bk][:, :nb], ALU.add,
                        [pb_b[bk], xT_b[bi]], [xT_b[bi]])

        def run_interleaved(gens):
            alive = [g for g in gens if g is not None]
            while alive:
                for g in list(alive):
                    try:
                        next(g)
                    except StopIteration:
                        alive.remove(g)

        def ffn_phase(layer):
            loads = {}

            def load_slice(s):
                up = load_rows(w_up[layer], 0, D, [(s * 512, (s + 1) * 512)], 2 * (s % 2))
                dn = load_rows(w_dn[layer], s * 512, 512, [(0, D)], 2 * (s % 2) + 1)
                loads[s] = (up, dn)

            load_slice(0)
            rmsnorm_to_xn(16 + 8 * layer)
            relu_r = Ring([128, 512], F32, 3)
            hT = walloc([128, 8, NT], BF16).rearrange("p (s m) t -> p s m t", s=2)
            hT_b = [[Buf() for _ in BLOCKS] for _ in range(2)]
            kup = 0
            kdn = 0
            for s in range(8):
                if s + 1 < 8:
                    load_slice(s + 1)
                (uv, ub), (dv, db) = loads.pop(s)
                par = s % 2
                for bi, (c0, nb) in enumerate(BLOCKS):
                    for m in range(4):
                        bk = kup % 4
                        kup += 1
                        for kc in range(KC):
                            mm(pbank[bk][:, :nb], uv[:, kc, m * 128:(m + 1) * 128], xnT[:, kc, c0:c0 + nb],
                               kc == 0, kc == KC - 1, [ub, xnT_b[bi]], [pb_b[bk]], signal=(kc == KC - 1))
                        r_t, r_b = relu_r.next()
                        act(r_t[:, :nb], pbank[bk][:, :nb], AF.Relu, [pb_b[bk]], [r_b])
                        act(hT[:, par, m, c0:c0 + nb], r_t[:, :nb], AF.Square, [r_b], [hT_b[par][bi]])
                for bi, (c0, nb) in enumerate(BLOCKS):
                    for m in range(8):
                        bk = 4 + kdn % 4
                        kdn += 1
                        for kc in range(4):
                            mm(pbank[bk][:, :nb], dv[:, kc, m * 128:(m + 1) * 128], hT[:, par, kc, c0:c0 + nb],
                               kc == 0, kc == 3, [db, hT_b[par][bi]], [pb_b[bk]], signal=(kc == 3))
                        vtt(xT[:, m, c0:c0 + nb], xT[:, m, c0:c0 + nb], pbank[bk][:, :nb], ALU.add,
                            [pb_b[bk], xT_b[bi]], [xT_b[bi]])
            phase_end()

        def hgrn_phase():
            w_head0 = load_rows(w_hin, 0, D, [(0, 512)], 0)
            rmsnorm_to_xn(0)
            oT = walloc([128, 4, NT], BF16)
            oT_b = [[Buf() for _ in BLOCKS] for _ in range(4)]
            R = Ring
            t_f = R([128, 512], F32, 1)
            t_kk = R([128, 512], F32, 1)
            t_b = R([128, 512], F32, 1)
            t_e1 = R([128, 512], F32, 1)
            t_sq = R([128, 512], F32, 1)
            t_sg = R([128, 512], F32, 2)
            t_qt = R([128, 512], BF16, 2)
            t_kt = R([128, 512], BF16, 1)
            t_v = R([128, 4, 128], BF16, 1)
            t_ktk = R([128, 4, 128], BF16, 1)
            t_at = R([128, 4, 128], BF16, 1)
            t_sc = R([128, 64], F32, 2)
            t_S = R([128, 128], F32, 5)
            t_Sp = R([128, 128], BF16, 2)
            t_T = R([128, 8, 128], F32, 1)
            S0r = R([128, 16, 128], F32, 1)
            Sp16 = R([128, 16, 128], BF16, 1)
            Vblk = R([128, 16, 128], BF16, 1)
            Tq = R([128, 4, 128], F32, 1)
            p_osq = R([128, 256], BF16, 1)
            p_rs = R([128, 256], F32, 1)
            p_on = R([128, 256], F32, 1)
            hs = {}
            bq, bf_, bg, bv = 0, 1, 2, 3
            half_bufs = {}

            def halves(bf):
                if id(bf) not in half_bufs:
                    half_bufs[id(bf)] = (bf, Buf())
                return half_bufs[id(bf)]

            class HB(list):
                pass

            sce_bufs = {}

            def sce_halves(bf):
                if id(bf) not in sce_bufs:
                    sce_bufs[id(bf)] = (Buf(), Buf())
                return sce_bufs[id(bf)]

            def chain_post(h, bi, d):
                yield from st_chain(h, bi, d)
                yield from st_post(h, bi, d)

            def head_cols(h):
                return [(h * 512, (h + 1) * 512)]

            wslots = {0: w_head0}

            def geom(bi):
                c0, nb = BLOCKS[bi]
                is_s = bi == 4
                L = 8 if is_s else 64
                return c0, nb, is_s, nb // 128, L, nb // L

            def st_proj(h, bi):
                c0, nb, is_s, ntile, L, nch = geom(bi)
                if bi == 0 and h + 1 < 8:
                    wslots[h + 1] = load_rows(w_hin, 0, D, head_cols(h + 1), (h + 1) % 2)
                wv, wb = wslots[h]
                for comp, bk in ((1, bf_), (0, bq), (3, bg)):
                    for kc in range(KC):
                        mm(pbank[bk][:, :nb], wv[:, kc, comp * 128:(comp + 1) * 128], xnT[:, kc, c0:c0 + nb],
                           kc == 0, kc == KC - 1, [wb, xnT_b[bi]], [pb_b[bk]], signal=(kc == KC - 1))
                for tt in range(ntile):
                    for kc in range(KC):
                        mm(pbank[bv][:, tt * 128:(tt + 1) * 128], xnT[:, kc, c0 + tt * 128:c0 + (tt + 1) * 128],
                           wv[:, kc, 256:384], kc == 0, kc == KC - 1, [wb, xnT_b[bi]], [pb_b[bv]],
                           signal=(kc == KC - 1 and tt == ntile - 1))

            def st_ew(h, bi, out):
                c0, nb, is_s, ntile, L, nch = geom(bi)
                f_t, f_b = t_f.next()
                kk_t, kk_b = t_kk.next()
                b_t, b_b = t_b.next()
                e1_t, e1_b = t_e1.next()
                sq_t, sq_b = t_sq.next()
                sg_t, sg_b = t_sg.next()
                qt_t, qt_b = t_qt.next()
                kt_t, kt_b = t_kt.next()
                v_t, v_b = t_v.next()
                sc_t, sc_b = t_sc.next()
                f_b, kk_b, b_b, e1_b, sq_b, sg_b, qt_b, kt_b, sc_b = (halves(x) for x in (f_b, kk_b, b_b, e1_b, sq_b, sg_b, qt_b, kt_b, sc_b))
                sce_b = sce_halves(sc_b[0])

                def ew_half(hf, lo, hi, ch0, nchh):
                    cl = slice(lo, hi)
                    fb, kkb, bb, e1b, sqb, sgb, qtb, ktb, scb = (x[hf] for x in (f_b, kk_b, b_b, e1_b, sq_b, sg_b, qt_b, kt_b, sc_b))
                    sceb = sce_b[hf]

                    def sig3(dst, dst_b, src_ps, src_b):
                        act(dst[:, cl], src_ps[:, cl], AF.Exp, [src_b], [dst_b], scale=-1.0)
                        yield
                        act(dst[:, cl], dst[:, cl], AF.Ln, [dst_b, misc_b], [dst_b], bias=onec[:, 0:1])
                        yield
                        act(dst[:, cl], dst[:, cl], AF.Exp, [dst_b], [dst_b], scale=-1.0)
                        yield

                    yield from sig3(f_t, fb, pbank[bf_], pb_b[bf_])
                    vts(f_t[:, cl], f_t[:, cl], lbt[:, 8 + h:9 + h], lbt[:, h:h + 1], ALU.mult, ALU.add, [fb, lbt_b], [fb])
                    yield
                    vts(kk_t[:, cl], f_t[:, cl], -1.0, 1.0, ALU.mult, ALU.add, [fb], [kkb])
                    yield
                    act(f_t[:, cl], f_t[:, cl], AF.Ln, [fb], [fb])
                    yield
                    yield from sig3(sq_t, sqb, pbank[bq], pb_b[bq])
                    smc = cs[:, C_SM8:C_SM8 + (hi - lo)] if is_s else cs[:, C_SM64:C_SM64 + (hi - lo)]
                    vscan(b_t[:, cl], smc, f_t[:, cl], 0.0, ALU.mult, ALU.add, [fb, cs_b], [bb])
                    yield
                    bview = b_t[:, cl].rearrange("p (c l) -> p c l", l=L)
                    bL = bview[:, :, L - 1]
                    vcopy(sc_t[:, ch0:ch0 + nchh], bL, [bb], [scb])
                    yield
                    vtt(sq_t[:, cl], sq_t[:, cl], pbank[bq][:, cl], ALU.mult, [sqb, pb_b[bq]], [sqb])
                    yield
                    act(sc_t[:, 16 + ch0:16 + ch0 + nchh], bL, AF.Exp, [bb], [sceb])
                    yield
                    vtt(bview, bview, bc_mid(sc_t[:, ch0:ch0 + nchh], L), ALU.subtract, [bb, scb], [bb])
                    yield
                    act(e1_t[:, cl], b_t[:, cl], AF.Exp, [bb], [e1b])
                    yield
                    act(b_t[:, cl], b_t[:, cl], AF.Exp, [bb], [bb], scale=-1.0)
                    yield
                    vtt(qt_t[:, cl], sq_t[:, cl], e1_t[:, cl], ALU.mult, [sqb, e1b], [qtb])
                    yield
                    vtt(kt_t[:, cl], kk_t[:, cl], b_t[:, cl], ALU.mult, [kkb, bb], [ktb])
                    yield
                    yield from sig3(sg_t, sgb, pbank[bg], pb_b[bg])
                    vtt(sg_t[:, cl], sg_t[:, cl], pbank[bg][:, cl], ALU.mult, [sgb, pb_b[bg]], [sgb])
                    yield

                if is_s:
                    ga, gb_ = ew_half(0, 0, 64, 0, 8), ew_half(1, 64, 128, 8, 8)
                else:
                    ga, gb_ = ew_half(0, 0, 256, 0, 4), ew_half(1, 256, 512, 4, 4)
                alive = [ga, gb_]
                while alive:
                    for g in list(alive):
                        try:
                            next(g)
                            yield
                        except StopIteration:
                            alive.remove(g)
                acopy(v_t[:, 0:ntile, :], pbank[bv][:, :nb].rearrange("p (a b) -> p a b", b=128), [pb_b[bv]], [v_b])
                yield
                out["d"] = dict(qt=(qt_t, HB(qt_b)), kt=(kt_t, HB(kt_b)), v=(v_t, v_b), sc=(sc_t, HB(list(sc_b) + list(sce_b))), sg=(sg_t, HB(sg_b)))

            def st_mid(h, bi, d):
                c0, nb, is_s, ntile, L, nch = geom(bi)
                qt_t, qt_b = d["qt"]
                kt_t, kt_b = d["kt"]
                v_t, v_b = d["v"]
                sc_t, sc_b = d["sc"]
                ktk_t, ktk_b = t_ktk.next()
                at_t, at_b = t_at.next()
                for tt in range(ntile):
                    mm(pbank[0][:, tt * 128:(tt + 1) * 128], kt_t[:, tt * 128:(tt + 1) * 128],
                       qt_t[:, tt * 128:(tt + 1) * 128], True, True, [kt_b, qt_b], [pb_b[0]],
                       signal=(tt == ntile - 1))
                for tt in range(ntile):
                    mm(pbank[1][:, tt * 128:(tt + 1) * 128], kt_t[:, tt * 128:(tt + 1) * 128], identb[:],
                       True, True, [kt_b, identb_b], [pb_b[1]], signal=(tt == ntile - 1))
                mask = cs[:, C_BD128:C_BD128 + 128] if is_s else cs[:, C_MASKH:C_MASKH + 128]
                vtt(at_t[:, 0:ntile, :], pbank[0][:, :nb].rearrange("p (a b) -> p a b", b=128),
                    bc_first(mask, ntile), ALU.mult, [pb_b[0], cs_b], [at_b])
                acopy(ktk_t[:, 0:ntile, :], pbank[1][:, :nb].rearrange("p (a b) -> p a b", b=128), [pb_b[1]], [ktk_b])
                for tt in range(ntile):
                    mm(pbank[4][:, tt * 128:(tt + 1) * 128], v_t[:, tt, :], at_t[:, tt, :], tt == 0, False,
                       [v_b, at_b], [pb_b[4]], signal=False, sgc=True)
                d["ktk"] = (ktk_t, ktk_b)
                if not is_s:
                    for ci in range(nch):
                        tt, hf = ci // 2, ci % 2
                        r0 = hf * 64
                        bk = 5 + hf
                        mm(pbank[bk][:, tt * 128:(tt + 1) * 128], ktk_t[r0:r0 + 64, tt, :], v_t[r0:r0 + 64, tt, :],
                           True, True, [ktk_b, v_b], [pb_b[bk]], signal=(ci >= nch - 2))

            def st_chain(h, bi, d):
                c0, nb, is_s, ntile, L, nch = geom(bi)
                qt_t, qt_b = d["qt"]
                v_t, v_b = d["v"]
                sc_t, sc_b = d["sc"]
                ktk_t, ktk_b = d["ktk"]
                if bi == 0:
                    S0t, S0b = S0r.next()
                    P.dma("sync", S0t[:], S0d[:, h, :, :].rearrange("j d v -> d j v"), writes=[S0b])
                    S_t, S_b = t_S.next()
                    vmemset(S_t[:], 0.0, [S_b])
                    hs["S"] = (S_t, S_b)
                    hs["S0"] = (S0t, S0b)
                S_t, S_b = hs["S"]
                yield
                if not is_s:
                    for ci in range(nch):
                        Sn_t, Sn_b = t_S.next()
                        vstt(Sn_t[:], S_t[:], sc_t[:, 16 + ci:17 + ci],
                             pbank[5 + ci % 2][:, (ci // 2) * 128:(ci // 2 + 1) * 128], ALU.mult, ALU.add,
                             [S_b, sc_b, pb_b[5 + ci % 2]], [Sn_b])
                        Sp_t, Sp_b = t_Sp.next()
                        amul(Sp_t[:], S_t[:], sc_t[:, 16 + ci:17 + ci], [S_b, sc_b], [Sp_b])
                        mm(pbank[4][:, ci * 64:(ci + 1) * 64], Sp_t[:], qt_t[:, ci * 64:(ci + 1) * 64], False, True,
                           [Sp_b, qt_b], [pb_b[4]], signal=True, sgc=True)
                        S_t, S_b = Sn_t, Sn_b
                        yield
                    hs["S"] = (S_t, S_b)
                else:
                    S0t, S0b = hs["S0"]
                    P.dma("sync", Sp_o[h, :, :], S_t[:], reads=[S_b])
                    sp16, sp16_b = Sp16.next()
                    vtt(sp16[:], S0t[:], bc_mid(sc_t[:, 16:32], 128), ALU.mult, [S0b, sc_b], [sp16_b])
                    for j in range(16):
                        mm(pbank[4][:, j * 8:(j + 1) * 8], sp16[:, j, :], qt_t[:, j * 8:(j + 1) * 8], False, True,
                           [sp16_b, qt_b], [pb_b[4]], signal=(j == 15), sgc=True)
                    vb_t, vb_b = Vblk.next()
                    vtt(vb_t[:], bc_first(v_t[:, 0, :], 16), bc_mid(cs[:, C_IND16:C_IND16 + 16], 128), ALU.mult,
                        [v_b, cs_b], [vb_b])
                    vtt(S0t[:], S0t[:], bc_mid(sc_t[:, 16:32], 128), ALU.mult, [S0b, sc_b], [S0b])
                    for q in range(4):
                        bk = 6 + q % 2
                        mm(pbank[bk][:, :], ktk_t[:, 0, :], flat(vb_t[:, 4 * q:4 * q + 4, :]),
                           True, True, [ktk_b, vb_b], [pb_b[bk]])
                        vtt(S0t[:, 4 * q:4 * q + 4, :], S0t[:, 4 * q:4 * q + 4, :],
                            pbank[bk][:, :].rearrange("p (a b) -> p a b", b=128), ALU.add, [S0b, pb_b[bk]], [S0b])
                        yield
                    P.dma("sync", Ss_o[:, h, :, :].rearrange("j d v -> d j v"), S0t[:], reads=[S0b])

            def st_post(h, bi, d):
                c0, nb, is_s, ntile, L, nch = geom(bi)
                hl = h % 4
                sg_t, sg_b = d["sg"]
                step = 256 if nb == 512 else nb
                for lo in range(0, nb, step):
                    cl = slice(lo, lo + step)
                    osq_t, osq_b = p_osq.next()
                    rs_t, rs_b = p_rs.next()
                    on_t, on_b = p_on.next()
                    act(osq_t[:, 0:step], pbank[4][:, cl], AF.Square, [pb_b[4]], [osq_b])
                    yield
                    mm(pbank[7][:, cl], ones128b[:], osq_t[:, 0:step], True, True, [osq_b, onesDb_b], [pb_b[7]])
                    yield
                    act(rs_t[:, 0:step], pbank[7][:, cl], AF.Ln, [pb_b[7], misc_b], [rs_b], bias=epsc[:, 0:1])
                    yield
                    act(rs_t[:, 0:step], rs_t[:, 0:step], AF.Exp, [rs_b], [rs_b], scale=-0.5)
                    yield
                    vtt(on_t[:, 0:step], pbank[4][:, cl], rs_t[:, 0:step], ALU.mult, [pb_b[4], rs_b], [on_b])
                    yield
                    vstt(oT[:, hl, c0 + lo:c0 + lo + step], on_t[:, 0:step], vT[:, 56 + h:57 + h], sg_t[:, cl], ALU.mult, ALU.mult,
                         [on_b, vT_b, sg_b], [oT_b[hl][bi]])
                    yield

            items = [(h, bi) for h in range(8) for bi in range(len(BLOCKS))]
            hold = {}
            st_proj(*items[0])
            run_interleaved([st_ew(items[0][0], items[0][1], hold)])
            cur = hold["d"]
            st_mid(items[0][0], items[0][1], cur)
            for i, (h, bi) in enumerate(items):
                nx = items[i + 1] if i + 1 < len(items) else None
                if nx:
                    st_proj(*nx)
                run_interleaved([st_ew(nx[0], nx[1], hold) if nx else None, chain_post(h, bi, cur)])
                if bi == len(BLOCKS) - 1 and h % 4 == 3:
                    out_proj(w_hout, (h // 4) * 512, 4, oT, oT_b)
                if nx:
                    cur = hold["d"]
                    st_mid(nx[0], nx[1], cur)
            phase_end()

        def mlstm_phase():
            pair0 = (load_rows(w_min, 0, D, [(0, 512)], 0), load_rows(w_min, 0, D, [(512, 768)], 1))
            rmsnorm_to_xn(8)
            R = Ring
            oT = walloc([128, 4, NT], BF16)
            oT_b = [[Buf() for _ in BLOCKS] for _ in range(4)]
            rsel = walloc([8, 65])
            rsel_b = Buf()
            tokq = walloc([128, 17, 24])
            tokq_b = Buf()
            nT = walloc([128, 16, 8])
            nT_b = Buf()
            selp = walloc([8, 512])
            selp_b = Buf()
            P.dma("sync", selp[:], cst[0:8, C_SELP:C_SELP + 512], writes=[selp_b])
            mark = wptr[0]
            wg = walloc([128, KC, 16], BF16)
            wg_b = Buf()
            P.dma("gpsimd", wg[:], w_min[:, 3072:3088].rearrange("(kc p) n -> p kc n", p=128), writes=[wg_b])
            gb = walloc([8, 4])
            gb_b = Buf()
            P.dma("sync", gb[:, 0:1], gbias[0:8].rearrange("(h o) -> h o", o=1), writes=[gb_b])
            P.dma("sync", gb[:, 1:2], gbias[8:16].rearrange("(h o) -> h o", o=1), writes=[gb_b])
            vts(gb[:, 2:4], gb[:, 0:2], 1.0 / CAP, None, ALU.mult, None, [gb_b], [gb_b])
            m0T = walloc([8, 16])
            m0T_b = Buf()
            P.dma("sync", m0T[:], m0d.rearrange("j h -> h j"), writes=[m0T_b], allow_slow_non_contiguous=True)
            stat = walloc([8, 96])
            stat_b = Buf()
            g1 = R([8, 512], F32, 2)
            g2 = R([8, 512], F32, 2)
            g3 = R([8, 512], F32, 2)
            g4 = R([8, 512], F32, 2)
            g5 = R([8, 512], F32, 2)
            g6 = R([8, 512], F32, 2)
            stat_bs = [Buf() for _ in BLOCKS]

            def gate_block(bi, bki, bkf):
                c0, nb = BLOCKS[bi]
                is_s = bi == 4
                L = 8 if is_s else 128
                nch = nb // L
                ntile = nb // 128
                sb_ = stat_bs[bi]
                for gsel, bk in ((0, bki), (1, bkf)):
                    for kc in range(KC):
                        mm(pbank[bk][0:8, :nb], wg[:, kc, gsel * 8:(gsel + 1) * 8], xnT[:, kc, c0:c0 + nb],
                           kc == 0, kc == KC - 1, [wg_b, xnT_b[bi]], [pb_b[bk]], signal=(kc == KC - 1))
                    yield
                li_t, li_b = g1.next()
                lf_t, lf_b = g2.next()
                b_t, b_b = g3.next()
                a_t, a_b = g4.next()
                e_t, e_b = g5.next()
                w_t, w_b = g6.next()
                act(li_t[:, :nb], pbank[bki][0:8, :nb], AF.Tanh, [pb_b[bki], gb_b], [li_b], bias=gb[:, 2:3], scale=1.0 / CAP)
                yield
                vts(li_t[:, :nb], li_t[:, :nb], CAP, None, ALU.mult, None, [li_b], [li_b])
                yield
                act(lf_t[:, :nb], pbank[bkf][0:8, :nb], AF.Tanh, [pb_b[bkf], gb_b], [lf_b], bias=gb[:, 3:4], scale=1.0 / CAP)
                yield
                act(lf_t[:, :nb], lf_t[:, :nb], AF.Exp, [lf_b], [lf_b], scale=-CAP)
                yield
                act(lf_t[:, :nb], lf_t[:, :nb], AF.Ln, [lf_b, misc_b], [lf_b], bias=onec[0:8, 0:1])
                yield
                vts(lf_t[:, :nb], lf_t[:, :nb], -1.0, None, ALU.mult, None, [lf_b], [lf_b])
                yield
                if is_s:
                    vscan(b_t[:, :nb], cs[0:8, C_SM8:C_SM8 + 128], lf_t[:, :nb], 0.0, ALU.mult, ALU.add, [lf_b, cs_b], [b_b])
                    yield
                else:
                    for tt in range(ntile):
                        vscan(b_t[:, tt * 128:(tt + 1) * 128], cs[0:8, C_ONES1:C_ONES1 + 128], lf_t[:, tt * 128:(tt + 1) * 128],
                              0.0, ALU.mult, ALU.add, [lf_b, cs_b], [b_b])
                        yield
                vtt(a_t[:, :nb], li_t[:, :nb], b_t[:, :nb], ALU.subtract, [li_b, b_b], [a_b])
                yield
                bview = b_t[:, :nb].rearrange("p (c l) -> p c l", l=L)
                aview = a_t[:, :nb].rearrange("p (c l) -> p c l", l=L)
                so = 16 if is_s else 4 * bi
                vcopy(stat[:, so:so + nch], bview[:, :, L - 1], [b_b], [sb_])
                yield
                vreduce(stat[:, 32 + so:32 + so + nch], aview, ALU.max, [a_b], [sb_])
                yield
                act(e_t[:, :nb], a_t[:, :nb], AF.Exp, [a_b], [e_b])
                yield
                vtt(w_t[:, :nb].rearrange("p (c l) -> p c l", l=L), aview, bc_mid(stat[:, so:so + nch], L), ALU.add,
                    [a_b, sb_], [w_b])
                yield
                act(w_t[:, :nb], w_t[:, :nb], AF.Exp, [w_b], [w_b])
                yield
                act(b_t[:, :nb], b_t[:, :nb], AF.Exp, [b_b], [b_b], scale=-1.0)
                yield
                for tt in range(ntile):
                    gt = c0 // 128 + tt
                    for qi, (src, srcb) in enumerate(((e_t, e_b), (w_t, w_b), (b_t, b_b))):
                        mm(pbank[2][:, gt * 24 + qi * 8:gt * 24 + qi * 8 + 8], src[0:8, tt * 128:(tt + 1) * 128],
                           cs[0:8, C_IDENT:C_IDENT + 8], True, True, [srcb, cs_b], [pb_b[2]])
                    yield

            run_interleaved([gate_block(0, 0, 1), gate_block(1, 4, 5)])
            run_interleaved([gate_block(2, 0, 1), gate_block(3, 4, 5)])
            run_interleaved([gate_block(4, 0, 1)])
            stat_b = stat_bs
            vcopy(flat(tokq[:]), pbank[2][:, 0:17 * 24], [pb_b[2]], [tokq_b])
            vscan(stat[:, 64:80], stat[:, 32:48], stat[:, 0:16], 0.0, ALU.max, ALU.add, [stat_b], [stat_b])
            vtt(stat[:, 80:96], stat[:, 48:64], m0T[:], ALU.max, [stat_b, m0T_b], [stat_b])
            vtt(stat[:, 80:96], stat[:, 80:96], stat[:, 16:32], ALU.add, [stat_b], [stat_b])
            act(rsel[:, 0:32], stat[:, 0:32], AF.Exp, [stat_b], [rsel_b])
            act(rsel[:, 32:48], m0T[:], AF.Exp, [m0T_b], [rsel_b])
            act(rsel[:, 48:64], stat[:, 80:96], AF.Exp, [stat_b], [rsel_b], scale=-1.0)
            act(rsel[:, 64:65], stat[:, 79:80], AF.Exp, [stat_b], [rsel_b], scale=-1.0)
            P.dma("sync", mp_o[:, :], stat[:, 79:80], reads=[stat_b])
            P.dma("sync", ms_o.rearrange("j h -> h j"), stat[:, 80:96], reads=[stat_b], allow_slow_non_contiguous=True)
            n0t = walloc([128, 128])
            n0t_b = Buf()
            P.dma("sync", n0t[:, 0:64], n0d[:, :], writes=[n0t_b])
            P.dma("sync", n0t[:, 64:128], n0d[:, :], writes=[n0t_b])
            mm(pbank[3][:, 0:128], n0t[:], identf, True, True, [n0t_b, cs_b], [pb_b[3]])
            vcopy(flat(nT[:]), pbank[3][:, 0:128], [pb_b[3]], [nT_b])
            phase_end(mark)

            qT_r = R([128, 512], BF16, 2)
            kT_r = R([128, 512], BF16, 2)
            vaug_r = R([128, 2, 129], BF16, 3)
            for it, itb in vaug_r.items:
                vmemset(it[:, :, 128:129], 1.0, [itb])
            k2_r = R([128, 2, 64], BF16, 3)
            sigo_r = R([128, 256], F32, 2)
            at_r = R([128, 128], BF16, 4)
            sm_r = R([128, 16], F32, 3)
            ssq_r = R([128, 4], F32, 3)
            t4_r = R([128, 256], F32, 2)
            hht_r = R([128, 256], BF16, 2)
            junk_r = R([128, 128], F32, 2)
            Cst = walloc([128, 129])
            Cst_b = [Buf(), Buf()]
            Cbf_r = R([128, 129], BF16, 2)
            selsb_r = R([128, 65], F32, 2)
            C0a_r = R([128, 16, 129], F32, 1)
            C0bf_r = R([128, 16, 129], BF16, 1)
            qm = walloc([128, NT], BF16)
            qm_b = Buf()
            vmemset(qm[:], 0.0, [qm_b])
            vblk_r = R([128, 16, 129], BF16, 1)
            nout_r = R([128, 16], F32, 2)
            nout2_r = R([16, 128], F32, 2)

            def load_pair(p):
                a = load_rows(w_min, 0, D, [(p * 768, p * 768 + 512)], 2 * (p % 2))
                b = load_rows(w_min, 0, D, [(p * 768 + 512, (p + 1) * 768)], 2 * (p % 2) + 1)
                return a, b

            b7_st = [pb_b[0], pb_b[0]]
            b7_ht = pb_b[1]
            st_ps = [pbank[0][:, 0:128], pbank[0][:, 128:256]]
            ht_ps = pbank[1][:, 0:256]

            def front_a(p, t, pc, wa, wa_b, wbv, wb_b, sel_t, sel_b):
                bi = min(t // 4, 4)
                c0, nb = BLOCKS[bi]
                is_s = bi == 4
                tt = t - 4 * bi
                if tt == 0:
                    qT_t, qT_b = qT_r.next()
                    kT_t, kT_b = kT_r.next()
                    pc["qk"] = (qT_t, qT_b, kT_t, kT_b)
                    for which, bk in ((0, 2), (1, 3)):
                        for kc in range(KC):
                            mm(pbank[bk][:, :nb], wa[:, kc, which * 128:(which + 1) * 128], xnT[:, kc, c0:c0 + nb],
                               kc == 0, kc == KC - 1, [wa_b, xnT_b[bi]], [pb_b[bk]], signal=(kc == KC - 1))
                            yield
                    vts(qT_t[:, :nb], pbank[2][:, :nb], 0.125, None, ALU.mult, None, [pb_b[2]], [qT_b])
                    acopy(kT_t[:, :nb], pbank[3][:, :nb], [pb_b[3]], [kT_b])
                    yield
                    if is_s:
                        C0a, C0a_b = C0a_r.next()
                        for hh in range(2):
                            P.dma("sync", C0a[hh * 64:(hh + 1) * 64, :, 0:128],
                                  C0d[:, 2 * p + hh, :, :].rearrange("j k v -> k j v"), writes=[C0a_b])
                            vcopy(C0a[hh * 64:(hh + 1) * 64, :, 128], nT[hh * 64:(hh + 1) * 64, :, 2 * p + hh],
                                  [nT_b], [C0a_b])
                        vtt(C0a[:], C0a[:], bc_mid(sel_t[:, 32:48], 129), ALU.mult, [C0a_b, sel_b], [C0a_b])
                        yield
                        C0bf, C0bf_b = C0bf_r.next()
                        acopy(C0bf[:], C0a[:], [C0a_b], [C0bf_b])
                        qmv = qm[:].rearrange("p (j x) -> p j x", x=136)[:, :, 0:8]
                        vcopy(qmv, qT_t[:, 0:128].rearrange("p (j i) -> p j i", i=8), [qT_b], [qm_b])
                        pc["c0"] = (C0a, C0a_b, C0bf, C0bf_b)
                        yield
                tc0 = t * 128
                for kc in range(KC):
                    mm(pbank[2][:, 0:384], xnT[:, kc, tc0:tc0 + 128], wa[:, kc, 128:512], kc == 0, kc == KC - 1,
                       [wa_b, xnT_b[bi]], [pb_b[2]], signal=(kc == KC - 1))
                    yield
                for kc in range(KC):
                    mm(pbank[3][:, 0:256], xnT[:, kc, tc0:tc0 + 128], wbv[:, kc, 0:256], kc == 0, kc == KC - 1,
                       [wb_b, xnT_b[bi]], [pb_b[3]], signal=(kc == KC - 1))
                    yield
                pc["fa"][t] = (pc["qk"], pc.get("c0"))

            def front_b(p, t, pc, wa, wa_b, wbv, wb_b, sel_t, sel_b):
                bi = min(t // 4, 4)
                c0, nb = BLOCKS[bi]
                is_s = bi == 4
                tt = t - 4 * bi
                (qT_t, qT_b, kT_t, kT_b), c0pack = pc["fa"].pop(t)
                gt = t
                tc0 = t * 128
                tl = tt * 128
                va_t, va_b = vaug_r.next()
                k2_t, k2_b = k2_r.next()
                so_t, so_b = sigo_r.next()
                vcopy(va_t[:, :, 0:128], pbank[2][:, 128:384].rearrange("p (a b) -> p a b", b=128), [pb_b[2]], [va_b])
                yield
                vtt(k2_t[:], pbank[2][:, 0:128].rearrange("p (a b) -> p a b", b=64),
                    bc_mid(tokq[:, gt, 8 + 2 * p:10 + 2 * p], 64), ALU.mult, [pb_b[2], tokq_b], [k2_b])
                yield
                act(so_t[:], pbank[3][:, 0:256], AF.Exp, [pb_b[3]], [so_b], scale=-1.0)
                yield
                act(so_t[:], so_t[:], AF.Ln, [so_b, misc_b], [so_b], bias=onec[:, 0:1])
                yield
                act(so_t[:], so_t[:], AF.Exp, [so_b], [so_b], scale=-1.0)
                yield
                mask = cs[:, C_BD128:C_BD128 + 128] if is_s else cs[:, C_TRI128:C_TRI128 + 128]
                ob = 4 + (gt % 2)
                ov = pbank[ob][:, :].rearrange("p (a b) -> p a b", a=2)
                dv_ = pbank[6][:, :].rearrange("p (a b) -> p a b", a=2)
                Cbf_t, Cbf_b = pc["cbf"]
                for hh in range(2):
                    r0 = hh * 64
                    hg = 2 * p + hh
                    mm(st_ps[hh], kT_t[r0:r0 + 64, tl:tl + 128], qT_t[r0:r0 + 64, tl:tl + 128], True, True,
                       [kT_b, qT_b], [b7_st[hh]])
                    yield
                    at_t, at_b = at_r.next()
                    vstt(at_t[:], st_ps[hh], tokq[:, gt, hg:hg + 1], mask, ALU.mult, ALU.mult,
                         [b7_st[hh], tokq_b, cs_b], [at_b])
                    yield
                    mm(ov[:, hh, 0:129], at_t[:], va_t[:, hh, :], True, False, [at_b, va_b], [pb_b[ob]], signal=True, sgc=True)
                    if not is_s:
                        mm(ov[:, hh, 0:129], qT_t[r0:r0 + 64, tl:tl + 128], Cbf_t[r0:r0 + 64, :], False, True,
                           [qT_b, Cbf_b], [pb_b[ob]], sgc=True)
                        yield
                    else:
                        C0a, C0a_b, C0bf, C0bf_b = c0pack
                        for j in range(16):
                            mm(ov[:, hh, 0:129], qm[r0:r0 + 64, j * 128:(j + 1) * 128], C0bf[r0:r0 + 64, j, :],
                               False, j == 15, [qm_b, C0bf_b], [pb_b[ob]], signal=True, sgc=True)
                        yield
                pc["post"] = (ob, ov, so_t, so_b, bi, tt, tc0)
                if not is_s:
                    for hh in range(2):
                        mm(dv_[:, hh, 0:129], flat(k2_t[:]), va_t[:, hh, :], True, True,
                           [k2_b, va_b], [pb_b[6]], signal=True)
                    yield
                    for hh in range(2):
                        r0 = hh * 64
                        vstt(Cst[r0:r0 + 64, :], Cst[r0:r0 + 64, :], sel_t[r0:r0 + 64, gt:gt + 1], dv_[r0:r0 + 64, hh, 0:129],
                             ALU.mult, ALU.add, [Cst_b[hh], sel_b, pb_b[6]], [Cst_b[hh]])
                        yield
                    Cbf_t, Cbf_b = Cbf_r.next()
                    acopy(Cbf_t[:], Cst[:], [Cst_b], [Cbf_b])
                    pc["cbf"] = (Cbf_t, Cbf_b)
                    yield
                    if gt == 15:
                        vts(Cst[:], Cst[:], sel_t[:, 64:65], None, ALU.mult, None, [Cst_b, sel_b], [Cst_b])
                        P.dma("sync", Cp_o[2 * p:2 * p + 2, :, :].rearrange("h k v -> (h k) v"), Cst[:, 0:128], reads=[Cst_b])
                        P.dma("sync", np_o[2 * p:2 * p + 2, :].rearrange("h (k o) -> (h k) o", o=1), Cst[:, 128:129],
                              reads=[Cst_b])
                        yield
                else:
                    C0a, C0a_b, C0bf, C0bf_b = c0pack
                    for hh in range(2):
                        r0 = hh * 64
                        vb_t, vb_b = vblk_r.next()
                        vtt(vb_t[:], bc_first(va_t[:, hh, :], 16), bc_mid(cs[:, C_IND16:C_IND16 + 16], 129), ALU.mult,
                            [va_b, cs_b], [vb_b])
                        yield
                        vtt(C0a[r0:r0 + 64, :, :], C0a[r0:r0 + 64, :, :], bc_mid(sel_t[r0:r0 + 64, 16:32], 129), ALU.mult,
                            [C0a_b, sel_b], [C0a_b])
                        yield
                        for g in range(6):
                            j0 = 3 * g
                            nj = min(3, 16 - j0)
                            mm(pbank[6][:, 0:nj * 129], flat(k2_t[:]), flat(vb_t[:, j0:j0 + nj, :]), True, True,
                               [k2_b, vb_b], [pb_b[6]])
                            vtt(C0a[r0:r0 + 64, j0:j0 + nj, :], C0a[r0:r0 + 64, j0:j0 + nj, :],
                                pbank[6][r0:r0 + 64, 0:nj * 129].rearrange("p (a b) -> p a b", b=129), ALU.add,
                                [C0a_b, pb_b[6]], [C0a_b])
                            yield
                    vtt(C0a[:], C0a[:], bc_mid(sel_t[:, 48:64], 129), ALU.mult, [C0a_b, sel_b], [C0a_b])
                    for hh in range(2):
                        P.dma("sync", Cs_o[:, 2 * p + hh, :, :].rearrange("j k v -> k j v"),
                              C0a[hh * 64:(hh + 1) * 64, :, 0:128], reads=[C0a_b])
                    yield
                    no_t, no_b = nout_r.next()
                    vcopy(no_t[:], C0a[:, :, 128], [C0a_b], [no_b])
                    mm(pbank[6][0:16, 0:128], no_t[:], identf, True, True, [no_b, cs_b], [pb_b[6]])
                    no2_t, no2_b = nout2_r.next()
                    vcopy(no2_t[:], pbank[6][0:16, 0:128], [pb_b[6]], [no2_b])
                    P.dma("sync", ns_o[:, 2 * p:2 * p + 2, :].rearrange("j h k -> j (h k)"), no2_t[:], reads=[no2_b])
                    yield

            def back(p, t, post):
                ob, ov, so_t, so_b, bi, tt, tc0 = post
                gt = t
                sm_t, sm_b = sm_r.next()
                den = ov[:, :, 128]
                einv2 = tokq[:, gt, 16 + 2 * p:18 + 2 * p]
                act(sm_t[:, 0:2], den, AF.Abs, [pb_b[ob]], [sm_b])
                yield
                ssq_t, ssq_b = ssq_r.next()
                for hh in range(2):
                    jk_t, jk_b = junk_r.next()
                    act(jk_t[:], ov[:, hh, 0:128], AF.Square, [pb_b[ob]], [jk_b, ssq_b], accum_out=ssq_t[:, hh:hh + 1])
                    yield
                vtt(sm_t[:, 0:2], sm_t[:, 0:2], einv2, ALU.max, [sm_b, tokq_b], [sm_b])
                yield
                vtt(sm_t[:, 2:4], sm_t[:, 0:2], sm_t[:, 0:2], ALU.mult, [sm_b], [sm_b])
                yield
                vstt(sm_t[:, 6:8], sm_t[:, 2:4], EPS * 128.0, ssq_t[:, 0:2], ALU.mult, ALU.add, [sm_b, ssq_b], [sm_b])
                yield
                act(sm_t[:, 8:10], sm_t[:, 6:8], AF.Ln, [sm_b], [sm_b], scale=1.0 / 128.0)
                yield
                act(sm_t[:, 10:12], sm_t[:, 8:10], AF.Exp, [sm_b], [sm_b], scale=-0.5)
                yield
                t4_t, t4_b = t4_r.next()
                hht_t, hht_b = hht_r.next()
                vtt(t4_t[:].rearrange("p (a b) -> p a b", a=2), ov[:, :, 0:128], bc_mid(sm_t[:, 10:12], 128), ALU.mult,
                    [pb_b[ob], sm_b], [t4_b])
                yield
                vtt(hht_t[:], t4_t[:], so_t[:], ALU.mult, [t4_b, so_b], [hht_b])
                yield
                for hh in range(2):
                    mm(ht_ps[:, hh * 128:(hh + 1) * 128], hht_t[:, hh * 128:(hh + 1) * 128], identb[:], True, True,
                       [hht_b, identb_b], [b7_ht], signal=True)
                yield
                for hh in range(2):
                    hl = (2 * p + hh) % 4
                    vts(oT[:, hl, tc0:tc0 + 128], ht_ps[:, hh * 128:(hh + 1) * 128], vT[:, 64 + 2 * p + hh:65 + 2 * p + hh], None,
                        ALU.mult, None, [b7_ht, vT_b], [oT_b[hl][bi]])
                    yield

            nxt = pair0
            for p in range(4):
                (wa, wa_b), (wbv, wb_b) = nxt
                if p + 1 < 4:
                    nxt = load_pair(p + 1)
                sel_t, sel_b = selsb_r.next()
                mm(pbank[6][:, 0:65], selp[0:8, p * 128:(p + 1) * 128], rsel[:, :], True, True,
                   [selp_b, rsel_b], [pb_b[6]])
                vcopy(sel_t[:], pbank[6][:, 0:65], [pb_b[6]], [sel_b])
                vmemset(Cst[:], 0.0, [Cst_b])
                Cbf_t, Cbf_b = Cbf_r.next()
                vmemset(Cbf_t[:], 0.0, [Cbf_b])
                pc = {"cbf": (Cbf_t, Cbf_b), "fa": {}}
                args = (pc, wa, wa_b, wbv, wb_b, sel_t, sel_b)
                run_interleaved([front_a(p, 0, *args)])
                prev_post = None
                for t in range(17):
                    g_a = front_a(p, t + 1, *args) if t + 1 < 17 else None
                    g_b = front_b(p, t, *args)
                    g_back = back(p, t - 1, prev_post) if prev_post is not None else None
                    for _ in range(5):
                        next(g_b)
                    run_interleaved([g_b, g_a, g_back])
                    prev_post = pc["post"]
                run_interleaved([back(p, 16, prev_post)])
                if p % 2 == 1:
                    out_proj(w_mout, (p // 2) * 512, 4, oT, oT_b)
            phase_end()

        def final_phase():
            wfin_bc = walloc([128, D])
            bc_b = Buf()
            P.dma("sync", wfin_bc[:], wfin.partition_broadcast(128), writes=[bc_b])
            yt_r = Ring([128, D], F32, 2)
            jk_r = Ring([128, 512], F32, 2)
            sm_r = Ring([128, 4], F32, 2)
            for tt in range(NT // 128):
                bi = min(tt // 4, 4)
                bks = [(2 * tt) % 8, (2 * tt + 1) % 8]
                for half in range(2):
                    bk = bks[half]
                    for q in range(4):
                        kc = half * 4 + q
                        mm(pbank[bk][:, q * 128:(q + 1) * 128], xT[:, kc, tt * 128:(tt + 1) * 128], identf, True, True,
                           [xT_b[bi], cs_b], [pb_b[bk]], signal=(q == 3))
                sm_t, sm_b = sm_r.next()
                for half in range(2):
                    jk_t, jk_b = jk_r.next()
                    act(jk_t[:], pbank[bks[half]][:, :], AF.Square, [pb_b[bks[half]]], [jk_b, sm_b], accum_out=sm_t[:, half:half + 1])
                vtt(sm_t[:, 2:3], sm_t[:, 0:1], sm_t[:, 1:2], ALU.add, [sm_b], [sm_b])
                act(sm_t[:, 3:4], sm_t[:, 2:3], AF.Ln, [sm_b, misc_b], [sm_b], bias=epsc[:, 0:1], scale=1.0 / D)
                act(sm_t[:, 3:4], sm_t[:, 3:4], AF.Exp, [sm_b], [sm_b], scale=-0.5)
                y_t, y_b = yt_r.next()
                for half in range(2):
                    vstt(y_t[:, half * 512:(half + 1) * 512], pbank[bks[half]][:, :], sm_t[:, 3:4],
                         wfin_bc[:, half * 512:(half + 1) * 512], ALU.mult, ALU.mult, [pb_b[bks[half]], sm_b, bc_b], [y_b])
                P.dma("sync", y_o[tt * 128:(tt + 1) * 128, :], y_t[:], reads=[y_b])
            phase_end()

        def dump_xT():
            P.dma("sync", dbg_o[:, :], flat(xT[:]), reads=xT_b)

        phases = [("hgrn", hgrn_phase), ("ffn0", lambda: ffn_phase(0)), ("mlstm", mlstm_phase), ("ffn1", lambda: ffn_phase(1))]
        if dbg_phase == "x0":
            dump_xT()
        for name, fn in phases:
            if os.environ.get("MK_SKIP_" + name.upper()):
                continue
            fn()
            if dbg_phase == name:
                dump_xT()
        final_phase()
        P.finish()
    print("n_inst", P.n_inst, {n: len(e.prog) for n, e in P.E.items()})
    return nc


def _relayout_mlstm(w):
    parts = []
    for p in range(4):
        parts += [w[:, p * 128:(p + 1) * 128], w[:, 512 + p * 128:512 + (p + 1) * 128],
                  w[:, 1024 + p * 256:1024 + (p + 1) * 256], w[:, 2048 + p * 256:2048 + (p + 1) * 256]]
    parts.append(w[:, 3072:3088])
    return np.ascontiguousarray(np.concatenate(parts, axis=1))


_NC_CACHE = {}


def kernel(x_prompt, x_sample, state_hgrn_S, state_mlstm_C, state_mlstm_n, state_mlstm_m,
           norm_mixer_w, norm_ffn_w, hgrn_w_in, hgrn_lower_bound_logits, hgrn_out_norm_w, hgrn_w_out,
           mlstm_w_in, mlstm_gate_bias, mlstm_out_norm_w, mlstm_w_out, ffn_w_up, ffn_w_down, final_norm_w,
           _dbg_phase=None, _cores=8):
    f = lambda a: np.ascontiguousarray(np.asarray(a, dtype=np.float32))
    x_prompt, x_sample = f(x_prompt), f(x_sample)
    S, C, n, m = f(state_hgrn_S), f(state_mlstm_C), f(state_mlstm_n), f(state_mlstm_m)
    vecs = np.concatenate([f(norm_mixer_w).reshape(16, 128), f(norm_ffn_w).reshape(16, 128),
                           f(hgrn_lower_bound_logits).reshape(24, 128), f(hgrn_out_norm_w).reshape(8, 128),
                           f(mlstm_out_norm_w).reshape(8, 128)], axis=0)
    shared = {
        "vecs": np.ascontiguousarray(vecs),
        "w_hin": np.ascontiguousarray(f(hgrn_w_in)[0].reshape(D, 4, 8, 128).transpose(0, 2, 1, 3).reshape(D, 4096)),
        "w_hout": f(hgrn_w_out)[0], "w_min": _relayout_mlstm(f(mlstm_w_in)[0]), "w_mout": f(mlstm_w_out)[0],
        "w_up": f(ffn_w_up), "w_dn": f(ffn_w_down),
        "gbias": f(mlstm_gate_bias).reshape(16), "wfin": f(final_norm_w),
        "cst": _make_consts(),
    }
    in_maps = []
    for c in range(_cores):
        d = dict(shared)
        d["xin"] = np.ascontiguousarray(np.concatenate([x_prompt[c], x_sample[16 * c:16 * (c + 1)].reshape(NS, D)], axis=0))
        d["S0"] = np.ascontiguousarray(S[0, 16 * c:16 * (c + 1)])
        d["C0"] = np.ascontiguousarray(C[0, 16 * c:16 * (c + 1)])
        d["n0"] = np.ascontiguousarray(n[0, 16 * c:16 * (c + 1)].reshape(128, 64))
        d["m0"] = np.ascontiguousarray(m[0, 16 * c:16 * (c + 1)])
        in_maps.append(d)
    key = _dbg_phase
    if key not in _NC_CACHE:
        _NC_CACHE[key] = build_program(_dbg_phase)
    nc = _NC_CACHE[key]
    runner = globals().get("_RUNNER") or (lambda nc_, im: run_bass_kernel_spmd(nc_, im, core_ids=list(range(len(im)))))
    res = runner(nc, in_maps).results
    B = _cores
    y_prompt = np.stack([res[c]["y"][:NP_] for c in range(B)], axis=0)
    y_sample = np.concatenate([res[c]["y"][NP_:].reshape(16, 8, D) for c in range(B)], axis=0)
    S_p = np.stack([res[c]["S_p"] for c in range(B)], axis=0)[None]
    C_p = np.stack([res[c]["C_p"] for c in range(B)], axis=0)[None]
    n_p = np.stack([res[c]["n_p"] for c in range(B)], axis=0)[None]
    m_p = np.stack([res[c]["m_p"].reshape(8) for c in range(B)], axis=0)[None]
    S_s = np.concatenate([res[c]["S_s"] for c in range(B)], axis=0)[None]
    C_s = np.concatenate([res[c]["C_s"] for c in range(B)], axis=0)[None]
    n_s = np.concatenate([res[c]["n_s"] for c in range(B)], axis=0)[None]
    m_s = np.concatenate([res[c]["m_s"] for c in range(B)], axis=0)[None]
    outs = (y_prompt, y_sample, S_p, C_p, n_p, m_p, S_s, C_s, n_s, m_s)
    if _dbg_phase is not None:
        return outs, [res[c]["dbg"] for c in range(B)]
    return tuple(np.ascontiguousarray(o, dtype=np.float32) for o in outs)
```
